# Optimizing a Trainium2 kernel written in Bass

```python
import math
import jax, jax.numpy as jnp
from jax import lax
import numpy as np

D_MODEL = 1024
BATCH = 32
SEQ = 2048
DEPTH = 4

N_MIXERS = 3
D_FF = 4 * D_MODEL
D_RNN = D_MODEL
HA = 8
HDA = D_RNN // HA
CONV_W = 4
RG_C = 8.0
SGU_CHUNK = 128
D_SGU = D_MODEL
GB = 8
DGB = D_SGU // GB
HC = 8
DK = 128
DV = 128
DC = HC * DV
GDN_CHUNK = 64

kernel_name = "hybrid_rglru_sgu_gdn_encoder"

F32 = jnp.float32


def rms_norm(x, g, eps=1e-6):
    xf = x.astype(F32)
    y = xf * lax.rsqrt(jnp.mean(xf * xf, axis=-1, keepdims=True) + eps)
    return (y * g.astype(F32)).astype(x.dtype)


def layer_norm(x, g, b, eps=1e-5):
    xf = x.astype(F32)
    mu = jnp.mean(xf, axis=-1, keepdims=True)
    var = jnp.mean(jnp.square(xf - mu), axis=-1, keepdims=True)
    return ((xf - mu) * lax.rsqrt(var + eps) * g.astype(F32) + b.astype(F32)).astype(x.dtype)


def l2_normalize(x, eps=1e-6):
    xf = x.astype(F32)
    return xf * lax.rsqrt(jnp.sum(xf * xf, axis=-1, keepdims=True) + eps)


def centred_dwconv(x, w):
    K, C = w.shape
    left = K // 2
    return lax.conv_general_dilated(x, w[:, None, :].astype(x.dtype), window_strides=(1,),
                                    padding=[(left, K - 1 - left)],
                                    dimension_numbers=('NWC', 'WIO', 'NWC'),
                                    feature_group_count=C)


def linear_scan(a, b, reverse):
    def combine(c1, c2):
        a1, b1 = c1
        a2, b2 = c2
        return a1 * a2, a2 * b1 + b2
    _, h = lax.associative_scan(combine, (a, b), axis=1, reverse=reverse)
    return h


def rglru_direction(xr, gate_w, gate_b, lam, reverse):
    Bsz, S, _ = xr.shape
    xh = xr.reshape(Bsz, S, HA, HDA)
    gates = jnp.einsum('bshi,ghij->gbshj', xh, gate_w) + gate_b[:, None, None]
    gates = jax.nn.sigmoid(gates.astype(F32)).reshape(2, Bsz, S, D_RNN)
    r, ig = gates[0], gates[1]
    log_a = -RG_C * r * jax.nn.softplus(-lam.astype(F32))
    a = jnp.exp(log_a)
    mult = jnp.sqrt(-jnp.expm1(2.0 * log_a))
    bx = mult * ig * xr.astype(F32)
    return linear_scan(a, bx, reverse)


def mixer_rglru(h, w_in, conv_w, conv_b, gate_w, gate_b, lam, w_out):
    z = h @ w_in
    gate, xr = z[..., :D_RNN], z[..., D_RNN:]
    xr = centred_dwconv(xr, conv_w) + conv_b
    y = (rglru_direction(xr, gate_w[0], gate_b[0], lam[0], reverse=False)
         + rglru_direction(xr, gate_w[1], gate_b[1], lam[1], reverse=True))
    y = y.astype(h.dtype) * jax.nn.gelu(gate)
    return y @ w_out


def mixer_sgu(h, w_in, ln_g, ln_b, w_s, b_s, w_out):
    Bsz, S, _ = h.shape
    n_chunks = S // SGU_CHUNK
    z = jax.nn.gelu(h @ w_in)
    u, v = z[..., :D_SGU], z[..., D_SGU:]
    v = layer_norm(v, ln_g, ln_b).reshape(Bsz, n_chunks, SGU_CHUNK, GB, DGB)
    vs = jnp.einsum('gpq,bnqgc->bnpgc', w_s, v) + b_s.T[None, None, :, :, None]
    y = u * vs.reshape(Bsz, S, D_SGU)
    return y @ w_out


def chunk_gated_delta(q, k, v, g, beta):
    Bsz, S, H, _ = q.shape
    C = GDN_CHUNK
    N = S // C
    ch = lambda t: t.reshape(Bsz, N, C, H, t.shape[-1]).transpose(0, 3, 1, 2, 4).astype(F32)
    chs = lambda t: t.reshape(Bsz, N, C, H).transpose(0, 3, 1, 2).astype(F32)
    q, k, v = ch(q), ch(k), ch(v)
    g, beta = chs(g), chs(beta)
    gc = jnp.cumsum(g, axis=-1)
    tril = jnp.tril(jnp.ones((C, C), bool))
    strict = jnp.tril(jnp.ones((C, C), bool), -1)
    diff = gc[..., :, None] - gc[..., None, :]
    decay = jnp.where(tril, jnp.exp(jnp.where(tril, diff, 0.0)), 0.0)
    k_beta = k * beta[..., None]
    v_beta = v * beta[..., None]
    A = jnp.where(strict, jnp.einsum('bhnid,bhnjd->bhnij', k_beta, k) * decay, 0.0)
    eye = jnp.eye(C, dtype=F32)
    T = lax.linalg.triangular_solve(eye + A, jnp.broadcast_to(eye, A.shape), left_side=True,
                                    lower=True, unit_diagonal=True)
    u = jnp.einsum('bhnij,bhnjd->bhnid', T, v_beta)
    w = jnp.einsum('bhnij,bhnjd->bhnid', T, k_beta * jnp.exp(gc)[..., None])
    qk = jnp.einsum('bhnid,bhnjd->bhnij', q, k) * decay

    def step(state, xs):
        q_c, k_c, u_c, w_c, qk_c, gc_c = xs
        v_new = u_c - jnp.einsum('bhck,bhkv->bhcv', w_c, state)
        o = (jnp.einsum('bhck,bhkv->bhcv', q_c * jnp.exp(gc_c)[..., None], state)
             + jnp.einsum('bhij,bhjv->bhiv', qk_c, v_new))
        g_last = gc_c[..., -1]
        state = (state * jnp.exp(g_last)[..., None, None]
                 + jnp.einsum('bhck,bhcv->bhkv', k_c * jnp.exp(g_last[..., None] - gc_c)[..., None], v_new))
        return state, o

    mv = lambda t: jnp.moveaxis(t, 2, 0)
    state0 = jnp.zeros((Bsz, H, q.shape[-1], v.shape[-1]), F32)
    _, o = lax.scan(step, state0, (mv(q), mv(k), mv(u), mv(w), mv(qk), mv(gc)))
    return o.transpose(1, 0, 3, 2, 4).reshape(Bsz, S, H, v.shape[-1])


def mixer_gdn(h, w_in, conv_w, a_log, dt_bias, norm_g, w_out):
    Bsz, S, _ = h.shape
    z = h @ w_in
    qkv = jax.nn.silu(centred_dwconv(z[..., :3 * DC], conv_w))
    gate = z[..., 3 * DC:4 * DC].reshape(Bsz, S, HC, DV)
    a_logit = z[..., 4 * DC:4 * DC + 2 * HC].reshape(Bsz, S, 2, HC).astype(F32)
    b_logit = z[..., 4 * DC + 2 * HC:].reshape(Bsz, S, 2, HC).astype(F32)
    q = l2_normalize(qkv[..., :DC].reshape(Bsz, S, HC, DK)) * (DK ** -0.5)
    k = l2_normalize(qkv[..., DC:2 * DC].reshape(Bsz, S, HC, DK))
    v = qkv[..., 2 * DC:].reshape(Bsz, S, HC, DV)
    g = -jnp.exp(a_log.astype(F32)) * jax.nn.softplus(a_logit + dt_bias.astype(F32))
    beta = jax.nn.sigmoid(b_logit)
    o_f = chunk_gated_delta(q, k, v, g[:, :, 0], beta[:, :, 0])
    fl = lambda t: jnp.flip(t, axis=1)
    o_b = fl(chunk_gated_delta(fl(q), fl(k), fl(v), fl(g[:, :, 1]), fl(beta[:, :, 1])))
    o = (o_f + o_b).astype(h.dtype)
    o = rms_norm(o, norm_g) * jax.nn.silu(gate)
    return o.reshape(Bsz, S, DC) @ w_out


def sqrelu_mlp(h, w_up, w_down):
    return jnp.square(jax.nn.relu(h @ w_up)) @ w_down


def setup_inputs(seed: int = 0) -> dict:
    key = jax.random.key(seed)
    ks = jax.random.split(key, 32)
    nA = (DEPTH + 2) // 3
    nB = (DEPTH + 1) // 3
    nC = DEPTH // 3
    nrm = lambda k, shape, scale: jax.random.normal(k, shape, F32) * scale
    gain = lambda k, shape: 1.0 + 0.02 * jax.random.normal(k, shape, F32)

    x = nrm(ks[0], (BATCH, SEQ, D_MODEL), 1.0)
    norm_mix_g = gain(ks[1], (DEPTH, D_MODEL))
    norm_mlp_g = gain(ks[2], (DEPTH, D_MODEL))
    mlp_w_up = nrm(ks[3], (DEPTH, D_MODEL, D_FF), D_MODEL ** -0.5)
    mlp_w_down = nrm(ks[4], (DEPTH, D_FF, D_MODEL), D_FF ** -0.5)
    norm_final_g = gain(ks[5], (D_MODEL,))

    a_w_in = nrm(ks[6], (nA, D_MODEL, 2 * D_RNN), D_MODEL ** -0.5)
    a_conv_w = nrm(ks[7], (nA, CONV_W, D_RNN), CONV_W ** -0.5)
    a_conv_b = nrm(ks[8], (nA, D_RNN), 0.02)
    a_gate_w = nrm(ks[9], (nA, 2, 2, HA, HDA, HDA), HDA ** -0.5)
    a_gate_b = nrm(ks[10], (nA, 2, 2, HA, HDA), 0.02)
    a0 = jax.random.uniform(ks[11], (nA, 2, D_RNN), F32, minval=0.9, maxval=0.999)
    a_base = a0 ** (1.0 / RG_C)
    a_lambda = jnp.log(a_base) - jnp.log1p(-a_base)
    a_w_out = nrm(ks[12], (nA, D_RNN, D_MODEL), D_RNN ** -0.5)

    b_w_in = nrm(ks[13], (nB, D_MODEL, 2 * D_SGU), D_MODEL ** -0.5)
    b_ln_g = gain(ks[14], (nB, D_SGU))
    b_ln_b = nrm(ks[15], (nB, D_SGU), 0.02)
    b_w_s = nrm(ks[16], (nB, GB, SGU_CHUNK, SGU_CHUNK), SGU_CHUNK ** -0.5)
    b_b_s = gain(ks[17], (nB, GB, SGU_CHUNK))
    b_w_out = nrm(ks[18], (nB, D_SGU, D_MODEL), D_SGU ** -0.5)

    c_w_in = nrm(ks[19], (nC, D_MODEL, 4 * DC + 4 * HC), D_MODEL ** -0.5)
    c_conv_w = nrm(ks[20], (nC, CONV_W, 3 * DC), CONV_W ** -0.5)
    c_a_log = jnp.log(jax.random.uniform(ks[21], (nC, 2, HC), F32, minval=1.0, maxval=16.0))
    dt = jnp.exp(jax.random.uniform(ks[22], (nC, 2, HC), F32, minval=math.log(1e-3), maxval=math.log(1e-1)))
    c_dt_bias = dt + jnp.log(-jnp.expm1(-dt))
    c_norm_g = gain(ks[23], (nC, DV))
    c_w_out = nrm(ks[24], (nC, DC, D_MODEL), DC ** -0.5)

    return {"x": x, "norm_mix_g": norm_mix_g, "norm_mlp_g": norm_mlp_g,
            "mlp_w_up": mlp_w_up, "mlp_w_down": mlp_w_down, "norm_final_g": norm_final_g,
            "a_w_in": a_w_in, "a_conv_w": a_conv_w, "a_conv_b": a_conv_b,
            "a_gate_w": a_gate_w, "a_gate_b": a_gate_b, "a_lambda": a_lambda, "a_w_out": a_w_out,
            "b_w_in": b_w_in, "b_ln_g": b_ln_g, "b_ln_b": b_ln_b, "b_w_s": b_w_s,
            "b_b_s": b_b_s, "b_w_out": b_w_out,
            "c_w_in": c_w_in, "c_conv_w": c_conv_w, "c_a_log": c_a_log, "c_dt_bias": c_dt_bias,
            "c_norm_g": c_norm_g, "c_w_out": c_w_out}


def reference(x, norm_mix_g, norm_mlp_g, mlp_w_up, mlp_w_down, norm_final_g,
              a_w_in, a_conv_w, a_conv_b, a_gate_w, a_gate_b, a_lambda, a_w_out,
              b_w_in, b_ln_g, b_ln_b, b_w_s, b_b_s, b_w_out,
              c_w_in, c_conv_w, c_a_log, c_dt_bias, c_norm_g, c_w_out):
    for i in range(DEPTH):
        kind, j = i % N_MIXERS, i // N_MIXERS
        hn = rms_norm(x, norm_mix_g[i])
        if kind == 0:
            m = mixer_rglru(hn, a_w_in[j], a_conv_w[j], a_conv_b[j], a_gate_w[j], a_gate_b[j],
                            a_lambda[j], a_w_out[j])
        elif kind == 1:
            m = mixer_sgu(hn, b_w_in[j], b_ln_g[j], b_ln_b[j], b_w_s[j], b_b_s[j], b_w_out[j])
        else:
            m = mixer_gdn(hn, c_w_in[j], c_conv_w[j], c_a_log[j], c_dt_bias[j], c_norm_g[j], c_w_out[j])
        x = x + m
        x = x + sqrelu_mlp(rms_norm(x, norm_mlp_g[i]), mlp_w_up[i], mlp_w_down[i])
    return rms_norm(x, norm_final_g)
```

```python
import contextlib
import numpy as np
import concourse.bass as bass
import concourse.mybir as mybir
from concourse.bass_utils import run_bass_kernel_spmd

F32 = mybir.dt.float32
BF16 = mybir.dt.bfloat16
AF = mybir.ActivationFunctionType
ALU = mybir.AluOpType

D = 1024
SEQ = 2048
NCORE = 8
SEM_LIMIT = 30000
INORDER_SAFE = ("pe", "sp")
NSLOT = 3
import os
KSTOP = int(os.environ.get('KSTOP', '99'))
KSTOPC = int(os.environ.get('KSTOPC', '99'))
KSUB = int(os.environ.get('KSUB', '99'))
TBF16 = os.environ.get('TBF16', '1') == '1'
ARENA_COLS = 20800


class Sched:
    ENGS = ("pe", "act", "dve", "pool", "sp")

    def __init__(self, nc, stack):
        self.nc = nc
        self.stack = stack
        self.dry = False
        self.E = {n: dict(ops=[], sem=None, cnt=0, seen={}, nsem=0, last=None) for n in self.ENGS}
        self.res = {}
        self.dsem = {}

    def _newsem(self, name):
        return self.stack.enter_context(self.nc.semaphore(name))

    def _tok(self, eng):
        E = self.E[eng]
        if E["sem"] is None or E["cnt"] >= SEM_LIMIT:
            E["sem"] = self._newsem("s_%s_%d" % (eng, E["nsem"]))
            E["nsem"] += 1
            E["cnt"] = 0
        E["cnt"] += 1
        t = (E["sem"], E["cnt"], eng, 1)
        E["last"] = t
        return t

    def _dtok(self, key):
        d = self.dsem.get(key)
        if d is None or d[1] >= SEM_LIMIT:
            n = 0 if d is None else d[2] + 1
            nm = "d_" + "".join(ch for ch in str(key) if ch.isalnum()) + "_%d" % n
            d = [self._newsem(nm), 0, n]
            self.dsem[key] = d
        d[1] += 16
        return (d[0], d[1], "dma", 16)

    def op(self, eng, fns, R=(), W=(), dma=None):
        if self.dry:
            return None
        if callable(fns):
            fns = [fns]
        psr = [r for r in R if isinstance(r, tuple) and r and r[0] == "ps"]
        if psr:
            R = [r for r in R if not (isinstance(r, tuple) and r and r[0] == "ps")]
            W = list(W) + psr
        E = self.E[eng]
        need = {}

        def add(tok):
            if tok is None:
                return
            sem, val, src, _ = tok
            if src == eng and eng in INORDER_SAFE:
                return
            k = id(sem)
            if k not in need or need[k][1] < val:
                need[k] = (sem, val)

        for r in R:
            st = self.res.get(r)
            if st:
                add(st[0])
        for w in W:
            st = self.res.get(w)
            if st:
                add(st[0])
                for t in st[1]:
                    add(t)
        waits = []
        for k, (sem, val) in need.items():
            if E["seen"].get(k, 0) >= val:
                continue
            E["seen"][k] = val
            waits.append((sem, val))
        tok = self._dtok(dma) if dma is not None else self._tok(eng)
        E["ops"].append((waits, fns, tok))
        for r in R:
            st = self.res.get(r)
            if st is None:
                self.res[r] = [None, [tok]]
            else:
                st[1].append(tok)
        for w in W:
            self.res[w] = [tok, []]
        return tok

    def barrier(self, engs=("pe", "act", "dve", "pool")):
        if self.dry:
            return
        lasts = [self.E[e]["last"] for e in engs if self.E[e]["last"] is not None]
        for e in engs:
            E = self.E[e]
            waits = []
            for (sem, val, src, _) in lasts:
                if src == e:
                    continue
                k = id(sem)
                if E["seen"].get(k, 0) >= val:
                    continue
                E["seen"][k] = val
                waits.append((sem, val))
            if waits:
                E["ops"].append((waits, [], None))
        E = self.E["sp"]
        waits = []
        for (sem, val, src, _) in lasts:
            k = id(sem)
            if E["seen"].get(k, 0) >= val:
                continue
            E["seen"][k] = val
            waits.append((sem, val))
        if waits:
            E["ops"].append((waits, [], None))

    def final_wait(self, eng, keys):
        if self.dry:
            return
        waits = []
        for k in keys:
            st = self.res.get(k)
            if not st:
                continue
            for t in [st[0]] + st[1]:
                if t is not None:
                    waits.append((t[0], t[1]))
        self.E[eng]["ops"].append((waits, [], None))

    def emit(self):
        nc = self.nc
        S = self

        def run(e, name):
            for waits, fns, tok in S.E[name]["ops"]:
                for sem, val in waits:
                    e.wait_ge(sem, val)
                inst = None
                for fn in fns:
                    inst = fn(e)
                if tok is not None and inst is not None:
                    inst.then_inc(tok[0], tok[3])

        with nc.Block() as block:
            @block.tensor
            def _(e):
                run(e, "pe")

            @block.scalar
            def _(e):
                run(e, "act")

            @block.vector
            def _(e):
                run(e, "dve")

            @block.gpsimd
            def _(e):
                run(e, "pool")

            @block.sync
            def _(e):
                run(e, "sp")


def _pk(w, k):
    n = w.shape[1]
    return np.ascontiguousarray(w.reshape(k, 128, n).transpose(1, 0, 2).reshape(128, k * n))


def weight_blocks(inp, cfg):
    B = {}
    for l, (kind, j) in enumerate(cfg):
        if kind == "A":
            w_in = inp["a_w_in"][j]
            gw = inp["a_gate_w"][j]
            for c in range(8):
                blk = np.concatenate([w_in[:, c * 128:(c + 1) * 128], w_in[:, 1024 + c * 128:1024 + (c + 1) * 128]], axis=1)
                B[("a_in", l, c)] = _pk(blk, 8)
                B[("a_gw", l, c)] = np.ascontiguousarray(gw[:, :, c].transpose(2, 0, 1, 3).reshape(128, 512))
            for half in range(2):
                B[("a_out", l, half)] = _pk(inp["a_w_out"][j][:, half * 512:(half + 1) * 512], 8)
        elif kind == "B":
            w_in = inp["b_w_in"][j]
            for blk in range(2):
                B[("b_in_u", l, blk)] = _pk(w_in[:, blk * 512:(blk + 1) * 512], 8)
            for blk in range(2):
                B[("b_in_v", l, blk)] = _pk(w_in[:, 1024 + blk * 512:1024 + (blk + 1) * 512], 8)
            B[("b_ws", l)] = np.ascontiguousarray(inp["b_w_s"][j].transpose(2, 0, 1).reshape(128, 1024))
            for half in range(2):
                B[("b_out", l, half)] = _pk(inp["b_w_out"][j][:, half * 512:(half + 1) * 512], 8)
        elif kind == "C":
            w_in = inp["c_w_in"][j]
            B[("c_ab", l)] = _pk(w_in[:, 4096:4128], 8)
            for hd in range(8):
                blk = np.concatenate([w_in[:, t * 1024 + hd * 128:t * 1024 + (hd + 1) * 128] for t in range(4)], axis=1)
                B[("c_in", l, hd)] = _pk(blk, 8)
                B[("c_out", l, hd)] = np.ascontiguousarray(inp["c_w_out"][j][hd * 128:(hd + 1) * 128, :])
        for fb in range(8):
            B[("mlp_up", l, fb)] = _pk(inp["mlp_w_up"][l][:, fb * 512:(fb + 1) * 512], 8)
            B[("mlp_dn", l, fb)] = _pk(inp["mlp_w_down"][l][fb * 512:(fb + 1) * 512, :], 4)
    return B


def _col(v):
    return np.ascontiguousarray(np.asarray(v).reshape(8, 128).T)


def param_blocks(inp, cfg):
    P = {}
    for l, (kind, j) in enumerate(cfg):
        P[("g_mix", l)] = _col(inp["norm_mix_g"][l])
        P[("g_mlp", l)] = _col(inp["norm_mlp_g"][l])
        if kind == "A":
            P[("a_cw", l)] = np.ascontiguousarray(inp["a_conv_w"][j].reshape(4, 8, 128).transpose(2, 1, 0).reshape(128, 32))
            P[("a_cb", l)] = _col(inp["a_conv_b"][j])
            P[("a_gb", l)] = np.ascontiguousarray(inp["a_gate_b"][j].transpose(3, 2, 0, 1).reshape(128, 32))
            P[("a_lam", l)] = np.ascontiguousarray(inp["a_lambda"][j].reshape(2, 8, 128).transpose(2, 1, 0).reshape(128, 16))
        elif kind == "C":
            P[("c_cw", l)] = np.ascontiguousarray(inp["c_conv_w"][j].reshape(4, 3, 8, 128).transpose(3, 2, 1, 0).reshape(128, 96))
            P[("c_alog", l)] = np.ascontiguousarray(np.broadcast_to(inp["c_a_log"][j].reshape(1, 16), (128, 16)))
            P[("c_dtb", l)] = np.ascontiguousarray(np.broadcast_to(inp["c_dt_bias"][j].reshape(1, 16), (128, 16)))
            P[("c_ng", l)] = np.ascontiguousarray(inp["c_norm_g"][j].reshape(128, 1))
    P[("g_fin",)] = _col(inp["norm_final_g"])
    return P


def paramb_blocks(inp, cfg):
    out = []
    for l, (kind, j) in enumerate(cfg):
        if kind == "B":
            row = np.concatenate([inp["b_ln_g"][j], inp["b_ln_b"][j], inp["b_b_s"][j].reshape(-1)])
            out.append(np.broadcast_to(row[None, :], (128, 3072)))
    if not out:
        out = [np.zeros((128, 3072), np.float32)]
    return np.ascontiguousarray(np.concatenate(out, axis=1)).astype(np.float32)


def layout(blocks):
    offs = {}
    o = 0
    for k, v in blocks.items():
        offs[k] = (o, v.shape[1])
        o += v.shape[1]
    return offs, o


def build(nseq, cfg, woffs, wtot, poffs, ptot, pbtot):
    nc = bass.Bass("TRN2", target_bir_lowering=False)
    xT = nc.dram_tensor("xT", [nseq, D, SEQ], F32, kind="ExternalInput").ap()
    ws = nc.dram_tensor("ws", [128, wtot], F32, kind="ExternalInput").ap()
    par = nc.dram_tensor("par", [128, ptot], F32, kind="ExternalInput").ap()
    parb = nc.dram_tensor("parb", [128, pbtot], F32, kind="ExternalInput").ap()
    yT = nc.dram_tensor("yT", [nseq, D, SEQ], F32, kind="ExternalOutput").ap()

    with contextlib.ExitStack() as st:
        S = Sched(nc, st)
        T = lambda name, shape, dt: st.enter_context(nc.sbuf_tensor(name, shape, dt))
        X = T("X", [128, 8, SEQ], F32)
        H = T("H", [128, 8, SEQ], BF16)
        RING = [T("ring%d" % i, [128, 4096], BF16) for i in range(NSLOT)]
        PAR = T("PAR", [128, ptot], F32)
        PC = T("PC", [128, 64], F32)
        ZC = T("ZC", [128, 2], F32)
        IDB = T("IDB", [128, 128], BF16)
        ONB = T("ONB", [128, 128], BF16)
        IDF = T("IDF", [128, 128], F32)
        MSK = T("MSK", [128, 4, 128], F32)
        LMK = T("LMK", [128, 7, 128], BF16)
        AR = T("AR", [128, ARENA_COLS], F32)
        PS = st.enter_context(nc.psum_tensor("PS", [128, 4096], F32))

        def psb(b, n=1):
            return PS[:, b * 512:(b + n) * 512]

        def pk(b, n=1):
            return [("ps", b + i) for i in range(n)]

        class Arena:
            def __init__(self):
                self.off = 0
                self.tag = 0

            def reset(self):
                self.off = 0
                self.tag += 1

            def f32(self, n):
                v = AR[:, self.off:self.off + n]
                self.off += n
                assert self.off <= ARENA_COLS, self.off
                return v

            def bf16(self, n):
                assert n % 2 == 0
                v = AR[:, self.off:self.off + n // 2].bitcast(BF16)
                self.off += n // 2
                assert self.off <= ARENA_COLS, self.off
                return v

        A = Arena()

        def act(out, in_, func, R, W, scale=1.0, bias=0.0):
            S.op("act", lambda e: e.activation(out=out, in_=in_, func=func, scale=scale, bias=bias), R, W)

        def tt(out, in0, in1, op, R, W, eng="dve"):
            S.op(eng, lambda e: e.tensor_tensor(out=out, in0=in0, in1=in1, op=op), R, W)

        def tsc(out, in0, s1, op0, R, W, s2=None, op1=None, eng="dve"):
            if op1 is None:
                S.op(eng, lambda e: e.tensor_scalar(out=out, in0=in0, scalar1=s1, scalar2=None, op0=op0), R, W)
            else:
                S.op(eng, lambda e: e.tensor_scalar(out=out, in0=in0, scalar1=s1, scalar2=s2, op0=op0, op1=op1), R, W)

        def stt(out, in0, scalar, in1, op0, op1, R, W):
            S.op("dve", lambda e: e.scalar_tensor_tensor(out=out, in0=in0, scalar=scalar, in1=in1, op0=op0, op1=op1), R, W)

        def mm(out, pairs, R, W):
            n = len(pairs)
            fns = []
            for i, (l, r) in enumerate(pairs):
                fns.append(lambda e, l=l, r=r, i=i: e.matmul(out, lhsT=l, rhs=r, start=(i == 0), stop=(i == n - 1)))
            S.op("pe", fns, R, W)

        def copy(out, in_, R, W, eng="dve"):
            S.op(eng, lambda e: e.tensor_copy(out=out, in_=in_), R, W)

        class WRing:
            def __init__(self):
                self.sched = []
                self.pos = 0
                self.issued = 0
                self.released = 0

            def reset(self):
                self.pos = 0
                self.issued = 0
                self.released = 0

            def pump(self):
                if S.dry:
                    return
                while self.issued < len(self.sched) and self.issued - NSLOT < self.released:
                    i = self.issued
                    off, n = woffs[self.sched[i]]
                    slot = i % NSLOT
                    dst = RING[slot][:, 0:n]
                    src = ws[:, off:off + n]
                    S.op("pool", lambda e, dst=dst, src=src: e.dma_start(out=dst, in_=src, max_dma_last_dim=8192),
                         W=[("ring", slot)], dma=("ring", slot))
                    self.issued += 1

            def get(self, name):
                if S.dry:
                    self.sched.append(name)
                    return RING[0], ("ring", 0)
                i = self.pos
                assert self.sched[i] == name, (self.sched[i], name)
                self.pos += 1
                self.pump()
                assert self.issued > i, "weight ring deadlock at %s" % (name,)
                return RING[i % NSLOT], ("ring", i % NSLOT)

            def done(self):
                if S.dry:
                    return
                self.released += 1
                self.pump()

        WR = WRing()

        def pcol(name, c, n=1):
            o, _ = poffs[name]
            return PAR[:, o + c:o + c + n]

        def setup():
            S.op("sp", lambda e: e.dma_start(out=PAR[:], in_=par[:, :]), W=["PAR"], dma="PAR")
            S.op("dve", lambda e: e.memset(ONB[:], 1.0), W=["ONB"])
            S.op("dve", lambda e: e.memset(ZC[:], 0.0), W=["ZC"])
            S.op("dve", lambda e: e.memset(IDF[:], 0.0), W=["IDF"])
            S.op("pool", lambda e: e.affine_select(out=IDF[:], in_=IDF[:], pattern=[[-1, 128]], base=0, channel_multiplier=1,
                                                   compare_op=ALU.not_equal, fill=1.0), R=["IDF"], W=["IDF"])
            copy(IDB[:], IDF[:], ["IDF"], ["IDB"])
            S.op("dve", lambda e: e.memset(MSK[:], 1.0), W=["MSK"])
            specs = [
                (0, 1, -1, 0, ALU.is_gt),
                (1, -1, 1, 0, ALU.is_gt),
                (2, -1, 1, 0, ALU.is_ge),
                (3, 1, -1, 0, ALU.is_ge),
            ]
            for (i, cm, stp, base, cmp_) in specs:
                S.op("pool", lambda e, i=i, cm=cm, stp=stp, base=base, cmp_=cmp_: e.affine_select(
                    out=MSK[:, i, :], in_=MSK[:, i, :], pattern=[[stp, 128]], base=base, channel_multiplier=cm,
                    compare_op=cmp_, fill=0.0), R=["MSK"], W=["MSK"])
            A.reset()
            Et = A.f32(128)
            BD = [A.f32(128), A.f32(128)]
            prev = IDF[:]
            prevk = "IDF"
            for li in range(7):
                s2 = 2 << li
                cur = BD[li % 2]
                curk = ("BD", li % 2)
                if s2 == 128:
                    S.op("dve", lambda e, cur=cur: e.memset(cur, 1.0), W=[curk])
                else:
                    nb = 128 // s2
                    S.op("dve", lambda e, nb=nb: e.memset(Et[0:nb, :], 1.0), W=["Et"])
                    S.op("pool", lambda e, nb=nb, s2=s2: e.affine_select(out=Et[0:nb, :], in_=Et[0:nb, :], pattern=[[1, 128]], base=0,
                                                                       channel_multiplier=-s2, compare_op=ALU.is_ge, fill=0.0),
                         R=["Et"], W=["Et"])
                    S.op("pool", lambda e, nb=nb, s2=s2: e.affine_select(out=Et[0:nb, :], in_=Et[0:nb, :], pattern=[[-1, 128]], base=s2 - 1,
                                                                       channel_multiplier=s2, compare_op=ALU.is_ge, fill=0.0),
                         R=["Et"], W=["Et"])
                    mm(PS[:, 0:128], [(Et[0:nb, :], Et[0:nb, :])], ["Et"], pk(0))
                    copy(cur, PS[:, 0:128], pk(0), [curk])
                tt(LMK[:, li, :], cur, prev, ALU.subtract, [curk, prevk], ["LMK"])
                prev, prevk = cur, curk
            S.barrier()

        def rmsnorm(gname, out_x=False):
            A.reset()
            SQ = [A.bf16(8 * 512).rearrange("p (k n) -> p k n", k=8) for _ in range(2)]
            RS = [A.f32(512) for _ in range(2)]
            for t in range(4):
                b = t % 2
                tsl = slice(t * 512, (t + 1) * 512)
                for k in range(8):
                    act(SQ[b][:, k, :], X[:, k, tsl], AF.Square, [("x", k, t)], [("sq", b, k)])
                bank = 2 * b
                mm(psb(bank), [(ONB[:], SQ[b][:, k, :]) for k in range(8)],
                   ["ONB"] + [("sq", b, k) for k in range(8)], pk(bank))
                act(RS[b], psb(bank), AF.Sqrt, pk(bank), [("rs", b)], scale=1.0 / D, bias=1e-6)
                S.op("dve", lambda e, b=b: e.reciprocal(out=RS[b], in_=RS[b]), [("rs", b)], [("rs", b)])
                for k in range(8):
                    if out_x:
                        stt(X[:, k, tsl], X[:, k, tsl], pcol(gname, k), RS[b], ALU.mult, ALU.mult,
                            [("x", k, t), ("rs", b), "PAR"], [("x", k, t)])
                    else:
                        stt(H[:, k, tsl], X[:, k, tsl], pcol(gname, k), RS[b], ALU.mult, ALU.mult,
                            [("x", k, t), ("rs", b), "PAR"], [("h", k, t)])
            S.barrier()

        def hkeys(t):
            return [("h", k, t) for k in range(8)]

        def outproj(name_fn, YT, ykeys):
            for half in range(2):
                slot, rk = WR.get(name_fn(half))
                wo = slot[:, 0:4096].rearrange("p (c n) -> p c n", c=8)
                for t in range(4):
                    tsl = slice(t * 512, (t + 1) * 512)
                    for mo in range(4):
                        bank = (t * 4 + mo) % 8
                        mm(psb(bank), [(wo[:, cc, mo * 128:(mo + 1) * 128], YT[:, cc, tsl]) for cc in range(8)],
                           [rk] + ykeys, pk(bank))
                        ko = half * 4 + mo
                        tt(X[:, ko, tsl], psb(bank), X[:, ko, tsl], ALU.add, pk(bank) + [("x", ko, t)], [("x", ko, t)])
                WR.done()

        def mlp(l):
            A.reset()
            H1 = [A.bf16(4 * 512).rearrange("p (c n) -> p c n", c=4) for _ in range(2)]
            SQ1 = [A.bf16(512) for _ in range(4)]
            steps = [(fb, t) for fb in range(8) for t in range(4)]
            held = {}

            def up(i):
                fb, t = steps[i]
                if t == 0:
                    held[("u", fb)] = WR.get(("mlp_up", l, fb))
                slot, rk = held[("u", fb)]
                wu = slot[:, 0:4096].rearrange("p (k n) -> p k n", k=8)
                tsl = slice(t * 512, (t + 1) * 512)
                hb = i % 2
                for mi in range(4):
                    bank = mi
                    mm(psb(bank), [(wu[:, k, mi * 128:(mi + 1) * 128], H[:, k, tsl]) for k in range(8)],
                       [rk] + hkeys(t), pk(bank))
                    act(SQ1[mi], psb(bank), AF.Square, pk(bank), [("sq1", mi)])
                    stt(H1[hb][:, mi, :], psb(bank), 0.0, SQ1[mi], ALU.is_gt, ALU.mult,
                        pk(bank) + [("sq1", mi)], [("h1", hb, mi)])
                if t == 3:
                    WR.done()

            def down(i):
                fb, t = steps[i]
                if t == 0:
                    held[("d", fb)] = WR.get(("mlp_dn", l, fb))
                slot, rk = held[("d", fb)]
                wd = slot[:, 0:4096].rearrange("p (c n) -> p c n", c=4)
                tsl = slice(t * 512, (t + 1) * 512)
                hb = i % 2
                for mo in range(8):
                    bank = 4 + (mo % 4)
                    mm(psb(bank), [(wd[:, c, mo * 128:(mo + 1) * 128], H1[hb][:, c, :]) for c in range(4)],
                       [rk] + [("h1", hb, c) for c in range(4)], pk(bank))
                    tt(X[:, mo, tsl], psb(bank), X[:, mo, tsl], ALU.add, pk(bank) + [("x", mo, t)], [("x", mo, t)])
                if t == 3:
                    WR.done()

            n = len(steps)
            for i in range(n + 1):
                if i < n:
                    up(i)
                if i >= 1:
                    down(i - 1)
            S.barrier()

        def mixer_a(l):
            A.reset()
            YT = A.bf16(8 * SEQ).rearrange("p (c n) -> p c n", c=8)
            XP = A.f32(SEQ + 4)
            XR = A.f32(SEQ)
            XB = A.bf16(SEQ)
            T1 = A.f32(SEQ)
            T2 = A.f32(SEQ)
            T3 = A.f32(SEQ)
            lo, _ = poffs[("a_lam", l)]
            act(PC[:, 32:48], PAR[:, lo:lo + 16], AF.Exp, ["PAR"], ["PC"], scale=-1.0)
            act(PC[:, 32:48], PC[:, 32:48], AF.Ln, ["PC"], ["PC"], bias=1.0)
            tsc(PC[:, 0:16], PC[:, 32:48], -8.0, ALU.mult, ["PC"], ["PC"])
            tsc(PC[:, 16:32], PC[:, 32:48], -16.0, ALU.mult, ["PC"], ["PC"])
            S.op("dve", lambda e: e.memset(XP[:, 0:2], 0.0), W=[("XP", 0)])
            S.op("dve", lambda e: e.memset(XP[:, SEQ + 2:SEQ + 4], 0.0), W=[("XP", 1)])
            HW_ = SEQ // 2

            def a_gen(c, hf, wv, rk, gw, rkg):
                hs = slice(hf * HW_, (hf + 1) * HW_)
                xs = slice(2 + hf * HW_, 2 + (hf + 1) * HW_)
                t0 = 2 * hf
                kXP, kXR, kXB, kT1, kT2, kT3 = [(nm, hf) for nm in ("XP", "XR", "XB", "T1", "T2", "T3")]
                XPK = [("XP", 0), ("XP", 1)]
                for t in (t0, t0 + 1):
                    tsl = slice(t * 512, (t + 1) * 512)
                    mm(psb(t), [(wv[:, k, 0:128], H[:, k, tsl]) for k in range(8)], [rk] + hkeys(t), pk(t))
                    mm(psb(4 + t), [(wv[:, k, 128:256], H[:, k, tsl]) for k in range(8)], [rk] + hkeys(t), pk(4 + t))
                yield
                act(YT[:, c, hs], psb(t0, 2), AF.Gelu_apprx_tanh, pk(t0, 2), [("yt", c)])
                act(XP[:, xs], psb(4 + t0, 2), AF.Copy, pk(4 + t0, 2), [kXP])
                yield
                tsc(XR[:, hs], XP[:, hf * HW_:hf * HW_ + HW_], pcol(("a_cw", l), c * 4 + 0), ALU.mult, XPK + ["PAR"], [kXR],
                    s2=pcol(("a_cb", l), c), op1=ALU.add)
                for tap in range(1, 4):
                    stt(XR[:, hs], XP[:, hf * HW_ + tap:hf * HW_ + tap + HW_], pcol(("a_cw", l), c * 4 + tap), XR[:, hs],
                        ALU.mult, ALU.add, XPK + [kXR, "PAR"], [kXR])
                yield
                act(XB[:, hs], XR[:, hs], AF.Copy, [kXR], [kXB])
                yield
                for dr in range(2):
                    for g in range(2):
                        for t in (t0, t0 + 1):
                            tsl = slice(t * 512, (t + 1) * 512)
                            mm(psb(g * 4 + t), [(gw[:, dr * 2 + g, :], XB[:, tsl])], [rkg, kXB], pk(g * 4 + t))
                    yield
                    gbo = c * 4 + dr * 2
                    act(T1[:, hs], psb(t0, 2), AF.Sigmoid, pk(t0, 2) + ["PAR"], [kT1], bias=pcol(("a_gb", l), gbo))
                    act(T2[:, hs], psb(4 + t0, 2), AF.Sigmoid, pk(4 + t0, 2) + ["PAR"], [kT2], bias=pcol(("a_gb", l), gbo + 1))
                    ci = c * 2 + dr
                    act(T3[:, hs], T1[:, hs], AF.Exp, [kT1, "PC"], [kT3], scale=PC[:, ci:ci + 1])
                    act(T1[:, hs], T1[:, hs], AF.Exp, [kT1, "PC"], [kT1], scale=PC[:, 16 + ci:16 + ci + 1])
                    act(T1[:, hs], T1[:, hs], AF.Sqrt, [kT1], [kT1], scale=-1.0, bias=1.0)
                    yield
                    tt(T2[:, hs], T2[:, hs], T1[:, hs], ALU.mult, [kT1, kT2], [kT2], eng="pool")
                    tt(T2[:, hs], T2[:, hs], XR[:, hs], ALU.mult, [kT2, kXR], [kT2], eng="pool")
                    yield
                    if dr == 0:
                        if hf == 0:
                            S.op("dve", lambda e: e.tensor_tensor_scan(out=XP[:, xs], data0=T3[:, hs], data1=T2[:, hs], initial=0.0,
                                                                       op0=ALU.mult, op1=ALU.add), [kT3, kT2], [kXP])
                        else:
                            S.op("dve", lambda e: e.tensor_tensor_scan(out=XP[:, xs], data0=T3[:, hs], data1=T2[:, hs],
                                                                       initial=XP[:, 2 + HW_ - 1:2 + HW_],
                                                                       op0=ALU.mult, op1=ALU.add), [kT3, kT2, ("XP", 0)], [kXP])
                    else:
                        if hf == 1:
                            S.op("dve", lambda e: e.tensor_tensor_scan(out=T1[:, hs][:, ::-1], data0=T3[:, hs][:, ::-1],
                                                                       data1=T2[:, hs][:, ::-1], initial=0.0,
                                                                       op0=ALU.mult, op1=ALU.add), [kT3, kT2], [kT1])
                        else:
                            yield
                            S.op("dve", lambda e: e.tensor_tensor_scan(out=T1[:, hs][:, ::-1], data0=T3[:, hs][:, ::-1],
                                                                       data1=T2[:, hs][:, ::-1], initial=T1[:, HW_:HW_ + 1],
                                                                       op0=ALU.mult, op1=ALU.add), [kT3, kT2, ("T1", 1)], [kT1])
                    yield
                tt(T1[:, hs], T1[:, hs], XP[:, xs], ALU.add, [kT1, kXP], [kT1])
                tt(YT[:, c, hs], T1[:, hs], YT[:, c, hs], ALU.mult, [kT1, ("yt", c)], [("yt", c)])

            for c in range(8):
                slot, rk = WR.get(("a_in", l, c))
                wv = slot[:, 0:2048].rearrange("p (k n) -> p k n", k=8)
                slotg, rkg = WR.get(("a_gw", l, c))
                gw = slotg[:, 0:512].rearrange("p (q o) -> p q o", q=4)
                gens = [a_gen(c, 0, wv, rk, gw, rkg), a_gen(c, 1, wv, rk, gw, rkg)]
                while gens:
                    for g_ in list(gens):
                        try:
                            next(g_)
                        except StopIteration:
                            gens.remove(g_)
                WR.done()
                WR.done()
            outproj(lambda half: ("a_out", l, half), YT, [("yt", c) for c in range(8)])
            S.barrier()

        def mixer_b(l, bidx):
            A.reset()
            UT = A.bf16(8 * SEQ).rearrange("p (c n) -> p c n", c=8)
            LGB = A.f32(3072)
            VT = [A.f32(1024) for _ in range(2)]
            VN = [A.bf16(1024) for _ in range(2)]
            TB = [A.f32(1024) for _ in range(2)]
            STT = [A.f32(12) for _ in range(2)]
            MV = [A.f32(4) for _ in range(2)]
            S.op("sp", lambda e: e.dma_start(out=LGB, in_=parb[:, bidx * 3072:(bidx + 1) * 3072]), W=["LGB"], dma="LGB")
            for blk in range(2):
                slot, rk = WR.get(("b_in_u", l, blk))
                wb = slot[:, 0:4096].rearrange("p (k n) -> p k n", k=8)
                for t in range(4):
                    tsl = slice(t * 512, (t + 1) * 512)
                    b0 = (t % 2) * 4
                    for mi in range(4):
                        mm(psb(b0 + mi), [(wb[:, k, mi * 128:(mi + 1) * 128], H[:, k, tsl]) for k in range(8)],
                           [rk] + hkeys(t), pk(b0 + mi))
                    act(UT[:, blk * 4:(blk + 1) * 4, tsl], PS[:, b0 * 512:(b0 + 4) * 512].rearrange("p (c n) -> p c n", c=4),
                        AF.Gelu_apprx_tanh, pk(b0, 4), [("ut", blk * 4 + mi, t) for mi in range(4)])
                WR.done()
            slot0, rk0 = WR.get(("b_in_v", l, 0))
            slot1, rk1 = WR.get(("b_in_v", l, 1))
            slot2, rk2 = WR.get(("b_ws", l))
            wvv = [slot0[:, 0:4096].rearrange("p (k n) -> p k n", k=8), slot1[:, 0:4096].rearrange("p (k n) -> p k n", k=8)]
            rkv = [rk0, rk1]
            wst = slot2[:, 0:1024].rearrange("p (g n) -> p g n", g=8)
            for tt_ in range(16):
                b = tt_ % 2
                tok = slice(tt_ * 128, (tt_ + 1) * 128)
                t = tt_ // 4
                pb = b * 4
                for blk in range(2):
                    mm(psb(pb + blk), [(H[:, k, tok], wvv[blk][:, k, :]) for k in range(8)], [rkv[blk]] + hkeys(t), pk(pb + blk))
                act(VT[b], psb(pb, 2), AF.Gelu_apprx_tanh, pk(pb, 2), [("vt", b)])
                S.op("dve", lambda e, b=b: e.bn_stats(out=STT[b][:, 0:6], in_=VT[b][:, 0:512]), [("vt", b)], [("st", b)])
                S.op("dve", lambda e, b=b: e.bn_stats(out=STT[b][:, 6:12], in_=VT[b][:, 512:1024]), [("vt", b)], [("st2", b)])
                S.op("dve", lambda e, b=b: e.bn_aggr(out=MV[b][:, 0:2], in_=STT[b][:, 0:12]), [("st", b), ("st2", b)], [("mv", b)])
                act(MV[b][:, 2:3], MV[b][:, 1:2], AF.Sqrt, [("mv", b)], [("mv2", b)], bias=1e-5)
                S.op("dve", lambda e, b=b: e.reciprocal(out=MV[b][:, 2:3], in_=MV[b][:, 2:3]), [("mv2", b)], [("mv2", b)])
                tsc(VT[b], VT[b], MV[b][:, 0:1], ALU.subtract, [("vt", b), ("mv", b), ("mv2", b)], [("vt", b)],
                    s2=MV[b][:, 2:3], op1=ALU.mult)
                tt(VT[b], VT[b], LGB[:, 0:1024], ALU.mult, [("vt", b), "LGB"], [("vt", b)], eng="pool")
                tt(VN[b], VT[b], LGB[:, 1024:2048], ALU.add, [("vt", b), "LGB"], [("vn", b)], eng="pool")
                for g in range(8):
                    bank = pb + 2 + g // 4
                    o = PS[:, bank * 512 + (g % 4) * 128: bank * 512 + (g % 4 + 1) * 128]
                    mm(o, [(VN[b][:, g * 128:(g + 1) * 128], wst[:, g, :])], [("vn", b), rk2], [("ps", bank)])
                tt(TB[b], psb(pb + 2, 2), LGB[:, 2048:3072], ALU.add, pk(pb + 2, 2) + ["LGB"], [("tb", b)])
                uk = [("ut", g, t) for g in range(8)]
                tt(UT[:, :, tok], TB[b][:].rearrange("p (g n) -> p g n", g=8), UT[:, :, tok], ALU.mult,
                   [("tb", b)] + uk, uk, eng="pool")
            WR.done()
            WR.done()
            WR.done()
            outproj(lambda half: ("b_out", l, half), UT, [("ut", g, t) for g in range(8) for t in range(4)])
            S.barrier()

        def mixer_c(l):
            A.reset()
            QSC = 128.0 ** -0.5
            S1K = [("S1", 0), ("S1", 1)]
            S2K = [("S2", 0), ("S2", 1)]
            S3K = [("gcb", c) for c in range(16)]
            BTOK = A.f32(512)
            BT3 = BTOK.rearrange("p (c n) -> p c n", c=16)
            WAB = A.bf16(256).rearrange("p (k n) -> p k n", k=8)
            SM = A.f32(128)
            GCT, EGT, NEGT, KDT, NBT, EGLT = [SM[:, i * 16:(i + 1) * 16] for i in range(6)]
            QT = A.bf16(SEQ)
            KT = A.bf16(SEQ)
            KTOK = A.bf16(SEQ).rearrange("p (c n) -> p c n", c=16)
            VTOK = A.bf16(SEQ).rearrange("p (c n) -> p c n", c=16)
            OT = A.f32(SEQ)
            S1 = A.f32(SEQ + 4)
            S2 = A.f32(SEQ)
            S3 = A.f32(SEQ)
            QG = A.bf16(SEQ)
            YS = A.bf16(SEQ).rearrange("p (c n) -> p c n", c=16)
            QKD = A.bf16(SEQ).rearrange("p (c n) -> p c n", c=16)
            NBC = 4
            An = A.f32(NBC * 128).rearrange("p (c n) -> p c n", c=NBC)
            At = A.f32(NBC * 128).rearrange("p (c n) -> p c n", c=NBC)
            Tt = A.f32(NBC * 128).rearrange("p (c n) -> p c n", c=NBC)
            Yy = A.f32(NBC * 128).rearrange("p (c n) -> p c n", c=NBC)
            D1 = A.f32(NBC * 128)
            D2 = A.f32(NBC * 128)
            M1s = An
            WREP = D1.bitcast(BF16).rearrange("p (k n) -> p k n", k=8)
            RH = [A.bf16(128) for _ in range(2)]
            VN = [A.bf16(128) for _ in range(2)]
            VN2 = [A.bf16(128) for _ in range(2)]
            S32 = A.f32(128)
            SB = A.bf16(128)
            RAW, ACC, SIL = S1, S2, S3
            D13 = D1.rearrange("p (c n) -> p c n", c=NBC)
            D23 = D2.rearrange("p (c n) -> p c n", c=NBC)

            def bank3(b):
                return PS[:, b * 512:b * 512 + NBC * 128].rearrange("p (c n) -> p c n", c=NBC)

            ao, _ = poffs[("c_alog", l)]
            act(PC[:, 48:64], PAR[:, ao:ao + 16], AF.Exp, ["PAR"], ["PCc"])
            tsc(PC[:, 48:64], PC[:, 48:64], -1.0, ALU.mult, ["PCc"], ["PCc"])
            S.op("dve", lambda e: e.memset(S1[:, 0:2], 0.0), W=[*S1K])
            S.op("dve", lambda e: e.memset(S1[:, SEQ + 2:SEQ + 4], 0.0), W=[*S1K])
            slot, rk = WR.get(("c_ab", l))
            wab = slot[:, 0:256].rearrange("p (k n) -> p k n", k=8)
            copy(WAB, wab, [rk], ["WAB"])
            for c in range(16):
                tok = slice(c * 128, (c + 1) * 128)
                mm(PS[:, c * 32:(c + 1) * 32], [(H[:, k, tok], wab[:, k, :]) for k in range(8)], [rk] + hkeys(c // 4), pk(0))
            WR.done()
            act(BTOK, psb(0), AF.Sigmoid, pk(0), ["BTOK"])

            for hd in range(8):
                slot, rk = WR.get(("c_in", l, hd))
                win = slot[:, 0:4096].rearrange("p (k n) -> p k n", k=8)
                cwo, _ = poffs[("c_cw", l)]
                SQ = YS.rearrange("p c n -> p (c n)")
                VTf = QG

                def proj_gen(typ, hf):
                    pb = (typ % 2) * 4
                    t0 = 2 * hf
                    HW_ = SEQ // 2
                    hs = slice(hf * HW_, (hf + 1) * HW_)
                    ys_k = [("ys", b_) for b_ in range(hf * (8 // NBC), (hf + 1) * (8 // NBC))]
                    ot_k = [("ot", c) for c in range(8 * hf, 8 * hf + 8)]
                    s3_k = [("gcb", c) for c in range(8 * hf, 8 * hf + 8)]
                    for t in (t0, t0 + 1):
                        tsl = slice(t * 512, (t + 1) * 512)
                        mm(psb(pb + t), [(win[:, k, typ * 128:(typ + 1) * 128], H[:, k, tsl]) for k in range(8)],
                           [rk] + hkeys(t), pk(pb + t))
                    yield
                    act(RAW[:, 2 + hf * HW_:2 + (hf + 1) * HW_], psb(pb + t0, 2), AF.Copy, pk(pb + t0, 2), [("S1", hf)])
                    yield
                    co = cwo + hd * 12 + typ * 4
                    tsc(ACC[:, hs], RAW[:, hf * HW_:hf * HW_ + HW_], PAR[:, co:co + 1], ALU.mult, S1K + ["PAR"], [("S2", hf)])
                    for tap in range(1, 4):
                        stt(ACC[:, hs], RAW[:, hf * HW_ + tap:hf * HW_ + tap + HW_], PAR[:, co + tap:co + tap + 1], ACC[:, hs],
                            ALU.mult, ALU.add, S1K + [("S2", hf), "PAR"], [("S2", hf)])
                    yield
                    p3d = PS[:, hf * HW_:(hf + 1) * HW_].rearrange("p (c n) -> p c n", c=8)
                    if typ == 2:
                        act(VTf[:, hs], ACC[:, hs], AF.Silu, [("S2", hf)], ["QG"])
                        yield
                        for c in range(8 * hf, 8 * hf + 8):
                            mm(PS[:, c * 128:(c + 1) * 128], [(VTf[:, c * 128:(c + 1) * 128], IDB[:])], ["QG", "IDB"], pk(c // 4))
                        yield
                        copy(VTOK[:, 8 * hf:8 * hf + 8, :], p3d, pk(2 * hf, 2), ["VTOK"])
                    else:
                        act(SIL[:, hs], ACC[:, hs], AF.Silu, [("S2", hf)], s3_k)
                        yield
                        act(SQ[:, hs], SIL[:, hs], AF.Square, s3_k, ys_k)
                        yield
                        pb2 = 4 - pb
                        for t in (t0, t0 + 1):
                            tsl = slice(t * 512, (t + 1) * 512)
                            mm(psb(pb2 + t), [(ONB[:], SQ[:, tsl])], ["ONB"] + ys_k, pk(pb2 + t))
                        yield
                        act(OT[:, hs], psb(pb2 + t0, 2), AF.Sqrt, pk(pb2 + t0, 2), ot_k, bias=1e-6)
                        yield
                        S.op("dve", lambda e: e.reciprocal(out=OT[:, hs], in_=OT[:, hs]), ot_k, ot_k)
                        yield
                        if typ == 0:
                            stt(QT[:, hs], SIL[:, hs], QSC, OT[:, hs], ALU.mult, ALU.mult, s3_k + ot_k, ["QT"])
                        else:
                            tt(KT[:, hs], SIL[:, hs], OT[:, hs], ALU.mult, s3_k + ot_k, ["KT"])
                            yield
                            for c in range(8 * hf, 8 * hf + 8):
                                mm(PS[:, c * 128:(c + 1) * 128], [(KT[:, c * 128:(c + 1) * 128], IDB[:])], ["KT", "IDB"], pk(c // 4))
                            yield
                            act(KTOK[:, 8 * hf:8 * hf + 8, :], p3d, AF.Copy, pk(2 * hf, 2), ["KTOK"])

                for typ in range(3):
                    gens = [proj_gen(typ, 0), proj_gen(typ, 1)]
                    while gens:
                        for g in list(gens):
                            try:
                                next(g)
                            except StopIteration:
                                gens.remove(g)

                gk = [("gcb", c) for c in range(16)]
                for dr in range(2):
                    if KSTOPC <= 1:
                        continue
                    n = dr * 8 + hd
                    bcol = BT3[:, :, 16 + n]
                    copy(WREP, WAB[:, :, n:n + 1].to_broadcast([128, 8, 128]), ["WAB"], ["D1"])
                    for t in range(4):
                        tsl = slice(t * 512, (t + 1) * 512)
                        mm(psb(t), [(WREP[:, k, :], H[:, k, tsl]) for k in range(8)], ["D1"] + hkeys(t), pk(t))
                    G1, GCB, EG = S1[:, 2:SEQ + 2], S3, S2[:, 0:SEQ // 2].bitcast(BF16)
                    act(G1, psb(0, 4), AF.Exp, pk(0, 4) + ["PAR"], [*S1K], bias=pcol(("c_dtb", l), n))
                    act(G1, G1, AF.Ln, [*S1K], [*S1K], bias=1.0)
                    tsc(G1, G1, PC[:, 48 + n:48 + n + 1], ALU.mult, [*S1K, "PCc"], [*S1K])
                    for c in range(16):
                        sl = slice(c * 128, (c + 1) * 128)
                        if dr == 0:
                            S.op("dve", lambda e, sl=sl: e.tensor_tensor_scan(out=GCB[:, sl], data0=ONB[:], data1=G1[:, sl], initial=0.0,
                                                                              op0=ALU.mult, op1=ALU.add), [*S1K, "ONB"], [("gcb", c)])
                        else:
                            S.op("dve", lambda e, sl=sl: e.tensor_tensor_scan(out=GCB[:, sl][:, ::-1], data0=ONB[:], data1=G1[:, sl][:, ::-1],
                                                                              initial=0.0, op0=ALU.mult, op1=ALU.add), [*S1K, "ONB"], [("gcb", c)])
                    gk = [("gcb", c) for c in range(16)]
                    act(EG, GCB, AF.Exp, gk, [*S2K])
                    TMP = S1[:, 2:SEQ + 2].rearrange("p (c n) -> p c n", c=16)
                    tt(TMP, GCB.rearrange("p (c n) -> p c n", c=16), IDF[:].unsqueeze(1).to_broadcast([128, 16, 128]), ALU.mult,
                       gk + ["IDF"], [*S1K])
                    S.op("dve", lambda e: e.tensor_reduce(out=GCT, in_=TMP, op=ALU.add, axis=mybir.AxisListType.X), [*S1K], ["GCT"])
                    act(EGT, GCT, AF.Exp, ["GCT"], ["EGT"])
                    tsc(NEGT, EGT, -1.0, ALU.mult, ["EGT"], ["NEGT"])
                    tsc(NBT, bcol, -1.0, ALU.mult, ["BTOK"], ["NBT"])
                    lastc = 127 if dr == 0 else 0
                    GL = GCB[:, lastc::128]
                    act(EGLT, GL, AF.Exp, gk, ["EGLT"])
                    EGL = EGLT
                    tt(KDT, GL, GCT, ALU.subtract, gk + ["GCT"], ["KDT"])
                    act(KDT, KDT, AF.Exp, ["KDT"], ["KDT"])
                    tt(QG, QT, EG, ALU.mult, ["QT", *S2K], ["QG"])
                    SET1K = ["An1", "At1", "T1", "Y1", "D1x1", "D2x1"]
                    S.op("dve", lambda e: e.memset(ZC[:, 1:2], 0.0), W=S1K + S2K + SET1K + ["ZCf"])
                    idb3 = IDF[:].unsqueeze(1).to_broadcast([128, NBC, 128])

                    def lm3(li):
                        return LMK[:, li, :].unsqueeze(1).to_broadcast([128, NBC, 128])

                    def v3(ap):
                        return ap.rearrange("p (c n) -> p c n", c=NBC)

                    def batch_gen(bq, sid):
                        W_ = NBC * 128
                        if sid == 0:
                            base = [An.rearrange("p c n -> p (c n)"), At.rearrange("p c n -> p (c n)"), Tt.rearrange("p c n -> p (c n)"),
                                    Yy.rearrange("p c n -> p (c n)"), D1, D2]
                            b0 = 4
                        else:
                            base = [S1[:, 2 + i * W_:2 + (i + 1) * W_] for i in range(4)] + \
                                   [S2[:, SEQ // 2:SEQ // 2 + W_], S2[:, SEQ // 2 + W_:SEQ // 2 + 2 * W_]]
                            b0 = 0
                        An_, At_, Tt_, Yy_ = [v3(b) for b in base[0:4]]
                        D1_, D2_ = base[4], base[5]
                        if TBF16:
                            lowv = [v3(b.bitcast(BF16)[:, 0:W_]) for b in base]
                        else:
                            lowv = [v3(b) for b in base]
                        M1L, _, TtL, YyL, D1L, D2L = lowv
                        kA, kAt, kT_, kY, kD1, kD2 = ["%s%d" % (nm, sid) for nm in ("An", "At", "T", "Y", "D1x", "D2x")]
                        if sid == 0:
                            kD1, kD2 = "D1", "D2"
                        D13_, D23_ = v3(D1_), v3(D2_)
                        M1_ = An_

                        def pq(bi, ci):
                            return PS[:, (b0 + bi) * 512 + ci * 128:(b0 + bi) * 512 + (ci + 1) * 128]

                        def pq3(bi):
                            return v3(PS[:, (b0 + bi) * 512:(b0 + bi) * 512 + W_])
                        cs = [bq * NBC + ci for ci in range(NBC)]
                        for ci, c in enumerate(cs):
                            sl = slice(c * 128, (c + 1) * 128)
                            mm(pq(0, ci), [(KT[:, sl], KT[:, sl])], ["KT"], pk(b0))
                            mm(pq(1, ci), [(KT[:, sl], QT[:, sl])], ["KT", "QT"], pk(b0 + 1))
                        for ci, c in enumerate(cs):
                            sl = slice(c * 128, (c + 1) * 128)
                            tsc(D13_[:, ci, :], GCB[:, sl], GCT[:, c:c + 1], ALU.subtract, [("gcb", c), "GCT", "ZC"], [kD1], s2=ZC[:, 0:1], op1=ALU.max)
                            tsc(D23_[:, ci, :], GCB[:, sl], GCT[:, c:c + 1], ALU.subtract, [("gcb", c), "GCT", "ZC"], [kD2], s2=ZC[:, 0:1], op1=ALU.min)
                        yield
                        act(D1_, D1_, AF.Exp, [kD1], [kD1], scale=-1.0)
                        act(D2_, D2_, AF.Exp, [kD2], [kD2])
                        yield
                        tt(D13_, D13_, MSK[:, dr, :].unsqueeze(1).to_broadcast([128, NBC, 128]), ALU.mult, [kD1, "MSK"], [kD1])
                        tt(D23_, D23_, MSK[:, 2 + dr, :].unsqueeze(1).to_broadcast([128, NBC, 128]), ALU.mult, [kD2, "MSK"], [kD2])
                        for ci, c in enumerate(cs):
                            stt(An_[:, ci, :], pq(0, ci), bcol[:, c:c + 1], D13_[:, ci, :], ALU.mult, ALU.mult, pk(b0) + ["BTOK", kD1], [kA])
                        tt(QKD[:, bq * NBC:(bq + 1) * NBC, :], pq3(1), D23_, ALU.mult, pk(b0 + 1) + [kD2], [("qkd", bq)])
                        yield
                        for ci in range(NBC):
                            mm(pq(2, ci), [(An_[:, ci, :], IDF[:])], [kA, "IDF"], pk(b0 + 2))
                        yield
                        act(At_, pq3(2), AF.Copy, pk(b0 + 2), [kAt])
                        tt(D13_, An_, lm3(0), ALU.mult, [kA, "LMK"], [kD1], eng="pool")
                        yield
                        tt(D23_, At_, lm3(0), ALU.mult, [kAt, "LMK"], [kD2], eng="pool")
                        tt(TtL, idb3, D13_, ALU.subtract, ["IDF", kD1], [kT_])
                        yield
                        tt(YyL, idb3, D23_, ALU.subtract, ["IDF", kD2], [kY])
                        for li in range(1, 7):
                            last = li == 6
                            dbuf, dkey = (D1L, kD1) if li % 2 else (D2L, kD2)
                            tt(dbuf, At_, lm3(li), ALU.mult, [kAt, "LMK"], [dkey], eng="pool")
                            yield
                            for ci in range(NBC):
                                mm(pq(0, ci), [(dbuf[:, ci, :], TtL[:, ci, :])], [dkey, kT_], pk(b0))
                            yield
                            act(M1L, pq3(0), AF.Copy, pk(b0), [kA])
                            yield
                            if not last:
                                for ci in range(NBC):
                                    mm(pq(1, ci), [(YyL[:, ci, :], M1L[:, ci, :])], [kY, kA], pk(b0 + 1))
                            for ci in range(NBC):
                                mm(pq(2, ci), [(M1L[:, ci, :], YyL[:, ci, :])], [kY, kA], pk(b0 + 2))
                            yield
                            if not last:
                                tt(TtL, TtL, pq3(1), ALU.subtract, [kT_] + pk(b0 + 1), [kT_])
                            tt(YyL, YyL, pq3(2), ALU.subtract, [kY] + pk(b0 + 2), [kY])
                        yield
                        for ci, c in enumerate(cs):
                            tsc(YS[:, c, :], YyL[:, ci, :], bcol[:, c:c + 1], ALU.mult, [kY, "BTOK"], [("ys", bq)])

                    order = list(range(16)) if dr == 0 else list(range(15, -1, -1))

                    def rec_gen(si0, chunks):
                        for k_, c in enumerate(chunks):
                            si = si0 + k_
                            par_ = si % 2
                            sl = slice(c * 128, (c + 1) * 128)
                            bq = c // NBC
                            o_ = par_ * 256
                            p1 = PS[:, 3 * 512 + o_:3 * 512 + o_ + 128]
                            p3 = PS[:, 3 * 512 + o_ + 128:3 * 512 + o_ + 256]
                            p2 = PS[:, 7 * 512 + o_:7 * 512 + o_ + 128]
                            p4 = PS[:, 7 * 512 + o_ + 128:7 * 512 + o_ + 256]
                            mm(p1, [(KT[:, sl], SB)], ["KT", "SB"], pk(3))
                            yield
                            stt(RH[par_], p1, NEGT[:, c:c + 1], VTOK[:, c, :], ALU.mult, ALU.add, pk(3) + ["NEGT", "VTOK"], [("rh", par_)])
                            yield
                            mm(p2, [(YS[:, c, :], RH[par_])], [("ys", bq), ("rh", par_)], pk(7))
                            yield
                            act(VN[par_], p2, AF.Copy, pk(7), [("vn", par_)])
                            tsc(VN2[par_], p2, KDT[:, c:c + 1], ALU.mult, pk(7) + ["KDT"], [("vn2", par_)])
                            yield
                            mm(p3, [(SB, QG[:, sl]), (VN[par_], QKD[:, c, :])], ["SB", "QG", ("vn", par_), ("qkd", bq)], pk(3))
                            mm(p4, [(KTOK[:, c, :], VN2[par_])], ["KTOK", ("vn2", par_)], pk(7))
                            yield
                            stt(SB, S32, EGL[:, c:c + 1], p4, ALU.mult, ALU.add, ["S32", "EGLT"] + pk(7), ["SB"])
                            stt(S32, S32, EGL[:, c:c + 1], p4, ALU.mult, ALU.add, ["S32", "EGLT"] + pk(7), ["S32"])
                            if dr == 0:
                                act(OT[:, sl], p3, AF.Copy, pk(3), [("ot", c)])
                            else:
                                tt(OT[:, sl], p3, OT[:, sl], ALU.add, pk(3) + [("ot", c)], [("ot", c)])
                            yield

                    def drive(gens):
                        while gens:
                            for g in list(gens):
                                try:
                                    next(g)
                                except StopIteration:
                                    gens.remove(g)

                    npair = 16 // NBC // 2
                    pairs = list(range(npair)) if dr == 0 else list(range(npair - 1, -1, -1))
                    nper = 16 // npair
                    if KSTOPC > 2:
                        drive([batch_gen(2 * pairs[0], 0), batch_gen(2 * pairs[0] + 1, 1)])
                    S.op("dve", lambda e: e.memset(S32, 0.0), W=["S32"])
                    S.op("dve", lambda e: e.memset(SB, 0.0), W=["SB"])
                    for pi in range(1, npair + 1):
                        gens = []
                        if pi < npair and KSTOPC > 2:
                            gens += [batch_gen(2 * pairs[pi], 0), batch_gen(2 * pairs[pi] + 1, 1)]
                        if KSTOPC > 5:
                            gens.append(rec_gen((pi - 1) * nper, order[(pi - 1) * nper:pi * nper]))
                        drive(gens)
                    S.op("dve", lambda e: e.memset(ZC[:, 1:2], 0.0), W=S1K + S2K + SET1K + ["ZCf"])
                ok_ = [("ot", c) for c in range(16)]
                SQ = YS.rearrange("p c n -> p (c n)")
                act(SQ, OT, AF.Square, ok_, [("ys", b_) for b_ in range(16 // NBC)])
                for t in range(4):
                    tsl = slice(t * 512, (t + 1) * 512)
                    mm(psb(t), [(ONB[:], SQ[:, tsl])], ["ONB"] + [("ys", b_) for b_ in range(16 // NBC)], pk(t))
                act(S3, psb(0, 4), AF.Sqrt, pk(0, 4), gk + [*S3K], scale=1.0 / 128, bias=1e-6)
                S.op("dve", lambda e: e.reciprocal(out=S3, in_=S3), gk + [*S3K], gk + [*S3K])
                for t in range(4):
                    tsl = slice(t * 512, (t + 1) * 512)
                    mm(psb(4 + t), [(win[:, k, 384:512], H[:, k, tsl]) for k in range(8)], [rk] + hkeys(t), pk(4 + t))
                WR.done()
                act(S2, psb(4, 4), AF.Silu, pk(4, 4), [*S2K])
                tt(S3, S3, OT, ALU.mult, gk + [*S3K] + ok_, gk + [*S3K])
                stt(QG, S3, pcol(("c_ng", l), 0), S2, ALU.mult, ALU.mult, gk + [*S3K, *S2K, "PAR"], ["QG"])
                slot, rk = WR.get(("c_out", l, hd))
                wo = slot[:, 0:1024]
                for t in range(4):
                    tsl = slice(t * 512, (t + 1) * 512)
                    for mo in range(8):
                        bank = mo
                        mm(psb(bank), [(wo[:, mo * 128:(mo + 1) * 128], QG[:, tsl])], [rk, "QG"], pk(bank))
                        tt(X[:, mo, tsl], psb(bank), X[:, mo, tsl], ALU.add, pk(bank) + [("x", mo, t)], [("x", mo, t)])
                WR.done()
            S.barrier()

        def program():
            WR.reset()
            setup()
            for s in range(nseq):
                for t in range(4):
                    tsl = slice(t * 512, (t + 1) * 512)
                    S.op("sp", lambda e, s=s, tsl=tsl: e.dma_start(
                        out=X[:, :, tsl], in_=xT[s].rearrange("(k p) n -> p k n", p=128)[:, :, tsl]),
                        W=[("x", k, t) for k in range(8)], dma=("xin", t))
                bcount = 0
                for l, (kind, j) in enumerate(cfg):
                    if kind != "N":
                        rmsnorm(("g_mix", l))
                    if kind == "N":
                        pass
                    elif kind == "A":
                        mixer_a(l)
                    elif kind == "B":
                        mixer_b(l, bcount)
                        bcount += 1
                    else:
                        mixer_c(l)
                    rmsnorm(("g_mlp", l))
                    mlp(l)
                rmsnorm(("g_fin",), out_x=True)
                for t in range(4):
                    tsl = slice(t * 512, (t + 1) * 512)
                    S.op("sp", lambda e, s=s, tsl=tsl: e.dma_start(
                        out=yT[s].rearrange("(k p) n -> p k n", p=128)[:, :, tsl], in_=X[:, :, tsl]),
                        R=[("x", k, t) for k in range(8)], W=[("y", s, t)], dma=("yout", t))
            S.final_wait("sp", [("y", s, t) for s in range(nseq) for t in range(4)])

        S.dry = True
        program()
        S.dry = False
        program()
        S.emit()
    return nc


DEFAULT_CFG = [("A", 0), ("B", 0), ("C", 0), ("A", 1)]


def run(inputs, cfg, ncore, nseq):
    wb = weight_blocks(inputs, cfg)
    woffs, wtot = layout(wb)
    wsarr = np.concatenate(list(wb.values()), axis=1).astype(np.float32)
    pb = param_blocks(inputs, cfg)
    poffs, ptot = layout(pb)
    pararr = np.concatenate(list(pb.values()), axis=1).astype(np.float32)
    parb = paramb_blocks(inputs, cfg)
    nc = build(nseq, cfg, woffs, wtot, poffs, ptot, parb.shape[1])
    x = np.asarray(inputs["x"], dtype=np.float32)
    in_maps = []
    for c in range(ncore):
        xs = np.ascontiguousarray(x[c * nseq:(c + 1) * nseq].transpose(0, 2, 1))
        in_maps.append({"xT": xs, "ws": wsarr, "par": pararr, "parb": parb})
    if os.environ.get("KTRACE") == "1":
        res = run_bass_kernel_spmd(nc, in_maps, core_ids=list(range(ncore)), trace=True)
        print("EXEC_NS", res.exec_time_ns)
    else:
        res = run_bass_kernel_spmd(nc, in_maps, core_ids=list(range(ncore)))
    outs = [np.asarray(r["yT"]).transpose(0, 2, 1) for r in res.results]
    return np.ascontiguousarray(np.concatenate(outs, axis=0)).astype(np.float32)


def kernel(**inputs):
    inputs = {k: np.asarray(v) for k, v in inputs.items()}
    return run(inputs, DEFAULT_CFG, NCORE, inputs["x"].shape[0] // NCORE)
```

```python
import contextlib
import numpy as np
import concourse.bass as bass
import concourse.mybir as mybir
from concourse.bass_utils import run_bass_kernel_spmd

F32 = mybir.dt.float32
BF16 = mybir.dt.bfloat16
AF = mybir.ActivationFunctionType
ALU = mybir.AluOpType

D = 1024
SEQ = 2048
NCORE = 8
SEM_LIMIT = 30000
INORDER_SAFE = ("pe", "sp")
NSLOT = 3
import os
KSTOP = int(os.environ.get('KSTOP', '99'))
KSTOPC = int(os.environ.get('KSTOPC', '99'))
KSUB = int(os.environ.get('KSUB', '99'))
TBF16 = os.environ.get('TBF16', '1') == '1'
ARENA_COLS = 20800


class Sched:
    ENGS = ("pe", "act", "dve", "pool", "sp")

    def __init__(self, nc, stack):
        self.nc = nc
        self.stack = stack
        self.dry = False
        self.E = {n: dict(ops=[], sem=None, cnt=0, seen={}, nsem=0, last=None) for n in self.ENGS}
        self.res = {}
        self.dsem = {}

    def _newsem(self, name):
        return self.stack.enter_context(self.nc.semaphore(name))

    def _tok(self, eng):
        E = self.E[eng]
        if E["sem"] is None or E["cnt"] >= SEM_LIMIT:
            E["sem"] = self._newsem("s_%s_%d" % (eng, E["nsem"]))
            E["nsem"] += 1
            E["cnt"] = 0
        E["cnt"] += 1
        t = (E["sem"], E["cnt"], eng, 1)
        E["last"] = t
        return t

    def _dtok(self, key):
        d = self.dsem.get(key)
        if d is None or d[1] >= SEM_LIMIT:
            n = 0 if d is None else d[2] + 1
            nm = "d_" + "".join(ch for ch in str(key) if ch.isalnum()) + "_%d" % n
            d = [self._newsem(nm), 0, n]
            self.dsem[key] = d
        d[1] += 16
        return (d[0], d[1], "dma", 16)

    def op(self, eng, fns, R=(), W=(), dma=None):
        if self.dry:
            return None
        if callable(fns):
            fns = [fns]
        psr = [r for r in R if isinstance(r, tuple) and r and r[0] == "ps"]
        if psr:
            R = [r for r in R if not (isinstance(r, tuple) and r and r[0] == "ps")]
            W = list(W) + psr
        E = self.E[eng]
        need = {}

        def add(tok):
            if tok is None:
                return
            sem, val, src, _ = tok
            if src == eng and eng in INORDER_SAFE:
                return
            k = id(sem)
            if k not in need or need[k][1] < val:
                need[k] = (sem, val)

        for r in R:
            st = self.res.get(r)
            if st:
                add(st[0])
        for w in W:
            st = self.res.get(w)
            if st:
                add(st[0])
                for t in st[1]:
                    add(t)
        waits = []
        for k, (sem, val) in need.items():
            if E["seen"].get(k, 0) >= val:
                continue
            E["seen"][k] = val
            waits.append((sem, val))
        tok = self._dtok(dma) if dma is not None else self._tok(eng)
        E["ops"].append((waits, fns, tok))
        for r in R:
            st = self.res.get(r)
            if st is None:
                self.res[r] = [None, [tok]]
            else:
                st[1].append(tok)
        for w in W:
            self.res[w] = [tok, []]
        return tok

    def barrier(self, engs=("pe", "act", "dve", "pool")):
        if self.dry:
            return
        lasts = [self.E[e]["last"] for e in engs if self.E[e]["last"] is not None]
        for e in engs:
            E = self.E[e]
            waits = []
            for (sem, val, src, _) in lasts:
                if src == e:
                    continue
                k = id(sem)
                if E["seen"].get(k, 0) >= val:
                    continue
                E["seen"][k] = val
                waits.append((sem, val))
            if waits:
                E["ops"].append((waits, [], None))
        E = self.E["sp"]
        waits = []
        for (sem, val, src, _) in lasts:
            k = id(sem)
            if E["seen"].get(k, 0) >= val:
                continue
            E["seen"][k] = val
            waits.append((sem, val))
        if waits:
            E["ops"].append((waits, [], None))

    def final_wait(self, eng, keys):
        if self.dry:
            return
        waits = []
        for k in keys:
            st = self.res.get(k)
            if not st:
                continue
            for t in [st[0]] + st[1]:
                if t is not None:
                    waits.append((t[0], t[1]))
        self.E[eng]["ops"].append((waits, [], None))

    def emit(self):
        nc = self.nc
        S = self

        def run(e, name):
            for waits, fns, tok in S.E[name]["ops"]:
                for sem, val in waits:
                    e.wait_ge(sem, val)
                inst = None
                for fn in fns:
                    inst = fn(e)
                if tok is not None and inst is not None:
                    inst.then_inc(tok[0], tok[3])

        with nc.Block() as block:
            @block.tensor
            def _(e):
                run(e, "pe")

            @block.scalar
            def _(e):
                run(e, "act")

            @block.vector
            def _(e):
                run(e, "dve")

            @block.gpsimd
            def _(e):
                run(e, "pool")

            @block.sync
            def _(e):
                run(e, "sp")


def _pk(w, k):
    n = w.shape[1]
    return np.ascontiguousarray(w.reshape(k, 128, n).transpose(1, 0, 2).reshape(128, k * n))


def weight_blocks(inp, cfg):
    B = {}
    for l, (kind, j) in enumerate(cfg):
        if kind == "A":
            w_in = inp["a_w_in"][j]
            gw = inp["a_gate_w"][j]
            for c in range(8):
                blk = np.concatenate([w_in[:, c * 128:(c + 1) * 128], w_in[:, 1024 + c * 128:1024 + (c + 1) * 128]], axis=1)
                B[("a_in", l, c)] = _pk(blk, 8)
                B[("a_gw", l, c)] = np.ascontiguousarray(gw[:, :, c].transpose(2, 0, 1, 3).reshape(128, 512))
            for half in range(2):
                B[("a_out", l, half)] = _pk(inp["a_w_out"][j][:, half * 512:(half + 1) * 512], 8)
        elif kind == "B":
            w_in = inp["b_w_in"][j]
            for blk in range(2):
                B[("b_in_u", l, blk)] = _pk(w_in[:, blk * 512:(blk + 1) * 512], 8)
            for blk in range(2):
                B[("b_in_v", l, blk)] = _pk(w_in[:, 1024 + blk * 512:1024 + (blk + 1) * 512], 8)
            B[("b_ws", l)] = np.ascontiguousarray(inp["b_w_s"][j].transpose(2, 0, 1).reshape(128, 1024))
            for half in range(2):
                B[("b_out", l, half)] = _pk(inp["b_w_out"][j][:, half * 512:(half + 1) * 512], 8)
        elif kind == "C":
            w_in = inp["c_w_in"][j]
            B[("c_ab", l)] = _pk(w_in[:, 4096:4128], 8)
            for hd in range(8):
                blk = np.concatenate([w_in[:, t * 1024 + hd * 128:t * 1024 + (hd + 1) * 128] for t in range(4)], axis=1)
                B[("c_in", l, hd)] = _pk(blk, 8)
                B[("c_out", l, hd)] = np.ascontiguousarray(inp["c_w_out"][j][hd * 128:(hd + 1) * 128, :])
        for fb in range(8):
            B[("mlp_up", l, fb)] = _pk(inp["mlp_w_up"][l][:, fb * 512:(fb + 1) * 512], 8)
            B[("mlp_dn", l, fb)] = _pk(inp["mlp_w_down"][l][fb * 512:(fb + 1) * 512, :], 4)
    return B


def _col(v):
    return np.ascontiguousarray(np.asarray(v).reshape(8, 128).T)


def param_blocks(inp, cfg):
    P = {}
    for l, (kind, j) in enumerate(cfg):
        P[("g_mix", l)] = _col(inp["norm_mix_g"][l])
        P[("g_mlp", l)] = _col(inp["norm_mlp_g"][l])
        if kind == "A":
            P[("a_cw", l)] = np.ascontiguousarray(inp["a_conv_w"][j].reshape(4, 8, 128).transpose(2, 1, 0).reshape(128, 32))
            P[("a_cb", l)] = _col(inp["a_conv_b"][j])
            P[("a_gb", l)] = np.ascontiguousarray(inp["a_gate_b"][j].transpose(3, 2, 0, 1).reshape(128, 32))
            P[("a_lam", l)] = np.ascontiguousarray(inp["a_lambda"][j].reshape(2, 8, 128).transpose(2, 1, 0).reshape(128, 16))
        elif kind == "C":
            P[("c_cw", l)] = np.ascontiguousarray(inp["c_conv_w"][j].reshape(4, 3, 8, 128).transpose(3, 2, 1, 0).reshape(128, 96))
            P[("c_alog", l)] = np.ascontiguousarray(np.broadcast_to(inp["c_a_log"][j].reshape(1, 16), (128, 16)))
            P[("c_dtb", l)] = np.ascontiguousarray(np.broadcast_to(inp["c_dt_bias"][j].reshape(1, 16), (128, 16)))
            P[("c_ng", l)] = np.ascontiguousarray(inp["c_norm_g"][j].reshape(128, 1))
    P[("g_fin",)] = _col(inp["norm_final_g"])
    return P


def paramb_blocks(inp, cfg):
    out = []
    for l, (kind, j) in enumerate(cfg):
        if kind == "B":
            row = np.concatenate([inp["b_ln_g"][j], inp["b_ln_b"][j], inp["b_b_s"][j].reshape(-1)])
            out.append(np.broadcast_to(row[None, :], (128, 3072)))
    if not out:
        out = [np.zeros((128, 3072), np.float32)]
    return np.ascontiguousarray(np.concatenate(out, axis=1)).astype(np.float32)


def layout(blocks):
    offs = {}
    o = 0
    for k, v in blocks.items():
        offs[k] = (o, v.shape[1])
        o += v.shape[1]
    return offs, o


def build(nseq, cfg, woffs, wtot, poffs, ptot, pbtot):
    nc = bass.Bass("TRN2", target_bir_lowering=False)
    xT = nc.dram_tensor("xT", [nseq, D, SEQ], F32, kind="ExternalInput").ap()
    ws = nc.dram_tensor("ws", [128, wtot], F32, kind="ExternalInput").ap()
    par = nc.dram_tensor("par", [128, ptot], F32, kind="ExternalInput").ap()
    parb = nc.dram_tensor("parb", [128, pbtot], F32, kind="ExternalInput").ap()
    yT = nc.dram_tensor("yT", [nseq, D, SEQ], F32, kind="ExternalOutput").ap()

    with contextlib.ExitStack() as st:
        S = Sched(nc, st)
        T = lambda name, shape, dt: st.enter_context(nc.sbuf_tensor(name, shape, dt))
        X = T("X", [128, 8, SEQ], F32)
        H = T("H", [128, 8, SEQ], BF16)
        RING = [T("ring%d" % i, [128, 4096], BF16) for i in range(NSLOT)]
        PAR = T("PAR", [128, ptot], F32)
        PC = T("PC", [128, 64], F32)
        ZC = T("ZC", [128, 2], F32)
        IDB = T("IDB", [128, 128], BF16)
        ONB = T("ONB", [128, 128], BF16)
        IDF = T("IDF", [128, 128], F32)
        MSK = T("MSK", [128, 4, 128], F32)
        LMK = T("LMK", [128, 7, 128], BF16)
        AR = T("AR", [128, ARENA_COLS], F32)
        PS = st.enter_context(nc.psum_tensor("PS", [128, 4096], F32))

        def psb(b, n=1):
            return PS[:, b * 512:(b + n) * 512]

        def pk(b, n=1):
            return [("ps", b + i) for i in range(n)]

        class Arena:
            def __init__(self):
                self.off = 0
                self.tag = 0

            def reset(self):
                self.off = 0
                self.tag += 1

            def f32(self, n):
                v = AR[:, self.off:self.off + n]
                self.off += n
                assert self.off <= ARENA_COLS, self.off
                return v

            def bf16(self, n):
                assert n % 2 == 0
                v = AR[:, self.off:self.off + n // 2].bitcast(BF16)
                self.off += n // 2
                assert self.off <= ARENA_COLS, self.off
                return v

        A = Arena()

        def act(out, in_, func, R, W, scale=1.0, bias=0.0):
            S.op("act", lambda e: e.activation(out=out, in_=in_, func=func, scale=scale, bias=bias), R, W)

        def tt(out, in0, in1, op, R, W, eng="dve"):
            S.op(eng, lambda e: e.tensor_tensor(out=out, in0=in0, in1=in1, op=op), R, W)

        def tsc(out, in0, s1, op0, R, W, s2=None, op1=None, eng="dve"):
            if op1 is None:
                S.op(eng, lambda e: e.tensor_scalar(out=out, in0=in0, scalar1=s1, scalar2=None, op0=op0), R, W)
            else:
                S.op(eng, lambda e: e.tensor_scalar(out=out, in0=in0, scalar1=s1, scalar2=s2, op0=op0, op1=op1), R, W)

        def stt(out, in0, scalar, in1, op0, op1, R, W):
            S.op("dve", lambda e: e.scalar_tensor_tensor(out=out, in0=in0, scalar=scalar, in1=in1, op0=op0, op1=op1), R, W)

        def mm(out, pairs, R, W):
            n = len(pairs)
            fns = []
            for i, (l, r) in enumerate(pairs):
                fns.append(lambda e, l=l, r=r, i=i: e.matmul(out, lhsT=l, rhs=r, start=(i == 0), stop=(i == n - 1)))
            S.op("pe", fns, R, W)

        def copy(out, in_, R, W, eng="dve"):
            S.op(eng, lambda e: e.tensor_copy(out=out, in_=in_), R, W)

        class WRing:
            def __init__(self):
                self.sched = []
                self.pos = 0
                self.issued = 0
                self.released = 0

            def reset(self):
                self.pos = 0
                self.issued = 0
                self.released = 0

            def pump(self):
                if S.dry:
                    return
                while self.issued < len(self.sched) and self.issued - NSLOT < self.released:
                    i = self.issued
                    off, n = woffs[self.sched[i]]
                    slot = i % NSLOT
                    dst = RING[slot][:, 0:n]
                    src = ws[:, off:off + n]
                    S.op("pool", lambda e, dst=dst, src=src: e.dma_start(out=dst, in_=src, max_dma_last_dim=8192),
                         W=[("ring", slot)], dma=("ring", slot))
                    self.issued += 1

            def get(self, name):
                if S.dry:
                    self.sched.append(name)
                    return RING[0], ("ring", 0)
                i = self.pos
                assert self.sched[i] == name, (self.sched[i], name)
                self.pos += 1
                self.pump()
                assert self.issued > i, "weight ring deadlock at %s" % (name,)
                return RING[i % NSLOT], ("ring", i % NSLOT)

            def done(self):
                if S.dry:
                    return
                self.released += 1
                self.pump()

        WR = WRing()

        def pcol(name, c, n=1):
            o, _ = poffs[name]
            return PAR[:, o + c:o + c + n]

        def setup():
            S.op("sp", lambda e: e.dma_start(out=PAR[:], in_=par[:, :]), W=["PAR"], dma="PAR")
            S.op("dve", lambda e: e.memset(ONB[:], 1.0), W=["ONB"])
            S.op("dve", lambda e: e.memset(ZC[:], 0.0), W=["ZC"])
            S.op("dve", lambda e: e.memset(IDF[:], 0.0), W=["IDF"])
            S.op("pool", lambda e: e.affine_select(out=IDF[:], in_=IDF[:], pattern=[[-1, 128]], base=0, channel_multiplier=1,
                                                   compare_op=ALU.not_equal, fill=1.0), R=["IDF"], W=["IDF"])
            copy(IDB[:], IDF[:], ["IDF"], ["IDB"])
            S.op("dve", lambda e: e.memset(MSK[:], 1.0), W=["MSK"])
            specs = [
                (0, 1, -1, 0, ALU.is_gt),
                (1, -1, 1, 0, ALU.is_gt),
                (2, -1, 1, 0, ALU.is_ge),
                (3, 1, -1, 0, ALU.is_ge),
            ]
            for (i, cm, stp, base, cmp_) in specs:
                S.op("pool", lambda e, i=i, cm=cm, stp=stp, base=base, cmp_=cmp_: e.affine_select(
                    out=MSK[:, i, :], in_=MSK[:, i, :], pattern=[[stp, 128]], base=base, channel_multiplier=cm,
                    compare_op=cmp_, fill=0.0), R=["MSK"], W=["MSK"])
            A.reset()
            Et = A.f32(128)
            BD = [A.f32(128), A.f32(128)]
            prev = IDF[:]
            prevk = "IDF"
            for li in range(7):
                s2 = 2 << li
                cur = BD[li % 2]
                curk = ("BD", li % 2)
                if s2 == 128:
                    S.op("dve", lambda e, cur=cur: e.memset(cur, 1.0), W=[curk])
                else:
                    nb = 128 // s2
                    S.op("dve", lambda e, nb=nb: e.memset(Et[0:nb, :], 1.0), W=["Et"])
                    S.op("pool", lambda e, nb=nb, s2=s2: e.affine_select(out=Et[0:nb, :], in_=Et[0:nb, :], pattern=[[1, 128]], base=0,
                                                                       channel_multiplier=-s2, compare_op=ALU.is_ge, fill=0.0),
                         R=["Et"], W=["Et"])
                    S.op("pool", lambda e, nb=nb, s2=s2: e.affine_select(out=Et[0:nb, :], in_=Et[0:nb, :], pattern=[[-1, 128]], base=s2 - 1,
                                                                       channel_multiplier=s2, compare_op=ALU.is_ge, fill=0.0),
                         R=["Et"], W=["Et"])
                    mm(PS[:, 0:128], [(Et[0:nb, :], Et[0:nb, :])], ["Et"], pk(0))
                    copy(cur, PS[:, 0:128], pk(0), [curk])
                tt(LMK[:, li, :], cur, prev, ALU.subtract, [curk, prevk], ["LMK"])
                prev, prevk = cur, curk
            S.barrier()

        def rmsnorm(gname, out_x=False):
            A.reset()
            SQ = [A.bf16(8 * 512).rearrange("p (k n) -> p k n", k=8) for _ in range(2)]
            RS = [A.f32(512) for _ in range(2)]
            for t in range(4):
                b = t % 2
                tsl = slice(t * 512, (t + 1) * 512)
                for k in range(8):
                    act(SQ[b][:, k, :], X[:, k, tsl], AF.Square, [("x", k, t)], [("sq", b, k)])
                bank = 2 * b
                mm(psb(bank), [(ONB[:], SQ[b][:, k, :]) for k in range(8)],
                   ["ONB"] + [("sq", b, k) for k in range(8)], pk(bank))
                act(RS[b], psb(bank), AF.Sqrt, pk(bank), [("rs", b)], scale=1.0 / D, bias=1e-6)
                S.op("dve", lambda e, b=b: e.reciprocal(out=RS[b], in_=RS[b]), [("rs", b)], [("rs", b)])
                for k in range(8):
                    if out_x:
                        stt(X[:, k, tsl], X[:, k, tsl], pcol(gname, k), RS[b], ALU.mult, ALU.mult,
                            [("x", k, t), ("rs", b), "PAR"], [("x", k, t)])
                    else:
                        stt(H[:, k, tsl], X[:, k, tsl], pcol(gname, k), RS[b], ALU.mult, ALU.mult,
                            [("x", k, t), ("rs", b), "PAR"], [("h", k, t)])
            S.barrier()

        def hkeys(t):
            return [("h", k, t) for k in range(8)]

        def outproj(name_fn, YT, ykeys):
            for half in range(2):
                slot, rk = WR.get(name_fn(half))
                wo = slot[:, 0:4096].rearrange("p (c n) -> p c n", c=8)
                for t in range(4):
                    tsl = slice(t * 512, (t + 1) * 512)
                    for mo in range(4):
                        bank = (t * 4 + mo) % 8
                        mm(psb(bank), [(wo[:, cc, mo * 128:(mo + 1) * 128], YT[:, cc, tsl]) for cc in range(8)],
                           [rk] + ykeys, pk(bank))
                        ko = half * 4 + mo
                        tt(X[:, ko, tsl], psb(bank), X[:, ko, tsl], ALU.add, pk(bank) + [("x", ko, t)], [("x", ko, t)])
                WR.done()

        def mlp(l):
            A.reset()
            H1 = [A.bf16(4 * 512).rearrange("p (c n) -> p c n", c=4) for _ in range(2)]
            SQ1 = [A.bf16(512) for _ in range(4)]
            steps = [(fb, t) for fb in range(8) for t in range(4)]
            held = {}

            def up(i):
                fb, t = steps[i]
                if t == 0:
                    held[("u", fb)] = WR.get(("mlp_up", l, fb))
                slot, rk = held[("u", fb)]
                wu = slot[:, 0:4096].rearrange("p (k n) -> p k n", k=8)
                tsl = slice(t * 512, (t + 1) * 512)
                hb = i % 2
                for mi in range(4):
                    bank = mi
                    mm(psb(bank), [(wu[:, k, mi * 128:(mi + 1) * 128], H[:, k, tsl]) for k in range(8)],
                       [rk] + hkeys(t), pk(bank))
                    act(SQ1[mi], psb(bank), AF.Square, pk(bank), [("sq1", mi)])
                    stt(H1[hb][:, mi, :], psb(bank), 0.0, SQ1[mi], ALU.is_gt, ALU.mult,
                        pk(bank) + [("sq1", mi)], [("h1", hb, mi)])
                if t == 3:
                    WR.done()

            def down(i):
                fb, t = steps[i]
                if t == 0:
                    held[("d", fb)] = WR.get(("mlp_dn", l, fb))
                slot, rk = held[("d", fb)]
                wd = slot[:, 0:4096].rearrange("p (c n) -> p c n", c=4)
                tsl = slice(t * 512, (t + 1) * 512)
                hb = i % 2
                for mo in range(8):
                    bank = 4 + (mo % 4)
                    mm(psb(bank), [(wd[:, c, mo * 128:(mo + 1) * 128], H1[hb][:, c, :]) for c in range(4)],
                       [rk] + [("h1", hb, c) for c in range(4)], pk(bank))
                    tt(X[:, mo, tsl], psb(bank), X[:, mo, tsl], ALU.add, pk(bank) + [("x", mo, t)], [("x", mo, t)])
                if t == 3:
                    WR.done()

            n = len(steps)
            for i in range(n + 1):
                if i < n:
                    up(i)
                if i >= 1:
                    down(i - 1)
            S.barrier()

        def mixer_a(l):
            A.reset()
            YT = A.bf16(8 * SEQ).rearrange("p (c n) -> p c n", c=8)
            XP = A.f32(SEQ + 4)
            XR = A.f32(SEQ)
            XB = A.bf16(SEQ)
            T1 = A.f32(SEQ)
            T2 = A.f32(SEQ)
            T3 = A.f32(SEQ)
            lo, _ = poffs[("a_lam", l)]
            act(PC[:, 32:48], PAR[:, lo:lo + 16], AF.Exp, ["PAR"], ["PC"], scale=-1.0)
            act(PC[:, 32:48], PC[:, 32:48], AF.Ln, ["PC"], ["PC"], bias=1.0)
            tsc(PC[:, 0:16], PC[:, 32:48], -8.0, ALU.mult, ["PC"], ["PC"])
            tsc(PC[:, 16:32], PC[:, 32:48], -16.0, ALU.mult, ["PC"], ["PC"])
            S.op("dve", lambda e: e.memset(XP[:, 0:2], 0.0), W=[("XP", 0)])
            S.op("dve", lambda e: e.memset(XP[:, SEQ + 2:SEQ + 4], 0.0), W=[("XP", 1)])
            HW_ = SEQ // 2

            def a_gen(c, hf, wv, rk, gw, rkg):
                hs = slice(hf * HW_, (hf + 1) * HW_)
                xs = slice(2 + hf * HW_, 2 + (hf + 1) * HW_)
                t0 = 2 * hf
                kXP, kXR, kXB, kT1, kT2, kT3 = [(nm, hf) for nm in ("XP", "XR", "XB", "T1", "T2", "T3")]
                XPK = [("XP", 0), ("XP", 1)]
                for t in (t0, t0 + 1):
                    tsl = slice(t * 512, (t + 1) * 512)
                    mm(psb(t), [(wv[:, k, 0:128], H[:, k, tsl]) for k in range(8)], [rk] + hkeys(t), pk(t))
                    mm(psb(4 + t), [(wv[:, k, 128:256], H[:, k, tsl]) for k in range(8)], [rk] + hkeys(t), pk(4 + t))
                yield
                act(YT[:, c, hs], psb(t0, 2), AF.Gelu_apprx_tanh, pk(t0, 2), [("yt", c)])
                act(XP[:, xs], psb(4 + t0, 2), AF.Copy, pk(4 + t0, 2), [kXP])
                yield
                tsc(XR[:, hs], XP[:, hf * HW_:hf * HW_ + HW_], pcol(("a_cw", l), c * 4 + 0), ALU.mult, XPK + ["PAR"], [kXR],
                    s2=pcol(("a_cb", l), c), op1=ALU.add)
                for tap in range(1, 4):
                    stt(XR[:, hs], XP[:, hf * HW_ + tap:hf * HW_ + tap + HW_], pcol(("a_cw", l), c * 4 + tap), XR[:, hs],
                        ALU.mult, ALU.add, XPK + [kXR, "PAR"], [kXR])
                yield
                act(XB[:, hs], XR[:, hs], AF.Copy, [kXR], [kXB])
                yield
                for dr in range(2):
                    for g in range(2):
                        for t in (t0, t0 + 1):
                            tsl = slice(t * 512, (t + 1) * 512)
                            mm(psb(g * 4 + t), [(gw[:, dr * 2 + g, :], XB[:, tsl])], [rkg, kXB], pk(g * 4 + t))
                    yield
                    gbo = c * 4 + dr * 2
                    act(T1[:, hs], psb(t0, 2), AF.Sigmoid, pk(t0, 2) + ["PAR"], [kT1], bias=pcol(("a_gb", l), gbo))
                    act(T2[:, hs], psb(4 + t0, 2), AF.Sigmoid, pk(4 + t0, 2) + ["PAR"], [kT2], bias=pcol(("a_gb", l), gbo + 1))
                    ci = c * 2 + dr
                    act(T3[:, hs], T1[:, hs], AF.Exp, [kT1, "PC"], [kT3], scale=PC[:, ci:ci + 1])
                    act(T1[:, hs], T1[:, hs], AF.Exp, [kT1, "PC"], [kT1], scale=PC[:, 16 + ci:16 + ci + 1])
                    act(T1[:, hs], T1[:, hs], AF.Sqrt, [kT1], [kT1], scale=-1.0, bias=1.0)
                    yield
                    tt(T2[:, hs], T2[:, hs], T1[:, hs], ALU.mult, [kT1, kT2], [kT2], eng="pool")
                    tt(T2[:, hs], T2[:, hs], XR[:, hs], ALU.mult, [kT2, kXR], [kT2], eng="pool")
                    yield
                    if dr == 0:
                        if hf == 0:
                            S.op("dve", lambda e: e.tensor_tensor_scan(out=XP[:, xs], data0=T3[:, hs], data1=T2[:, hs], initial=0.0,
                                                                       op0=ALU.mult, op1=ALU.add), [kT3, kT2], [kXP])
                        else:
                            S.op("dve", lambda e: e.tensor_tensor_scan(out=XP[:, xs], data0=T3[:, hs], data1=T2[:, hs],
                                                                       initial=XP[:, 2 + HW_ - 1:2 + HW_],
                                                                       op0=ALU.mult, op1=ALU.add), [kT3, kT2, ("XP", 0)], [kXP])
                    else:
                        if hf == 1:
                            S.op("dve", lambda e: e.tensor_tensor_scan(out=T1[:, hs][:, ::-1], data0=T3[:, hs][:, ::-1],
                                                                       data1=T2[:, hs][:, ::-1], initial=0.0,
                                                                       op0=ALU.mult, op1=ALU.add), [kT3, kT2], [kT1])
                        else:
                            yield
                            S.op("dve", lambda e: e.tensor_tensor_scan(out=T1[:, hs][:, ::-1], data0=T3[:, hs][:, ::-1],
                                                                       data1=T2[:, hs][:, ::-1], initial=T1[:, HW_:HW_ + 1],
                                                                       op0=ALU.mult, op1=ALU.add), [kT3, kT2, ("T1", 1)], [kT1])
                    yield
                tt(T1[:, hs], T1[:, hs], XP[:, xs], ALU.add, [kT1, kXP], [kT1])
                tt(YT[:, c, hs], T1[:, hs], YT[:, c, hs], ALU.mult, [kT1, ("yt", c)], [("yt", c)])

            for c in range(8):
                slot, rk = WR.get(("a_in", l, c))
                wv = slot[:, 0:2048].rearrange("p (k n) -> p k n", k=8)
                slotg, rkg = WR.get(("a_gw", l, c))
                gw = slotg[:, 0:512].rearrange("p (q o) -> p q o", q=4)
                gens = [a_gen(c, 0, wv, rk, gw, rkg), a_gen(c, 1, wv, rk, gw, rkg)]
                while gens:
                    for g_ in list(gens):
                        try:
                            next(g_)
                        except StopIteration:
                            gens.remove(g_)
                WR.done()
                WR.done()
            outproj(lambda half: ("a_out", l, half), YT, [("yt", c) for c in range(8)])
            S.barrier()

        def mixer_b(l, bidx):
            A.reset()
            UT = A.bf16(8 * SEQ).rearrange("p (c n) -> p c n", c=8)
            LGB = A.f32(3072)
            VT = [A.f32(1024) for _ in range(2)]
            VN = [A.bf16(1024) for _ in range(2)]
            TB = [A.f32(1024) for _ in range(2)]
            STT = [A.f32(12) for _ in range(2)]
            MV = [A.f32(4) for _ in range(2)]
            S.op("sp", lambda e: e.dma_start(out=LGB, in_=parb[:, bidx * 3072:(bidx + 1) * 3072]), W=["LGB"], dma="LGB")
            for blk in range(2):
                slot, rk = WR.get(("b_in_u", l, blk))
                wb = slot[:, 0:4096].rearrange("p (k n) -> p k n", k=8)
                for t in range(4):
                    tsl = slice(t * 512, (t + 1) * 512)
                    b0 = (t % 2) * 4
                    for mi in range(4):
                        mm(psb(b0 + mi), [(wb[:, k, mi * 128:(mi + 1) * 128], H[:, k, tsl]) for k in range(8)],
                           [rk] + hkeys(t), pk(b0 + mi))
                    act(UT[:, blk * 4:(blk + 1) * 4, tsl], PS[:, b0 * 512:(b0 + 4) * 512].rearrange("p (c n) -> p c n", c=4),
                        AF.Gelu_apprx_tanh, pk(b0, 4), [("ut", blk * 4 + mi, t) for mi in range(4)])
                WR.done()
            slot0, rk0 = WR.get(("b_in_v", l, 0))
            slot1, rk1 = WR.get(("b_in_v", l, 1))
            slot2, rk2 = WR.get(("b_ws", l))
            wvv = [slot0[:, 0:4096].rearrange("p (k n) -> p k n", k=8), slot1[:, 0:4096].rearrange("p (k n) -> p k n", k=8)]
            rkv = [rk0, rk1]
            wst = slot2[:, 0:1024].rearrange("p (g n) -> p g n", g=8)
            for tt_ in range(16):
                b = tt_ % 2
                tok = slice(tt_ * 128, (tt_ + 1) * 128)
                t = tt_ // 4
                pb = b * 4
                for blk in range(2):
                    mm(psb(pb + blk), [(H[:, k, tok], wvv[blk][:, k, :]) for k in range(8)], [rkv[blk]] + hkeys(t), pk(pb + blk))
                act(VT[b], psb(pb, 2), AF.Gelu_apprx_tanh, pk(pb, 2), [("vt", b)])
                S.op("dve", lambda e, b=b: e.bn_stats(out=STT[b][:, 0:6], in_=VT[b][:, 0:512]), [("vt", b)], [("st", b)])
                S.op("dve", lambda e, b=b: e.bn_stats(out=STT[b][:, 6:12], in_=VT[b][:, 512:1024]), [("vt", b)], [("st2", b)])
                S.op("dve", lambda e, b=b: e.bn_aggr(out=MV[b][:, 0:2], in_=STT[b][:, 0:12]), [("st", b), ("st2", b)], [("mv", b)])
                act(MV[b][:, 2:3], MV[b][:, 1:2], AF.Sqrt, [("mv", b)], [("mv2", b)], bias=1e-5)
                S.op("dve", lambda e, b=b: e.reciprocal(out=MV[b][:, 2:3], in_=MV[b][:, 2:3]), [("mv2", b)], [("mv2", b)])
                tsc(VT[b], VT[b], MV[b][:, 0:1], ALU.subtract, [("vt", b), ("mv", b), ("mv2", b)], [("vt", b)],
                    s2=MV[b][:, 2:3], op1=ALU.mult)
                tt(VT[b], VT[b], LGB[:, 0:1024], ALU.mult, [("vt", b), "LGB"], [("vt", b)], eng="pool")
                tt(VN[b], VT[b], LGB[:, 1024:2048], ALU.add, [("vt", b), "LGB"], [("vn", b)], eng="pool")
                for g in range(8):
                    bank = pb + 2 + g // 4
                    o = PS[:, bank * 512 + (g % 4) * 128: bank * 512 + (g % 4 + 1) * 128]
                    mm(o, [(VN[b][:, g * 128:(g + 1) * 128], wst[:, g, :])], [("vn", b), rk2], [("ps", bank)])
                tt(TB[b], psb(pb + 2, 2), LGB[:, 2048:3072], ALU.add, pk(pb + 2, 2) + ["LGB"], [("tb", b)])
                uk = [("ut", g, t) for g in range(8)]
                tt(UT[:, :, tok], TB[b][:].rearrange("p (g n) -> p g n", g=8), UT[:, :, tok], ALU.mult,
                   [("tb", b)] + uk, uk, eng="pool")
            WR.done()
            WR.done()
            WR.done()
            outproj(lambda half: ("b_out", l, half), UT, [("ut", g, t) for g in range(8) for t in range(4)])
            S.barrier()

        def mixer_c(l):
            A.reset()
            QSC = 128.0 ** -0.5
            S1K = [("S1", 0), ("S1", 1)]
            S2K = [("S2", 0), ("S2", 1)]
            S3K = [("gcb", c) for c in range(16)]
            BTOK = A.f32(512)
            BT3 = BTOK.rearrange("p (c n) -> p c n", c=16)
            WAB = A.bf16(256).rearrange("p (k n) -> p k n", k=8)
            SM = A.f32(128)
            GCT, EGT, NEGT, KDT, NBT, EGLT = [SM[:, i * 16:(i + 1) * 16] for i in range(6)]
            QT = A.bf16(SEQ)
            KT = A.bf16(SEQ)
            KTOK = A.bf16(SEQ).rearrange("p (c n) -> p c n", c=16)
            VTOK = A.bf16(SEQ).rearrange("p (c n) -> p c n", c=16)
            OT = A.f32(SEQ)
            S1 = A.f32(SEQ + 4)
            S2 = A.f32(SEQ)
            S3 = A.f32(SEQ)
            QG = A.bf16(SEQ)
            YS = A.bf16(SEQ).rearrange("p (c n) -> p c n", c=16)
            QKD = A.bf16(SEQ).rearrange("p (c n) -> p c n", c=16)
            NBC = 4
            An = A.f32(NBC * 128).rearrange("p (c n) -> p c n", c=NBC)
            At = A.f32(NBC * 128).rearrange("p (c n) -> p c n", c=NBC)
            Tt = A.f32(NBC * 128).rearrange("p (c n) -> p c n", c=NBC)
            Yy = A.f32(NBC * 128).rearrange("p (c n) -> p c n", c=NBC)
            D1 = A.f32(NBC * 128)
            D2 = A.f32(NBC * 128)
            M1s = An
            WREP = D1.bitcast(BF16).rearrange("p (k n) -> p k n", k=8)
            RH = [A.bf16(128) for _ in range(2)]
            VN = [A.bf16(128) for _ in range(2)]
            VN2 = [A.bf16(128) for _ in range(2)]
            S32 = A.f32(128)
            SB = A.bf16(128)
            RAW, ACC, SIL = S1, S2, S3
            D13 = D1.rearrange("p (c n) -> p c n", c=NBC)
            D23 = D2.rearrange("p (c n) -> p c n", c=NBC)

            def bank3(b):
                return PS[:, b * 512:b * 512 + NBC * 128].rearrange("p (c n) -> p c n", c=NBC)

            ao, _ = poffs[("c_alog", l)]
            act(PC[:, 48:64], PAR[:, ao:ao + 16], AF.Exp, ["PAR"], ["PCc"])
            tsc(PC[:, 48:64], PC[:, 48:64], -1.0, ALU.mult, ["PCc"], ["PCc"])
            S.op("dve", lambda e: e.memset(S1[:, 0:2], 0.0), W=[*S1K])
            S.op("dve", lambda e: e.memset(S1[:, SEQ + 2:SEQ + 4], 0.0), W=[*S1K])
            slot, rk = WR.get(("c_ab", l))
            wab = slot[:, 0:256].rearrange("p (k n) -> p k n", k=8)
            copy(WAB, wab, [rk], ["WAB"])
            for c in range(16):
                tok = slice(c * 128, (c + 1) * 128)
                mm(PS[:, c * 32:(c + 1) * 32], [(H[:, k, tok], wab[:, k, :]) for k in range(8)], [rk] + hkeys(c // 4), pk(0))
            WR.done()
            act(BTOK, psb(0), AF.Sigmoid, pk(0), ["BTOK"])

            for hd in range(8):
                slot, rk = WR.get(("c_in", l, hd))
                win = slot[:, 0:4096].rearrange("p (k n) -> p k n", k=8)
                cwo, _ = poffs[("c_cw", l)]
                SQ = YS.rearrange("p c n -> p (c n)")
                VTf = QG

                def proj_gen(typ, hf):
                    pb = (typ % 2) * 4
                    t0 = 2 * hf
                    HW_ = SEQ // 2
                    hs = slice(hf * HW_, (hf + 1) * HW_)
                    ys_k = [("ys", b_) for b_ in range(hf * (8 // NBC), (hf + 1) * (8 // NBC))]
                    ot_k = [("ot", c) for c in range(8 * hf, 8 * hf + 8)]
                    s3_k = [("gcb", c) for c in range(8 * hf, 8 * hf + 8)]
                    for t in (t0, t0 + 1):
                        tsl = slice(t * 512, (t + 1) * 512)
                        mm(psb(pb + t), [(win[:, k, typ * 128:(typ + 1) * 128], H[:, k, tsl]) for k in range(8)],
                           [rk] + hkeys(t), pk(pb + t))
                    yield
                    act(RAW[:, 2 + hf * HW_:2 + (hf + 1) * HW_], psb(pb + t0, 2), AF.Copy, pk(pb + t0, 2), [("S1", hf)])
                    yield
                    co = cwo + hd * 12 + typ * 4
                    tsc(ACC[:, hs], RAW[:, hf * HW_:hf * HW_ + HW_], PAR[:, co:co + 1], ALU.mult, S1K + ["PAR"], [("S2", hf)])
                    for tap in range(1, 4):
                        stt(ACC[:, hs], RAW[:, hf * HW_ + tap:hf * HW_ + tap + HW_], PAR[:, co + tap:co + tap + 1], ACC[:, hs],
                            ALU.mult, ALU.add, S1K + [("S2", hf), "PAR"], [("S2", hf)])
                    yield
                    p3d = PS[:, hf * HW_:(hf + 1) * HW_].rearrange("p (c n) -> p c n", c=8)
                    if typ == 2:
                        act(VTf[:, hs], ACC[:, hs], AF.Silu, [("S2", hf)], ["QG"])
                        yield
                        for c in range(8 * hf, 8 * hf + 8):
                            mm(PS[:, c * 128:(c + 1) * 128], [(VTf[:, c * 128:(c + 1) * 128], IDB[:])], ["QG", "IDB"], pk(c // 4))
                        yield
                        copy(VTOK[:, 8 * hf:8 * hf + 8, :], p3d, pk(2 * hf, 2), ["VTOK"])
                    else:
                        act(SIL[:, hs], ACC[:, hs], AF.Silu, [("S2", hf)], s3_k)
                        yield
                        act(SQ[:, hs], SIL[:, hs], AF.Square, s3_k, ys_k)
                        yield
                        pb2 = 4 - pb
                        for t in (t0, t0 + 1):
                            tsl = slice(t * 512, (t + 1) * 512)
                            mm(psb(pb2 + t), [(ONB[:], SQ[:, tsl])], ["ONB"] + ys_k, pk(pb2 + t))
                        yield
                        act(OT[:, hs], psb(pb2 + t0, 2), AF.Sqrt, pk(pb2 + t0, 2), ot_k, bias=1e-6)
                        yield
                        S.op("dve", lambda e: e.reciprocal(out=OT[:, hs], in_=OT[:, hs]), ot_k, ot_k)
                        yield
                        if typ == 0:
                            stt(QT[:, hs], SIL[:, hs], QSC, OT[:, hs], ALU.mult, ALU.mult, s3_k + ot_k, ["QT"])
                        else:
                            tt(KT[:, hs], SIL[:, hs], OT[:, hs], ALU.mult, s3_k + ot_k, ["KT"])
                            yield
                            for c in range(8 * hf, 8 * hf + 8):
                                mm(PS[:, c * 128:(c + 1) * 128], [(KT[:, c * 128:(c + 1) * 128], IDB[:])], ["KT", "IDB"], pk(c // 4))
                            yield
                            act(KTOK[:, 8 * hf:8 * hf + 8, :], p3d, AF.Copy, pk(2 * hf, 2), ["KTOK"])

                for typ in range(3):
                    gens = [proj_gen(typ, 0), proj_gen(typ, 1)]
                    while gens:
                        for g in list(gens):
                            try:
                                next(g)
                            except StopIteration:
                                gens.remove(g)

                gk = [("gcb", c) for c in range(16)]
                for dr in range(2):
                    if KSTOPC <= 1:
                        continue
                    n = dr * 8 + hd
                    bcol = BT3[:, :, 16 + n]
                    copy(WREP, WAB[:, :, n:n + 1].to_broadcast([128, 8, 128]), ["WAB"], ["D1"])
                    GCB, EG = S3, S2[:, 0:SEQ // 2].bitcast(BF16)
                    gk = [("gcb", c) for c in range(16)]
                    lastc = 127 if dr == 0 else 0
                    GL = GCB[:, lastc::128]
                    EGL = EGLT

                    def dir_gen(hf):
                        HW_ = SEQ // 2
                        hs = slice(hf * HW_, (hf + 1) * HW_)
                        cr = range(8 * hf, 8 * hf + 8)
                        cols = slice(8 * hf, 8 * hf + 8)
                        t0 = 2 * hf
                        g1 = S1[:, 2 + hf * HW_:2 + (hf + 1) * HW_]
                        kS1 = ("S1", hf)
                        gkh = [("gcb", c) for c in cr]
                        for t in (t0, t0 + 1):
                            tsl = slice(t * 512, (t + 1) * 512)
                            mm(psb(t), [(WREP[:, k, :], H[:, k, tsl]) for k in range(8)], ["D1"] + hkeys(t), pk(t))
                        yield
                        act(g1, psb(t0, 2), AF.Exp, pk(t0, 2) + ["PAR"], [kS1], bias=pcol(("c_dtb", l), n))
                        act(g1, g1, AF.Ln, [kS1], [kS1], bias=1.0)
                        yield
                        tsc(g1, g1, PC[:, 48 + n:48 + n + 1], ALU.mult, [kS1, "PCc"], [kS1])
                        for c in cr:
                            sl = slice(c * 128, (c + 1) * 128)
                            gl = S1[:, 2 + c * 128:2 + (c + 1) * 128]
                            if dr == 0:
                                S.op("dve", lambda e, sl=sl, gl=gl: e.tensor_tensor_scan(out=GCB[:, sl], data0=ONB[:], data1=gl, initial=0.0,
                                                                                        op0=ALU.mult, op1=ALU.add), [kS1, "ONB"], [("gcb", c)])
                            else:
                                S.op("dve", lambda e, sl=sl, gl=gl: e.tensor_tensor_scan(out=GCB[:, sl][:, ::-1], data0=ONB[:], data1=gl[:, ::-1],
                                                                                        initial=0.0, op0=ALU.mult, op1=ALU.add),
                                     [kS1, "ONB"], [("gcb", c)])
                        yield
                        act(EG[:, hs], GCB[:, hs], AF.Exp, gkh, [("S2", 0)])
                        TMPh = g1.rearrange("p (c n) -> p c n", c=8)
                        tt(TMPh, GCB[:, hs].rearrange("p (c n) -> p c n", c=8), IDF[:].unsqueeze(1).to_broadcast([128, 8, 128]), ALU.mult,
                           gkh + ["IDF"], [kS1])
                        S.op("dve", lambda e: e.tensor_reduce(out=GCT[:, cols], in_=TMPh, op=ALU.add, axis=mybir.AxisListType.X),
                             [kS1], [("GCT", hf)])
                        yield
                        act(EGT[:, cols], GCT[:, cols], AF.Exp, [("GCT", hf)], [("EGT", hf)])
                        act(EGLT[:, cols], GL[:, cols], AF.Exp, gkh, [("EGLT", hf)])
                        tt(QG[:, hs], QT[:, hs], EG[:, hs], ALU.mult, ["QT", ("S2", 0)], ["QG"])
                        yield
                        tsc(NEGT[:, cols], EGT[:, cols], -1.0, ALU.mult, [("EGT", hf)], [("NEGT", hf)])
                        tt(KDT[:, cols], GL[:, cols], GCT[:, cols], ALU.subtract, gkh + [("GCT", hf)], [("KDT", hf)])
                        yield
                        act(KDT[:, cols], KDT[:, cols], AF.Exp, [("KDT", hf)], [("KDT", hf)])

                    gens = [dir_gen(0), dir_gen(1)]
                    while gens:
                        for g in list(gens):
                            try:
                                next(g)
                            except StopIteration:
                                gens.remove(g)
                    SET1K = ["An1", "At1", "T1", "Y1", "D1x1", "D2x1"]
                    S.op("dve", lambda e: e.memset(ZC[:, 1:2], 0.0), W=S1K + S2K + SET1K + ["ZCf"])
                    idb3 = IDF[:].unsqueeze(1).to_broadcast([128, NBC, 128])

                    def lm3(li):
                        return LMK[:, li, :].unsqueeze(1).to_broadcast([128, NBC, 128])

                    def v3(ap):
                        return ap.rearrange("p (c n) -> p c n", c=NBC)

                    def batch_gen(bq, sid):
                        W_ = NBC * 128
                        if sid == 0:
                            base = [An.rearrange("p c n -> p (c n)"), At.rearrange("p c n -> p (c n)"), Tt.rearrange("p c n -> p (c n)"),
                                    Yy.rearrange("p c n -> p (c n)"), D1, D2]
                            b0 = 4
                        else:
                            base = [S1[:, 2 + i * W_:2 + (i + 1) * W_] for i in range(4)] + \
                                   [S2[:, SEQ // 2:SEQ // 2 + W_], S2[:, SEQ // 2 + W_:SEQ // 2 + 2 * W_]]
                            b0 = 0
                        An_, At_, Tt_, Yy_ = [v3(b) for b in base[0:4]]
                        D1_, D2_ = base[4], base[5]
                        if TBF16:
                            lowv = [v3(b.bitcast(BF16)[:, 0:W_]) for b in base]
                        else:
                            lowv = [v3(b) for b in base]
                        M1L, _, TtL, YyL, D1L, D2L = lowv
                        kA, kAt, kT_, kY, kD1, kD2 = ["%s%d" % (nm, sid) for nm in ("An", "At", "T", "Y", "D1x", "D2x")]
                        if sid == 0:
                            kD1, kD2 = "D1", "D2"
                        D13_, D23_ = v3(D1_), v3(D2_)
                        M1_ = An_

                        def pq(bi, ci):
                            return PS[:, (b0 + bi) * 512 + ci * 128:(b0 + bi) * 512 + (ci + 1) * 128]

                        def pq3(bi):
                            return v3(PS[:, (b0 + bi) * 512:(b0 + bi) * 512 + W_])
                        cs = [bq * NBC + ci for ci in range(NBC)]
                        for ci, c in enumerate(cs):
                            sl = slice(c * 128, (c + 1) * 128)
                            mm(pq(0, ci), [(KT[:, sl], KT[:, sl])], ["KT"], pk(b0))
                            mm(pq(1, ci), [(KT[:, sl], QT[:, sl])], ["KT", "QT"], pk(b0 + 1))
                        for ci, c in enumerate(cs):
                            sl = slice(c * 128, (c + 1) * 128)
                            tsc(D13_[:, ci, :], GCB[:, sl], GCT[:, c:c + 1], ALU.subtract, [("gcb", c), ("GCT", c // 8), "ZC"], [kD1], s2=ZC[:, 0:1], op1=ALU.max)
                            tsc(D23_[:, ci, :], GCB[:, sl], GCT[:, c:c + 1], ALU.subtract, [("gcb", c), ("GCT", c // 8), "ZC"], [kD2], s2=ZC[:, 0:1], op1=ALU.min)
                        yield
                        act(D1_, D1_, AF.Exp, [kD1], [kD1], scale=-1.0)
                        act(D2_, D2_, AF.Exp, [kD2], [kD2])
                        yield
                        tt(D13_, D13_, MSK[:, dr, :].unsqueeze(1).to_broadcast([128, NBC, 128]), ALU.mult, [kD1, "MSK"], [kD1])
                        tt(D23_, D23_, MSK[:, 2 + dr, :].unsqueeze(1).to_broadcast([128, NBC, 128]), ALU.mult, [kD2, "MSK"], [kD2])
                        for ci, c in enumerate(cs):
                            stt(An_[:, ci, :], pq(0, ci), bcol[:, c:c + 1], D13_[:, ci, :], ALU.mult, ALU.mult, pk(b0) + ["BTOK", kD1], [kA])
                        tt(QKD[:, bq * NBC:(bq + 1) * NBC, :], pq3(1), D23_, ALU.mult, pk(b0 + 1) + [kD2], [("qkd", bq)])
                        yield
                        for ci in range(NBC):
                            mm(pq(2, ci), [(An_[:, ci, :], IDF[:])], [kA, "IDF"], pk(b0 + 2))
                        yield
                        act(At_, pq3(2), AF.Copy, pk(b0 + 2), [kAt])
                        tt(D13_, An_, lm3(0), ALU.mult, [kA, "LMK"], [kD1], eng="pool")
                        yield
                        tt(D23_, At_, lm3(0), ALU.mult, [kAt, "LMK"], [kD2], eng="pool")
                        tt(TtL, idb3, D13_, ALU.subtract, ["IDF", kD1], [kT_])
                        yield
                        tt(YyL, idb3, D23_, ALU.subtract, ["IDF", kD2], [kY])
                        for li in range(1, 7):
                            last = li == 6
                            dbuf, dkey = (D1L, kD1) if li % 2 else (D2L, kD2)
                            tt(dbuf, At_, lm3(li), ALU.mult, [kAt, "LMK"], [dkey], eng="pool")
                            yield
                            for ci in range(NBC):
                                mm(pq(0, ci), [(dbuf[:, ci, :], TtL[:, ci, :])], [dkey, kT_], pk(b0))
                            yield
                            act(M1L, pq3(0), AF.Copy, pk(b0), [kA])
                            yield
                            if not last:
                                for ci in range(NBC):
                                    mm(pq(1, ci), [(YyL[:, ci, :], M1L[:, ci, :])], [kY, kA], pk(b0 + 1))
                            for ci in range(NBC):
                                mm(pq(2, ci), [(M1L[:, ci, :], YyL[:, ci, :])], [kY, kA], pk(b0 + 2))
                            yield
                            if not last:
                                tt(TtL, TtL, pq3(1), ALU.subtract, [kT_] + pk(b0 + 1), [kT_])
                            tt(YyL, YyL, pq3(2), ALU.subtract, [kY] + pk(b0 + 2), [kY])
                        yield
                        for ci, c in enumerate(cs):
                            tsc(YS[:, c, :], YyL[:, ci, :], bcol[:, c:c + 1], ALU.mult, [kY, "BTOK"], [("ys", bq)])

                    order = list(range(16)) if dr == 0 else list(range(15, -1, -1))

                    def rec_gen(si0, chunks):
                        for k_, c in enumerate(chunks):
                            si = si0 + k_
                            par_ = si % 2
                            sl = slice(c * 128, (c + 1) * 128)
                            bq = c // NBC
                            o_ = par_ * 256
                            p1 = PS[:, 3 * 512 + o_:3 * 512 + o_ + 128]
                            p3 = PS[:, 3 * 512 + o_ + 128:3 * 512 + o_ + 256]
                            p2 = PS[:, 7 * 512 + o_:7 * 512 + o_ + 128]
                            p4 = PS[:, 7 * 512 + o_ + 128:7 * 512 + o_ + 256]
                            mm(p1, [(KT[:, sl], SB)], ["KT", "SB"], pk(3))
                            yield
                            stt(RH[par_], p1, NEGT[:, c:c + 1], VTOK[:, c, :], ALU.mult, ALU.add, pk(3) + [("NEGT", c // 8), "VTOK"], [("rh", par_)])
                            yield
                            mm(p2, [(YS[:, c, :], RH[par_])], [("ys", bq), ("rh", par_)], pk(7))
                            yield
                            act(VN[par_], p2, AF.Copy, pk(7), [("vn", par_)])
                            tsc(VN2[par_], p2, KDT[:, c:c + 1], ALU.mult, pk(7) + [("KDT", c // 8)], [("vn2", par_)])
                            yield
                            mm(p3, [(SB, QG[:, sl]), (VN[par_], QKD[:, c, :])], ["SB", "QG", ("vn", par_), ("qkd", bq)], pk(3))
                            mm(p4, [(KTOK[:, c, :], VN2[par_])], ["KTOK", ("vn2", par_)], pk(7))
                            yield
                            stt(SB, S32, EGL[:, c:c + 1], p4, ALU.mult, ALU.add, ["S32", ("EGLT", c // 8)] + pk(7), ["SB"])
                            stt(S32, S32, EGL[:, c:c + 1], p4, ALU.mult, ALU.add, ["S32", ("EGLT", c // 8)] + pk(7), ["S32"])
                            if dr == 0:
                                act(OT[:, sl], p3, AF.Copy, pk(3), [("ot", c)])
                            else:
                                tt(OT[:, sl], p3, OT[:, sl], ALU.add, pk(3) + [("ot", c)], [("ot", c)])
                            yield

                    def drive(gens):
                        while gens:
                            for g in list(gens):
                                try:
                                    next(g)
                                except StopIteration:
                                    gens.remove(g)

                    npair = 16 // NBC // 2
                    pairs = list(range(npair)) if dr == 0 else list(range(npair - 1, -1, -1))
                    nper = 16 // npair
                    if KSTOPC > 2:
                        drive([batch_gen(2 * pairs[0], 0), batch_gen(2 * pairs[0] + 1, 1)])
                    S.op("dve", lambda e: e.memset(S32, 0.0), W=["S32"])
                    S.op("dve", lambda e: e.memset(SB, 0.0), W=["SB"])
                    for pi in range(1, npair + 1):
                        gens = []
                        if pi < npair and KSTOPC > 2:
                            gens += [batch_gen(2 * pairs[pi], 0), batch_gen(2 * pairs[pi] + 1, 1)]
                        if KSTOPC > 5:
                            gens.append(rec_gen((pi - 1) * nper, order[(pi - 1) * nper:pi * nper]))
                        drive(gens)
                    S.op("dve", lambda e: e.memset(ZC[:, 1:2], 0.0), W=S1K + S2K + SET1K + ["ZCf"])
                ok_ = [("ot", c) for c in range(16)]
                SQ = YS.rearrange("p c n -> p (c n)")
                act(SQ, OT, AF.Square, ok_, [("ys", b_) for b_ in range(16 // NBC)])
                for t in range(4):
                    tsl = slice(t * 512, (t + 1) * 512)
                    mm(psb(t), [(ONB[:], SQ[:, tsl])], ["ONB"] + [("ys", b_) for b_ in range(16 // NBC)], pk(t))
                act(S3, psb(0, 4), AF.Sqrt, pk(0, 4), gk + [*S3K], scale=1.0 / 128, bias=1e-6)
                S.op("dve", lambda e: e.reciprocal(out=S3, in_=S3), gk + [*S3K], gk + [*S3K])
                for t in range(4):
                    tsl = slice(t * 512, (t + 1) * 512)
                    mm(psb(4 + t), [(win[:, k, 384:512], H[:, k, tsl]) for k in range(8)], [rk] + hkeys(t), pk(4 + t))
                WR.done()
                act(S2, psb(4, 4), AF.Silu, pk(4, 4), [*S2K])
                tt(S3, S3, OT, ALU.mult, gk + [*S3K] + ok_, gk + [*S3K])
                stt(QG, S3, pcol(("c_ng", l), 0), S2, ALU.mult, ALU.mult, gk + [*S3K, *S2K, "PAR"], ["QG"])
                slot, rk = WR.get(("c_out", l, hd))
                wo = slot[:, 0:1024]
                for t in range(4):
                    tsl = slice(t * 512, (t + 1) * 512)
                    for mo in range(8):
                        bank = mo
                        mm(psb(bank), [(wo[:, mo * 128:(mo + 1) * 128], QG[:, tsl])], [rk, "QG"], pk(bank))
                        tt(X[:, mo, tsl], psb(bank), X[:, mo, tsl], ALU.add, pk(bank) + [("x", mo, t)], [("x", mo, t)])
                WR.done()
            S.barrier()

        def program():
            WR.reset()
            setup()
            for s in range(nseq):
                for t in range(4):
                    tsl = slice(t * 512, (t + 1) * 512)
                    S.op("sp", lambda e, s=s, tsl=tsl: e.dma_start(
                        out=X[:, :, tsl], in_=xT[s].rearrange("(k p) n -> p k n", p=128)[:, :, tsl]),
                        W=[("x", k, t) for k in range(8)], dma=("xin", t))
                bcount = 0
                for l, (kind, j) in enumerate(cfg):
                    if kind != "N":
                        rmsnorm(("g_mix", l))
                    if kind == "N":
                        pass
                    elif kind == "A":
                        mixer_a(l)
                    elif kind == "B":
                        mixer_b(l, bcount)
                        bcount += 1
                    else:
                        mixer_c(l)
                    rmsnorm(("g_mlp", l))
                    mlp(l)
                rmsnorm(("g_fin",), out_x=True)
                for t in range(4):
                    tsl = slice(t * 512, (t + 1) * 512)
                    S.op("sp", lambda e, s=s, tsl=tsl: e.dma_start(
                        out=yT[s].rearrange("(k p) n -> p k n", p=128)[:, :, tsl], in_=X[:, :, tsl]),
                        R=[("x", k, t) for k in range(8)], W=[("y", s, t)], dma=("yout", t))
            S.final_wait("sp", [("y", s, t) for s in range(nseq) for t in range(4)])

        S.dry = True
        program()
        S.dry = False
        program()
        S.emit()
    return nc


DEFAULT_CFG = [("A", 0), ("B", 0), ("C", 0), ("A", 1)]


def run(inputs, cfg, ncore, nseq):
    wb = weight_blocks(inputs, cfg)
    woffs, wtot = layout(wb)
    wsarr = np.concatenate(list(wb.values()), axis=1).astype(np.float32)
    pb = param_blocks(inputs, cfg)
    poffs, ptot = layout(pb)
    pararr = np.concatenate(list(pb.values()), axis=1).astype(np.float32)
    parb = paramb_blocks(inputs, cfg)
    nc = build(nseq, cfg, woffs, wtot, poffs, ptot, parb.shape[1])
    x = np.asarray(inputs["x"], dtype=np.float32)
    in_maps = []
    for c in range(ncore):
        xs = np.ascontiguousarray(x[c * nseq:(c + 1) * nseq].transpose(0, 2, 1))
        in_maps.append({"xT": xs, "ws": wsarr, "par": pararr, "parb": parb})
    if os.environ.get("KTRACE") == "1":
        res = run_bass_kernel_spmd(nc, in_maps, core_ids=list(range(ncore)), trace=True)
        print("EXEC_NS", res.exec_time_ns)
    else:
        res = run_bass_kernel_spmd(nc, in_maps, core_ids=list(range(ncore)))
    outs = [np.asarray(r["yT"]).transpose(0, 2, 1) for r in res.results]
    return np.ascontiguousarray(np.concatenate(outs, axis=0)).astype(np.float32)


def kernel(**inputs):
    inputs = {k: np.asarray(v) for k, v in inputs.items()}
    return run(inputs, DEFAULT_CFG, NCORE, inputs["x"].shape[0] // NCORE)
```

```python
import contextlib
import numpy as np
import concourse.bass as bass
import concourse.mybir as mybir
from concourse.bass_utils import run_bass_kernel_spmd

F32 = mybir.dt.float32
BF16 = mybir.dt.bfloat16
AF = mybir.ActivationFunctionType
ALU = mybir.AluOpType

D = 1024
SEQ = 2048
NCORE = 8
SEM_LIMIT = 30000
INORDER_SAFE = ("pe", "sp")
NSLOT = 3
import os
KSTOP = int(os.environ.get('KSTOP', '99'))
KSTOPC = int(os.environ.get('KSTOPC', '99'))
KSUB = int(os.environ.get('KSUB', '99'))
TBF16 = os.environ.get('TBF16', '1') == '1'
ARENA_COLS = 20800


class Sched:
    ENGS = ("pe", "act", "dve", "pool", "sp")

    def __init__(self, nc, stack):
        self.nc = nc
        self.stack = stack
        self.dry = False
        self.E = {n: dict(ops=[], sem=None, cnt=0, seen={}, nsem=0, last=None) for n in self.ENGS}
        self.res = {}
        self.dsem = {}

    def _newsem(self, name):
        return self.stack.enter_context(self.nc.semaphore(name))

    def _tok(self, eng):
        E = self.E[eng]
        if E["sem"] is None or E["cnt"] >= SEM_LIMIT:
            E["sem"] = self._newsem("s_%s_%d" % (eng, E["nsem"]))
            E["nsem"] += 1
            E["cnt"] = 0
        E["cnt"] += 1
        t = (E["sem"], E["cnt"], eng, 1)
        E["last"] = t
        return t

    def _dtok(self, key):
        d = self.dsem.get(key)
        if d is None or d[1] >= SEM_LIMIT:
            n = 0 if d is None else d[2] + 1
            nm = "d_" + "".join(ch for ch in str(key) if ch.isalnum()) + "_%d" % n
            d = [self._newsem(nm), 0, n]
            self.dsem[key] = d
        d[1] += 16
        return (d[0], d[1], "dma", 16)

    def op(self, eng, fns, R=(), W=(), dma=None):
        if self.dry:
            return None
        if callable(fns):
            fns = [fns]
        psr = [r for r in R if isinstance(r, tuple) and r and r[0] == "ps"]
        if psr:
            R = [r for r in R if not (isinstance(r, tuple) and r and r[0] == "ps")]
            W = list(W) + psr
        E = self.E[eng]
        need = {}

        def add(tok):
            if tok is None:
                return
            sem, val, src, _ = tok
            if src == eng and eng in INORDER_SAFE:
                return
            k = id(sem)
            if k not in need or need[k][1] < val:
                need[k] = (sem, val)

        for r in R:
            st = self.res.get(r)
            if st:
                add(st[0])
        for w in W:
            st = self.res.get(w)
            if st:
                add(st[0])
                for t in st[1]:
                    add(t)
        waits = []
        for k, (sem, val) in need.items():
            if E["seen"].get(k, 0) >= val:
                continue
            E["seen"][k] = val
            waits.append((sem, val))
        tok = self._dtok(dma) if dma is not None else self._tok(eng)
        E["ops"].append((waits, fns, tok))
        for r in R:
            st = self.res.get(r)
            if st is None:
                self.res[r] = [None, [tok]]
            else:
                st[1].append(tok)
        for w in W:
            self.res[w] = [tok, []]
        return tok

    def barrier(self, engs=("pe", "act", "dve", "pool")):
        if self.dry:
            return
        lasts = [self.E[e]["last"] for e in engs if self.E[e]["last"] is not None]
        for e in engs:
            E = self.E[e]
            waits = []
            for (sem, val, src, _) in lasts:
                if src == e:
                    continue
                k = id(sem)
                if E["seen"].get(k, 0) >= val:
                    continue
                E["seen"][k] = val
                waits.append((sem, val))
            if waits:
                E["ops"].append((waits, [], None))
        E = self.E["sp"]
        waits = []
        for (sem, val, src, _) in lasts:
            k = id(sem)
            if E["seen"].get(k, 0) >= val:
                continue
            E["seen"][k] = val
            waits.append((sem, val))
        if waits:
            E["ops"].append((waits, [], None))

    def final_wait(self, eng, keys):
        if self.dry:
            return
        waits = []
        for k in keys:
            st = self.res.get(k)
            if not st:
                continue
            for t in [st[0]] + st[1]:
                if t is not None:
                    waits.append((t[0], t[1]))
        self.E[eng]["ops"].append((waits, [], None))

    def emit(self):
        nc = self.nc
        S = self

        def run(e, name):
            for waits, fns, tok in S.E[name]["ops"]:
                for sem, val in waits:
                    e.wait_ge(sem, val)
                inst = None
                for fn in fns:
                    inst = fn(e)
                if tok is not None and inst is not None:
                    inst.then_inc(tok[0], tok[3])

        with nc.Block() as block:
            @block.tensor
            def _(e):
                run(e, "pe")

            @block.scalar
            def _(e):
                run(e, "act")

            @block.vector
            def _(e):
                run(e, "dve")

            @block.gpsimd
            def _(e):
                run(e, "pool")

            @block.sync
            def _(e):
                run(e, "sp")


def _pk(w, k):
    n = w.shape[1]
    return np.ascontiguousarray(w.reshape(k, 128, n).transpose(1, 0, 2).reshape(128, k * n))


def weight_blocks(inp, cfg):
    B = {}
    for l, (kind, j) in enumerate(cfg):
        if kind == "A":
            w_in = inp["a_w_in"][j]
            gw = inp["a_gate_w"][j]
            for c in range(8):
                blk = np.concatenate([w_in[:, c * 128:(c + 1) * 128], w_in[:, 1024 + c * 128:1024 + (c + 1) * 128]], axis=1)
                B[("a_in", l, c)] = _pk(blk, 8)
                B[("a_gw", l, c)] = np.ascontiguousarray(gw[:, :, c].transpose(2, 0, 1, 3).reshape(128, 512))
            for half in range(2):
                B[("a_out", l, half)] = _pk(inp["a_w_out"][j][:, half * 512:(half + 1) * 512], 8)
        elif kind == "B":
            w_in = inp["b_w_in"][j]
            for blk in range(2):
                B[("b_in_u", l, blk)] = _pk(w_in[:, blk * 512:(blk + 1) * 512], 8)
            for blk in range(2):
                B[("b_in_v", l, blk)] = _pk(w_in[:, 1024 + blk * 512:1024 + (blk + 1) * 512], 8)
            B[("b_ws", l)] = np.ascontiguousarray(inp["b_w_s"][j].transpose(2, 0, 1).reshape(128, 1024))
            for half in range(2):
                B[("b_out", l, half)] = _pk(inp["b_w_out"][j][:, half * 512:(half + 1) * 512], 8)
        elif kind == "C":
            w_in = inp["c_w_in"][j]
            B[("c_ab", l)] = _pk(w_in[:, 4096:4128], 8)
            for hd in range(8):
                blk = np.concatenate([w_in[:, t * 1024 + hd * 128:t * 1024 + (hd + 1) * 128] for t in range(4)], axis=1)
                B[("c_in", l, hd)] = _pk(blk, 8)
                B[("c_out", l, hd)] = np.ascontiguousarray(inp["c_w_out"][j][hd * 128:(hd + 1) * 128, :])
        for fb in range(8):
            B[("mlp_up", l, fb)] = _pk(inp["mlp_w_up"][l][:, fb * 512:(fb + 1) * 512], 8)
            B[("mlp_dn", l, fb)] = _pk(inp["mlp_w_down"][l][fb * 512:(fb + 1) * 512, :], 4)
    return B


def _col(v):
    return np.ascontiguousarray(np.asarray(v).reshape(8, 128).T)


def param_blocks(inp, cfg):
    P = {}
    for l, (kind, j) in enumerate(cfg):
        P[("g_mix", l)] = _col(inp["norm_mix_g"][l])
        P[("g_mlp", l)] = _col(inp["norm_mlp_g"][l])
        if kind == "A":
            P[("a_cw", l)] = np.ascontiguousarray(inp["a_conv_w"][j].reshape(4, 8, 128).transpose(2, 1, 0).reshape(128, 32))
            P[("a_cb", l)] = _col(inp["a_conv_b"][j])
            P[("a_gb", l)] = np.ascontiguousarray(inp["a_gate_b"][j].transpose(3, 2, 0, 1).reshape(128, 32))
            P[("a_lam", l)] = np.ascontiguousarray(inp["a_lambda"][j].reshape(2, 8, 128).transpose(2, 1, 0).reshape(128, 16))
        elif kind == "C":
            P[("c_cw", l)] = np.ascontiguousarray(inp["c_conv_w"][j].reshape(4, 3, 8, 128).transpose(3, 2, 1, 0).reshape(128, 96))
            P[("c_alog", l)] = np.ascontiguousarray(np.broadcast_to(inp["c_a_log"][j].reshape(1, 16), (128, 16)))
            P[("c_dtb", l)] = np.ascontiguousarray(np.broadcast_to(inp["c_dt_bias"][j].reshape(1, 16), (128, 16)))
            P[("c_ng", l)] = np.ascontiguousarray(inp["c_norm_g"][j].reshape(128, 1))
    P[("g_fin",)] = _col(inp["norm_final_g"])
    return P


def paramb_blocks(inp, cfg):
    out = []
    for l, (kind, j) in enumerate(cfg):
        if kind == "B":
            row = np.concatenate([inp["b_ln_g"][j], inp["b_ln_b"][j], inp["b_b_s"][j].reshape(-1)])
            out.append(np.broadcast_to(row[None, :], (128, 3072)))
    if not out:
        out = [np.zeros((128, 3072), np.float32)]
    return np.ascontiguousarray(np.concatenate(out, axis=1)).astype(np.float32)


def layout(blocks):
    offs = {}
    o = 0
    for k, v in blocks.items():
        offs[k] = (o, v.shape[1])
        o += v.shape[1]
    return offs, o


def build(nseq, cfg, woffs, wtot, poffs, ptot, pbtot):
    nc = bass.Bass("TRN2", target_bir_lowering=False)
    xT = nc.dram_tensor("xT", [nseq, D, SEQ], F32, kind="ExternalInput").ap()
    ws = nc.dram_tensor("ws", [128, wtot], F32, kind="ExternalInput").ap()
    par = nc.dram_tensor("par", [128, ptot], F32, kind="ExternalInput").ap()
    parb = nc.dram_tensor("parb", [128, pbtot], F32, kind="ExternalInput").ap()
    yT = nc.dram_tensor("yT", [nseq, D, SEQ], F32, kind="ExternalOutput").ap()

    with contextlib.ExitStack() as st:
        S = Sched(nc, st)
        T = lambda name, shape, dt: st.enter_context(nc.sbuf_tensor(name, shape, dt))
        X = T("X", [128, 8, SEQ], F32)
        H = T("H", [128, 8, SEQ], BF16)
        RING = [T("ring%d" % i, [128, 4096], BF16) for i in range(NSLOT)]
        PAR = T("PAR", [128, ptot], F32)
        PC = T("PC", [128, 64], F32)
        ZC = T("ZC", [128, 2], F32)
        IDB = T("IDB", [128, 128], BF16)
        ONB = T("ONB", [128, 128], BF16)
        IDF = T("IDF", [128, 128], F32)
        MSK = T("MSK", [128, 4, 128], F32)
        LMK = T("LMK", [128, 7, 128], BF16)
        AR = T("AR", [128, ARENA_COLS], F32)
        PS = st.enter_context(nc.psum_tensor("PS", [128, 4096], F32))

        def psb(b, n=1):
            return PS[:, b * 512:(b + n) * 512]

        def pk(b, n=1):
            return [("ps", b + i) for i in range(n)]

        class Arena:
            def __init__(self):
                self.off = 0
                self.tag = 0

            def reset(self):
                self.off = 0
                self.tag += 1

            def f32(self, n):
                v = AR[:, self.off:self.off + n]
                self.off += n
                assert self.off <= ARENA_COLS, self.off
                return v

            def bf16(self, n):
                assert n % 2 == 0
                v = AR[:, self.off:self.off + n // 2].bitcast(BF16)
                self.off += n // 2
                assert self.off <= ARENA_COLS, self.off
                return v

        A = Arena()

        def act(out, in_, func, R, W, scale=1.0, bias=0.0):
            S.op("act", lambda e: e.activation(out=out, in_=in_, func=func, scale=scale, bias=bias), R, W)

        def tt(out, in0, in1, op, R, W, eng="dve"):
            S.op(eng, lambda e: e.tensor_tensor(out=out, in0=in0, in1=in1, op=op), R, W)

        def tsc(out, in0, s1, op0, R, W, s2=None, op1=None, eng="dve"):
            if op1 is None:
                S.op(eng, lambda e: e.tensor_scalar(out=out, in0=in0, scalar1=s1, scalar2=None, op0=op0), R, W)
            else:
                S.op(eng, lambda e: e.tensor_scalar(out=out, in0=in0, scalar1=s1, scalar2=s2, op0=op0, op1=op1), R, W)

        def stt(out, in0, scalar, in1, op0, op1, R, W):
            S.op("dve", lambda e: e.scalar_tensor_tensor(out=out, in0=in0, scalar=scalar, in1=in1, op0=op0, op1=op1), R, W)

        def mm(out, pairs, R, W):
            n = len(pairs)
            fns = []
            for i, (l, r) in enumerate(pairs):
                fns.append(lambda e, l=l, r=r, i=i: e.matmul(out, lhsT=l, rhs=r, start=(i == 0), stop=(i == n - 1)))
            S.op("pe", fns, R, W)

        def copy(out, in_, R, W, eng="dve"):
            S.op(eng, lambda e: e.tensor_copy(out=out, in_=in_), R, W)

        class WRing:
            def __init__(self):
                self.sched = []
                self.pos = 0
                self.issued = 0
                self.released = 0

            def reset(self):
                self.pos = 0
                self.issued = 0
                self.released = 0

            def pump(self):
                if S.dry:
                    return
                while self.issued < len(self.sched) and self.issued - NSLOT < self.released:
                    i = self.issued
                    off, n = woffs[self.sched[i]]
                    slot = i % NSLOT
                    dst = RING[slot][:, 0:n]
                    src = ws[:, off:off + n]
                    S.op("pool", lambda e, dst=dst, src=src: e.dma_start(out=dst, in_=src, max_dma_last_dim=8192),
                         W=[("ring", slot)], dma=("ring", slot))
                    self.issued += 1

            def get(self, name):
                if S.dry:
                    self.sched.append(name)
                    return RING[0], ("ring", 0)
                i = self.pos
                assert self.sched[i] == name, (self.sched[i], name)
                self.pos += 1
                self.pump()
                assert self.issued > i, "weight ring deadlock at %s" % (name,)
                return RING[i % NSLOT], ("ring", i % NSLOT)

            def done(self):
                if S.dry:
                    return
                self.released += 1
                self.pump()

        WR = WRing()

        def pcol(name, c, n=1):
            o, _ = poffs[name]
            return PAR[:, o + c:o + c + n]

        def setup():
            S.op("sp", lambda e: e.dma_start(out=PAR[:], in_=par[:, :]), W=["PAR"], dma="PAR")
            S.op("dve", lambda e: e.memset(ONB[:], 1.0), W=["ONB"])
            S.op("dve", lambda e: e.memset(ZC[:], 0.0), W=["ZC"])
            S.op("dve", lambda e: e.memset(IDF[:], 0.0), W=["IDF"])
            S.op("pool", lambda e: e.affine_select(out=IDF[:], in_=IDF[:], pattern=[[-1, 128]], base=0, channel_multiplier=1,
                                                   compare_op=ALU.not_equal, fill=1.0), R=["IDF"], W=["IDF"])
            copy(IDB[:], IDF[:], ["IDF"], ["IDB"])
            S.op("dve", lambda e: e.memset(MSK[:], 1.0), W=["MSK"])
            specs = [
                (0, 1, -1, 0, ALU.is_gt),
                (1, -1, 1, 0, ALU.is_gt),
                (2, -1, 1, 0, ALU.is_ge),
                (3, 1, -1, 0, ALU.is_ge),
            ]
            for (i, cm, stp, base, cmp_) in specs:
                S.op("pool", lambda e, i=i, cm=cm, stp=stp, base=base, cmp_=cmp_: e.affine_select(
                    out=MSK[:, i, :], in_=MSK[:, i, :], pattern=[[stp, 128]], base=base, channel_multiplier=cm,
                    compare_op=cmp_, fill=0.0), R=["MSK"], W=["MSK"])
            A.reset()
            Et = A.f32(128)
            BD = [A.f32(128), A.f32(128)]
            prev = IDF[:]
            prevk = "IDF"
            for li in range(7):
                s2 = 2 << li
                cur = BD[li % 2]
                curk = ("BD", li % 2)
                if s2 == 128:
                    S.op("dve", lambda e, cur=cur: e.memset(cur, 1.0), W=[curk])
                else:
                    nb = 128 // s2
                    S.op("dve", lambda e, nb=nb: e.memset(Et[0:nb, :], 1.0), W=["Et"])
                    S.op("pool", lambda e, nb=nb, s2=s2: e.affine_select(out=Et[0:nb, :], in_=Et[0:nb, :], pattern=[[1, 128]], base=0,
                                                                       channel_multiplier=-s2, compare_op=ALU.is_ge, fill=0.0),
                         R=["Et"], W=["Et"])
                    S.op("pool", lambda e, nb=nb, s2=s2: e.affine_select(out=Et[0:nb, :], in_=Et[0:nb, :], pattern=[[-1, 128]], base=s2 - 1,
                                                                       channel_multiplier=s2, compare_op=ALU.is_ge, fill=0.0),
                         R=["Et"], W=["Et"])
                    mm(PS[:, 0:128], [(Et[0:nb, :], Et[0:nb, :])], ["Et"], pk(0))
                    copy(cur, PS[:, 0:128], pk(0), [curk])
                tt(LMK[:, li, :], cur, prev, ALU.subtract, [curk, prevk], ["LMK"])
                prev, prevk = cur, curk
            S.barrier()

        def rmsnorm(gname, out_x=False):
            A.reset()
            SQ = [A.bf16(8 * 512).rearrange("p (k n) -> p k n", k=8) for _ in range(2)]
            RS = [A.f32(512) for _ in range(2)]
            for t in range(4):
                b = t % 2
                tsl = slice(t * 512, (t + 1) * 512)
                act(SQ[b], X[:, :, tsl], AF.Square, [("x", k, t) for k in range(8)], [("sq", b, k) for k in range(8)])
                bank = 2 * b
                mm(psb(bank), [(ONB[:], SQ[b][:, k, :]) for k in range(8)],
                   ["ONB"] + [("sq", b, k) for k in range(8)], pk(bank))
                act(RS[b], psb(bank), AF.Sqrt, pk(bank), [("rs", b)], scale=1.0 / D, bias=1e-6)
                S.op("dve", lambda e, b=b: e.reciprocal(out=RS[b], in_=RS[b]), [("rs", b)], [("rs", b)])
                for k in range(8):
                    if out_x:
                        stt(X[:, k, tsl], X[:, k, tsl], pcol(gname, k), RS[b], ALU.mult, ALU.mult,
                            [("x", k, t), ("rs", b), "PAR"], [("x", k, t)])
                    else:
                        stt(H[:, k, tsl], X[:, k, tsl], pcol(gname, k), RS[b], ALU.mult, ALU.mult,
                            [("x", k, t), ("rs", b), "PAR"], [("h", k, t)])
            S.barrier()

        def hkeys(t):
            return [("h", k, t) for k in range(8)]

        def outproj(name_fn, YT, ykeys):
            for half in range(2):
                slot, rk = WR.get(name_fn(half))
                wo = slot[:, 0:4096].rearrange("p (c n) -> p c n", c=8)
                for t in range(4):
                    tsl = slice(t * 512, (t + 1) * 512)
                    for mo in range(4):
                        bank = (t * 4 + mo) % 8
                        mm(psb(bank), [(wo[:, cc, mo * 128:(mo + 1) * 128], YT[:, cc, tsl]) for cc in range(8)],
                           [rk] + ykeys, pk(bank))
                        ko = half * 4 + mo
                        tt(X[:, ko, tsl], psb(bank), X[:, ko, tsl], ALU.add, pk(bank) + [("x", ko, t)], [("x", ko, t)])
                WR.done()

        def mlp(l):
            A.reset()
            H1 = [A.bf16(4 * 512).rearrange("p (c n) -> p c n", c=4) for _ in range(2)]
            SQ1 = [A.bf16(512) for _ in range(4)]
            steps = [(fb, t) for fb in range(8) for t in range(4)]
            held = {}

            def up(i):
                fb, t = steps[i]
                if t == 0:
                    held[("u", fb)] = WR.get(("mlp_up", l, fb))
                slot, rk = held[("u", fb)]
                wu = slot[:, 0:4096].rearrange("p (k n) -> p k n", k=8)
                tsl = slice(t * 512, (t + 1) * 512)
                hb = i % 2
                for mi in range(4):
                    bank = mi
                    mm(psb(bank), [(wu[:, k, mi * 128:(mi + 1) * 128], H[:, k, tsl]) for k in range(8)],
                       [rk] + hkeys(t), pk(bank))
                    act(SQ1[mi], psb(bank), AF.Square, pk(bank), [("sq1", mi)])
                    stt(H1[hb][:, mi, :], psb(bank), 0.0, SQ1[mi], ALU.is_gt, ALU.mult,
                        pk(bank) + [("sq1", mi)], [("h1", hb, mi)])
                if t == 3:
                    WR.done()

            def down(i):
                fb, t = steps[i]
                if t == 0:
                    held[("d", fb)] = WR.get(("mlp_dn", l, fb))
                slot, rk = held[("d", fb)]
                wd = slot[:, 0:4096].rearrange("p (c n) -> p c n", c=4)
                tsl = slice(t * 512, (t + 1) * 512)
                hb = i % 2
                for mo in range(8):
                    bank = 4 + (mo % 4)
                    mm(psb(bank), [(wd[:, c, mo * 128:(mo + 1) * 128], H1[hb][:, c, :]) for c in range(4)],
                       [rk] + [("h1", hb, c) for c in range(4)], pk(bank))
                    tt(X[:, mo, tsl], psb(bank), X[:, mo, tsl], ALU.add, pk(bank) + [("x", mo, t)], [("x", mo, t)])
                if t == 3:
                    WR.done()

            n = len(steps)
            for i in range(n + 1):
                if i < n:
                    up(i)
                if i >= 1:
                    down(i - 1)
            S.barrier()

        def mixer_a(l):
            A.reset()
            YT = A.bf16(8 * SEQ).rearrange("p (c n) -> p c n", c=8)
            XP = A.f32(SEQ + 4)
            XR = A.f32(SEQ)
            XB = A.bf16(SEQ)
            T1 = A.f32(SEQ)
            T2 = A.f32(SEQ)
            T3 = A.f32(SEQ)
            lo, _ = poffs[("a_lam", l)]
            act(PC[:, 32:48], PAR[:, lo:lo + 16], AF.Exp, ["PAR"], ["PC"], scale=-1.0)
            act(PC[:, 32:48], PC[:, 32:48], AF.Ln, ["PC"], ["PC"], bias=1.0)
            tsc(PC[:, 0:16], PC[:, 32:48], -8.0, ALU.mult, ["PC"], ["PC"])
            tsc(PC[:, 16:32], PC[:, 32:48], -16.0, ALU.mult, ["PC"], ["PC"])
            S.op("dve", lambda e: e.memset(XP[:, 0:2], 0.0), W=[("XP", 0)])
            S.op("dve", lambda e: e.memset(XP[:, SEQ + 2:SEQ + 4], 0.0), W=[("XP", 1)])
            HW_ = SEQ // 2

            def a_gen(c, hf, wv, rk, gw, rkg):
                hs = slice(hf * HW_, (hf + 1) * HW_)
                xs = slice(2 + hf * HW_, 2 + (hf + 1) * HW_)
                t0 = 2 * hf
                kXP, kXR, kXB, kT1, kT2, kT3 = [(nm, hf) for nm in ("XP", "XR", "XB", "T1", "T2", "T3")]
                XPK = [("XP", 0), ("XP", 1)]
                for t in (t0, t0 + 1):
                    tsl = slice(t * 512, (t + 1) * 512)
                    mm(psb(t), [(wv[:, k, 0:128], H[:, k, tsl]) for k in range(8)], [rk] + hkeys(t), pk(t))
                    mm(psb(4 + t), [(wv[:, k, 128:256], H[:, k, tsl]) for k in range(8)], [rk] + hkeys(t), pk(4 + t))
                yield
                act(YT[:, c, hs], psb(t0, 2), AF.Gelu_apprx_tanh, pk(t0, 2), [("yt", c)])
                act(XP[:, xs], psb(4 + t0, 2), AF.Copy, pk(4 + t0, 2), [kXP])
                yield
                tsc(XR[:, hs], XP[:, hf * HW_:hf * HW_ + HW_], pcol(("a_cw", l), c * 4 + 0), ALU.mult, XPK + ["PAR"], [kXR],
                    s2=pcol(("a_cb", l), c), op1=ALU.add)
                for tap in range(1, 4):
                    stt(XR[:, hs], XP[:, hf * HW_ + tap:hf * HW_ + tap + HW_], pcol(("a_cw", l), c * 4 + tap), XR[:, hs],
                        ALU.mult, ALU.add, XPK + [kXR, "PAR"], [kXR])
                yield
                act(XB[:, hs], XR[:, hs], AF.Copy, [kXR], [kXB])
                yield
                for dr in range(2):
                    for g in range(2):
                        for t in (t0, t0 + 1):
                            tsl = slice(t * 512, (t + 1) * 512)
                            mm(psb(g * 4 + t), [(gw[:, dr * 2 + g, :], XB[:, tsl])], [rkg, kXB], pk(g * 4 + t))
                    yield
                    gbo = c * 4 + dr * 2
                    act(T1[:, hs], psb(t0, 2), AF.Sigmoid, pk(t0, 2) + ["PAR"], [kT1], bias=pcol(("a_gb", l), gbo))
                    act(T2[:, hs], psb(4 + t0, 2), AF.Sigmoid, pk(4 + t0, 2) + ["PAR"], [kT2], bias=pcol(("a_gb", l), gbo + 1))
                    ci = c * 2 + dr
                    act(T3[:, hs], T1[:, hs], AF.Exp, [kT1, "PC"], [kT3], scale=PC[:, ci:ci + 1])
                    act(T1[:, hs], T1[:, hs], AF.Exp, [kT1, "PC"], [kT1], scale=PC[:, 16 + ci:16 + ci + 1])
                    act(T1[:, hs], T1[:, hs], AF.Sqrt, [kT1], [kT1], scale=-1.0, bias=1.0)
                    yield
                    tt(T2[:, hs], T2[:, hs], T1[:, hs], ALU.mult, [kT1, kT2], [kT2], eng="pool")
                    tt(T2[:, hs], T2[:, hs], XR[:, hs], ALU.mult, [kT2, kXR], [kT2], eng="pool")
                    yield
                    if dr == 0:
                        if hf == 0:
                            S.op("dve", lambda e: e.tensor_tensor_scan(out=XP[:, xs], data0=T3[:, hs], data1=T2[:, hs], initial=0.0,
                                                                       op0=ALU.mult, op1=ALU.add), [kT3, kT2], [kXP])
                        else:
                            S.op("dve", lambda e: e.tensor_tensor_scan(out=XP[:, xs], data0=T3[:, hs], data1=T2[:, hs],
                                                                       initial=XP[:, 2 + HW_ - 1:2 + HW_],
                                                                       op0=ALU.mult, op1=ALU.add), [kT3, kT2, ("XP", 0)], [kXP])
                    else:
                        if hf == 1:
                            S.op("dve", lambda e: e.tensor_tensor_scan(out=T1[:, hs][:, ::-1], data0=T3[:, hs][:, ::-1],
                                                                       data1=T2[:, hs][:, ::-1], initial=0.0,
                                                                       op0=ALU.mult, op1=ALU.add), [kT3, kT2], [kT1])
                        else:
                            yield
                            S.op("dve", lambda e: e.tensor_tensor_scan(out=T1[:, hs][:, ::-1], data0=T3[:, hs][:, ::-1],
                                                                       data1=T2[:, hs][:, ::-1], initial=T1[:, HW_:HW_ + 1],
                                                                       op0=ALU.mult, op1=ALU.add), [kT3, kT2, ("T1", 1)], [kT1])
                    yield
                tt(T1[:, hs], T1[:, hs], XP[:, xs], ALU.add, [kT1, kXP], [kT1])
                tt(YT[:, c, hs], T1[:, hs], YT[:, c, hs], ALU.mult, [kT1, ("yt", c)], [("yt", c)])

            for c in range(8):
                slot, rk = WR.get(("a_in", l, c))
                wv = slot[:, 0:2048].rearrange("p (k n) -> p k n", k=8)
                slotg, rkg = WR.get(("a_gw", l, c))
                gw = slotg[:, 0:512].rearrange("p (q o) -> p q o", q=4)
                gens = [a_gen(c, 0, wv, rk, gw, rkg), a_gen(c, 1, wv, rk, gw, rkg)]
                while gens:
                    for g_ in list(gens):
                        try:
                            next(g_)
                        except StopIteration:
                            gens.remove(g_)
                WR.done()
                WR.done()
            outproj(lambda half: ("a_out", l, half), YT, [("yt", c) for c in range(8)])
            S.barrier()

        def mixer_b(l, bidx):
            A.reset()
            UT = A.bf16(8 * SEQ).rearrange("p (c n) -> p c n", c=8)
            LGB = A.f32(3072)
            VT = [A.f32(1024) for _ in range(2)]
            VN = [A.bf16(1024) for _ in range(2)]
            TB = [A.f32(1024) for _ in range(2)]
            STT = [A.f32(12) for _ in range(2)]
            MV = [A.f32(4) for _ in range(2)]
            S.op("sp", lambda e: e.dma_start(out=LGB, in_=parb[:, bidx * 3072:(bidx + 1) * 3072]), W=["LGB"], dma="LGB")
            for blk in range(2):
                slot, rk = WR.get(("b_in_u", l, blk))
                wb = slot[:, 0:4096].rearrange("p (k n) -> p k n", k=8)
                for t in range(4):
                    tsl = slice(t * 512, (t + 1) * 512)
                    b0 = (t % 2) * 4
                    for mi in range(4):
                        mm(psb(b0 + mi), [(wb[:, k, mi * 128:(mi + 1) * 128], H[:, k, tsl]) for k in range(8)],
                           [rk] + hkeys(t), pk(b0 + mi))
                    act(UT[:, blk * 4:(blk + 1) * 4, tsl], PS[:, b0 * 512:(b0 + 4) * 512].rearrange("p (c n) -> p c n", c=4),
                        AF.Gelu_apprx_tanh, pk(b0, 4), [("ut", blk * 4 + mi, t) for mi in range(4)])
                WR.done()
            slot0, rk0 = WR.get(("b_in_v", l, 0))
            slot1, rk1 = WR.get(("b_in_v", l, 1))
            slot2, rk2 = WR.get(("b_ws", l))
            wvv = [slot0[:, 0:4096].rearrange("p (k n) -> p k n", k=8), slot1[:, 0:4096].rearrange("p (k n) -> p k n", k=8)]
            rkv = [rk0, rk1]
            wst = slot2[:, 0:1024].rearrange("p (g n) -> p g n", g=8)
            for tt_ in range(16):
                b = tt_ % 2
                tok = slice(tt_ * 128, (tt_ + 1) * 128)
                t = tt_ // 4
                pb = b * 4
                for blk in range(2):
                    mm(psb(pb + blk), [(H[:, k, tok], wvv[blk][:, k, :]) for k in range(8)], [rkv[blk]] + hkeys(t), pk(pb + blk))
                act(VT[b], psb(pb, 2), AF.Gelu_apprx_tanh, pk(pb, 2), [("vt", b)])
                S.op("dve", lambda e, b=b: e.bn_stats(out=STT[b][:, 0:6], in_=VT[b][:, 0:512]), [("vt", b)], [("st", b)])
                S.op("dve", lambda e, b=b: e.bn_stats(out=STT[b][:, 6:12], in_=VT[b][:, 512:1024]), [("vt", b)], [("st2", b)])
                S.op("dve", lambda e, b=b: e.bn_aggr(out=MV[b][:, 0:2], in_=STT[b][:, 0:12]), [("st", b), ("st2", b)], [("mv", b)])
                act(MV[b][:, 2:3], MV[b][:, 1:2], AF.Sqrt, [("mv", b)], [("mv2", b)], bias=1e-5)
                S.op("dve", lambda e, b=b: e.reciprocal(out=MV[b][:, 2:3], in_=MV[b][:, 2:3]), [("mv2", b)], [("mv2", b)])
                tsc(VT[b], VT[b], MV[b][:, 0:1], ALU.subtract, [("vt", b), ("mv", b), ("mv2", b)], [("vt", b)],
                    s2=MV[b][:, 2:3], op1=ALU.mult)
                tt(VT[b], VT[b], LGB[:, 0:1024], ALU.mult, [("vt", b), "LGB"], [("vt", b)], eng="pool")
                tt(VN[b], VT[b], LGB[:, 1024:2048], ALU.add, [("vt", b), "LGB"], [("vn", b)], eng="pool")
                for g in range(8):
                    bank = pb + 2 + g // 4
                    o = PS[:, bank * 512 + (g % 4) * 128: bank * 512 + (g % 4 + 1) * 128]
                    mm(o, [(VN[b][:, g * 128:(g + 1) * 128], wst[:, g, :])], [("vn", b), rk2], [("ps", bank)])
                tt(TB[b], psb(pb + 2, 2), LGB[:, 2048:3072], ALU.add, pk(pb + 2, 2) + ["LGB"], [("tb", b)])
                uk = [("ut", g, t) for g in range(8)]
                tt(UT[:, :, tok], TB[b][:].rearrange("p (g n) -> p g n", g=8), UT[:, :, tok], ALU.mult,
                   [("tb", b)] + uk, uk, eng="pool")
            WR.done()
            WR.done()
            WR.done()
            outproj(lambda half: ("b_out", l, half), UT, [("ut", g, t) for g in range(8) for t in range(4)])
            S.barrier()

        def mixer_c(l):
            A.reset()
            QSC = 128.0 ** -0.5
            S1K = [("S1", 0), ("S1", 1)]
            S2K = [("S2", 0), ("S2", 1)]
            S3K = [("gcb", c) for c in range(16)]
            BTOK = A.f32(512)
            BT3 = BTOK.rearrange("p (c n) -> p c n", c=16)
            WAB = A.bf16(256).rearrange("p (k n) -> p k n", k=8)
            SM = A.f32(128)
            GCT, EGT, NEGT, KDT, NBT, EGLT = [SM[:, i * 16:(i + 1) * 16] for i in range(6)]
            QT = A.bf16(SEQ)
            KT = A.bf16(SEQ)
            KTOK = A.bf16(SEQ).rearrange("p (c n) -> p c n", c=16)
            VTOK = A.bf16(SEQ).rearrange("p (c n) -> p c n", c=16)
            OT = A.f32(SEQ)
            S1 = A.f32(SEQ + 4)
            S2 = A.f32(SEQ)
            S3 = A.f32(SEQ)
            QG = A.bf16(SEQ)
            YS = A.bf16(SEQ).rearrange("p (c n) -> p c n", c=16)
            QKD = A.bf16(SEQ).rearrange("p (c n) -> p c n", c=16)
            NBC = 4
            An = A.f32(NBC * 128).rearrange("p (c n) -> p c n", c=NBC)
            At = A.f32(NBC * 128).rearrange("p (c n) -> p c n", c=NBC)
            Tt = A.f32(NBC * 128).rearrange("p (c n) -> p c n", c=NBC)
            Yy = A.f32(NBC * 128).rearrange("p (c n) -> p c n", c=NBC)
            D1 = A.f32(NBC * 128)
            D2 = A.f32(NBC * 128)
            M1s = An
            WREP = D1.bitcast(BF16).rearrange("p (k n) -> p k n", k=8)
            RH = [A.bf16(128) for _ in range(2)]
            VN = [A.bf16(128) for _ in range(2)]
            VN2 = [A.bf16(128) for _ in range(2)]
            S32 = A.f32(128)
            SB = A.bf16(128)
            RAW, ACC, SIL = S1, S2, S3
            D13 = D1.rearrange("p (c n) -> p c n", c=NBC)
            D23 = D2.rearrange("p (c n) -> p c n", c=NBC)

            def bank3(b):
                return PS[:, b * 512:b * 512 + NBC * 128].rearrange("p (c n) -> p c n", c=NBC)

            ao, _ = poffs[("c_alog", l)]
            act(PC[:, 48:64], PAR[:, ao:ao + 16], AF.Exp, ["PAR"], ["PCc"])
            tsc(PC[:, 48:64], PC[:, 48:64], -1.0, ALU.mult, ["PCc"], ["PCc"])
            S.op("dve", lambda e: e.memset(S1[:, 0:2], 0.0), W=[*S1K])
            S.op("dve", lambda e: e.memset(S1[:, SEQ + 2:SEQ + 4], 0.0), W=[*S1K])
            slot, rk = WR.get(("c_ab", l))
            wab = slot[:, 0:256].rearrange("p (k n) -> p k n", k=8)
            copy(WAB, wab, [rk], ["WAB"])
            for c in range(16):
                tok = slice(c * 128, (c + 1) * 128)
                mm(PS[:, c * 32:(c + 1) * 32], [(H[:, k, tok], wab[:, k, :]) for k in range(8)], [rk] + hkeys(c // 4), pk(0))
            WR.done()
            act(BTOK, psb(0), AF.Sigmoid, pk(0), ["BTOK"])

            for hd in range(8):
                slot, rk = WR.get(("c_in", l, hd))
                win = slot[:, 0:4096].rearrange("p (k n) -> p k n", k=8)
                cwo, _ = poffs[("c_cw", l)]
                SQ = YS.rearrange("p c n -> p (c n)")
                VTf = QG

                def proj_gen(typ, hf):
                    pb = (typ % 2) * 4
                    t0 = 2 * hf
                    HW_ = SEQ // 2
                    hs = slice(hf * HW_, (hf + 1) * HW_)
                    ys_k = [("ys", b_) for b_ in range(hf * (8 // NBC), (hf + 1) * (8 // NBC))]
                    ot_k = [("ot", c) for c in range(8 * hf, 8 * hf + 8)]
                    s3_k = [("gcb", c) for c in range(8 * hf, 8 * hf + 8)]
                    for t in (t0, t0 + 1):
                        tsl = slice(t * 512, (t + 1) * 512)
                        mm(psb(pb + t), [(win[:, k, typ * 128:(typ + 1) * 128], H[:, k, tsl]) for k in range(8)],
                           [rk] + hkeys(t), pk(pb + t))
                    yield
                    act(RAW[:, 2 + hf * HW_:2 + (hf + 1) * HW_], psb(pb + t0, 2), AF.Copy, pk(pb + t0, 2), [("S1", hf)])
                    yield
                    co = cwo + hd * 12 + typ * 4
                    tsc(ACC[:, hs], RAW[:, hf * HW_:hf * HW_ + HW_], PAR[:, co:co + 1], ALU.mult, S1K + ["PAR"], [("S2", hf)])
                    for tap in range(1, 4):
                        stt(ACC[:, hs], RAW[:, hf * HW_ + tap:hf * HW_ + tap + HW_], PAR[:, co + tap:co + tap + 1], ACC[:, hs],
                            ALU.mult, ALU.add, S1K + [("S2", hf), "PAR"], [("S2", hf)])
                    yield
                    p3d = PS[:, hf * HW_:(hf + 1) * HW_].rearrange("p (c n) -> p c n", c=8)
                    if typ == 2:
                        act(VTf[:, hs], ACC[:, hs], AF.Silu, [("S2", hf)], ["QG"])
                        yield
                        for c in range(8 * hf, 8 * hf + 8):
                            mm(PS[:, c * 128:(c + 1) * 128], [(VTf[:, c * 128:(c + 1) * 128], IDB[:])], ["QG", "IDB"], pk(c // 4))
                        yield
                        copy(VTOK[:, 8 * hf:8 * hf + 8, :], p3d, pk(2 * hf, 2), ["VTOK"])
                    else:
                        act(SIL[:, hs], ACC[:, hs], AF.Silu, [("S2", hf)], s3_k)
                        yield
                        act(SQ[:, hs], SIL[:, hs], AF.Square, s3_k, ys_k)
                        yield
                        pb2 = 4 - pb
                        for t in (t0, t0 + 1):
                            tsl = slice(t * 512, (t + 1) * 512)
                            mm(psb(pb2 + t), [(ONB[:], SQ[:, tsl])], ["ONB"] + ys_k, pk(pb2 + t))
                        yield
                        act(OT[:, hs], psb(pb2 + t0, 2), AF.Sqrt, pk(pb2 + t0, 2), ot_k, bias=1e-6)
                        yield
                        S.op("dve", lambda e: e.reciprocal(out=OT[:, hs], in_=OT[:, hs]), ot_k, ot_k)
                        yield
                        if typ == 0:
                            stt(QT[:, hs], SIL[:, hs], QSC, OT[:, hs], ALU.mult, ALU.mult, s3_k + ot_k, ["QT"])
                        else:
                            tt(KT[:, hs], SIL[:, hs], OT[:, hs], ALU.mult, s3_k + ot_k, ["KT"])
                            yield
                            for c in range(8 * hf, 8 * hf + 8):
                                mm(PS[:, c * 128:(c + 1) * 128], [(KT[:, c * 128:(c + 1) * 128], IDB[:])], ["KT", "IDB"], pk(c // 4))
                            yield
                            act(KTOK[:, 8 * hf:8 * hf + 8, :], p3d, AF.Copy, pk(2 * hf, 2), ["KTOK"])

                for typ in range(3):
                    gens = [proj_gen(typ, 0), proj_gen(typ, 1)]
                    while gens:
                        for g in list(gens):
                            try:
                                next(g)
                            except StopIteration:
                                gens.remove(g)

                gk = [("gcb", c) for c in range(16)]
                for dr in range(2):
                    if KSTOPC <= 1:
                        continue
                    n = dr * 8 + hd
                    bcol = BT3[:, :, 16 + n]
                    copy(WREP, WAB[:, :, n:n + 1].to_broadcast([128, 8, 128]), ["WAB"], ["D1"])
                    GCB, EG = S3, S2[:, 0:SEQ // 2].bitcast(BF16)
                    gk = [("gcb", c) for c in range(16)]
                    lastc = 127 if dr == 0 else 0
                    GL = GCB[:, lastc::128]
                    EGL = EGLT

                    def dir_gen(hf):
                        HW_ = SEQ // 2
                        hs = slice(hf * HW_, (hf + 1) * HW_)
                        cr = range(8 * hf, 8 * hf + 8)
                        cols = slice(8 * hf, 8 * hf + 8)
                        t0 = 2 * hf
                        g1 = S1[:, 2 + hf * HW_:2 + (hf + 1) * HW_]
                        kS1 = ("S1", hf)
                        gkh = [("gcb", c) for c in cr]
                        for t in (t0, t0 + 1):
                            tsl = slice(t * 512, (t + 1) * 512)
                            mm(psb(t), [(WREP[:, k, :], H[:, k, tsl]) for k in range(8)], ["D1"] + hkeys(t), pk(t))
                        yield
                        act(g1, psb(t0, 2), AF.Exp, pk(t0, 2) + ["PAR"], [kS1], bias=pcol(("c_dtb", l), n))
                        act(g1, g1, AF.Ln, [kS1], [kS1], bias=1.0)
                        yield
                        tsc(g1, g1, PC[:, 48 + n:48 + n + 1], ALU.mult, [kS1, "PCc"], [kS1])
                        for c in cr:
                            sl = slice(c * 128, (c + 1) * 128)
                            gl = S1[:, 2 + c * 128:2 + (c + 1) * 128]
                            if dr == 0:
                                S.op("dve", lambda e, sl=sl, gl=gl: e.tensor_tensor_scan(out=GCB[:, sl], data0=ONB[:], data1=gl, initial=0.0,
                                                                                        op0=ALU.mult, op1=ALU.add), [kS1, "ONB"], [("gcb", c)])
                            else:
                                S.op("dve", lambda e, sl=sl, gl=gl: e.tensor_tensor_scan(out=GCB[:, sl][:, ::-1], data0=ONB[:], data1=gl[:, ::-1],
                                                                                        initial=0.0, op0=ALU.mult, op1=ALU.add),
                                     [kS1, "ONB"], [("gcb", c)])
                        yield
                        act(EG[:, hs], GCB[:, hs], AF.Exp, gkh, [("S2", 0)])
                        TMPh = g1.rearrange("p (c n) -> p c n", c=8)
                        tt(TMPh, GCB[:, hs].rearrange("p (c n) -> p c n", c=8), IDF[:].unsqueeze(1).to_broadcast([128, 8, 128]), ALU.mult,
                           gkh + ["IDF"], [kS1])
                        S.op("dve", lambda e: e.tensor_reduce(out=GCT[:, cols], in_=TMPh, op=ALU.add, axis=mybir.AxisListType.X),
                             [kS1], [("GCT", hf)])
                        yield
                        act(EGT[:, cols], GCT[:, cols], AF.Exp, [("GCT", hf)], [("EGT", hf)])
                        act(EGLT[:, cols], GL[:, cols], AF.Exp, gkh, [("EGLT", hf)])
                        tt(QG[:, hs], QT[:, hs], EG[:, hs], ALU.mult, ["QT", ("S2", 0)], ["QG"])
                        yield
                        tsc(NEGT[:, cols], EGT[:, cols], -1.0, ALU.mult, [("EGT", hf)], [("NEGT", hf)])
                        tt(KDT[:, cols], GL[:, cols], GCT[:, cols], ALU.subtract, gkh + [("GCT", hf)], [("KDT", hf)])
                        yield
                        act(KDT[:, cols], KDT[:, cols], AF.Exp, [("KDT", hf)], [("KDT", hf)])

                    gens = [dir_gen(0), dir_gen(1)]
                    while gens:
                        for g in list(gens):
                            try:
                                next(g)
                            except StopIteration:
                                gens.remove(g)
                    SET1K = ["An1", "At1", "T1", "Y1", "D1x1", "D2x1"]
                    S.op("dve", lambda e: e.memset(ZC[:, 1:2], 0.0), W=S1K + S2K + SET1K + ["ZCf"])
                    idb3 = IDF[:].unsqueeze(1).to_broadcast([128, NBC, 128])

                    def lm3(li):
                        return LMK[:, li, :].unsqueeze(1).to_broadcast([128, NBC, 128])

                    def v3(ap):
                        return ap.rearrange("p (c n) -> p c n", c=NBC)

                    def batch_gen(bq, sid):
                        W_ = NBC * 128
                        if sid == 0:
                            base = [An.rearrange("p c n -> p (c n)"), At.rearrange("p c n -> p (c n)"), Tt.rearrange("p c n -> p (c n)"),
                                    Yy.rearrange("p c n -> p (c n)"), D1, D2]
                            b0 = 4
                        else:
                            base = [S1[:, 2 + i * W_:2 + (i + 1) * W_] for i in range(4)] + \
                                   [S2[:, SEQ // 2:SEQ // 2 + W_], S2[:, SEQ // 2 + W_:SEQ // 2 + 2 * W_]]
                            b0 = 0
                        An_, At_, Tt_, Yy_ = [v3(b) for b in base[0:4]]
                        D1_, D2_ = base[4], base[5]
                        if TBF16:
                            lowv = [v3(b.bitcast(BF16)[:, 0:W_]) for b in base]
                        else:
                            lowv = [v3(b) for b in base]
                        M1L, _, TtL, YyL, D1L, D2L = lowv
                        kA, kAt, kT_, kY, kD1, kD2 = ["%s%d" % (nm, sid) for nm in ("An", "At", "T", "Y", "D1x", "D2x")]
                        if sid == 0:
                            kD1, kD2 = "D1", "D2"
                        D13_, D23_ = v3(D1_), v3(D2_)
                        M1_ = An_

                        def pq(bi, ci):
                            return PS[:, (b0 + bi) * 512 + ci * 128:(b0 + bi) * 512 + (ci + 1) * 128]

                        def pq3(bi):
                            return v3(PS[:, (b0 + bi) * 512:(b0 + bi) * 512 + W_])
                        cs = [bq * NBC + ci for ci in range(NBC)]
                        for ci, c in enumerate(cs):
                            sl = slice(c * 128, (c + 1) * 128)
                            mm(pq(0, ci), [(KT[:, sl], KT[:, sl])], ["KT"], pk(b0))
                            mm(pq(1, ci), [(KT[:, sl], QT[:, sl])], ["KT", "QT"], pk(b0 + 1))
                        for ci, c in enumerate(cs):
                            sl = slice(c * 128, (c + 1) * 128)
                            tsc(D13_[:, ci, :], GCB[:, sl], GCT[:, c:c + 1], ALU.subtract, [("gcb", c), ("GCT", c // 8), "ZC"], [kD1], s2=ZC[:, 0:1], op1=ALU.max)
                            tsc(D23_[:, ci, :], GCB[:, sl], GCT[:, c:c + 1], ALU.subtract, [("gcb", c), ("GCT", c // 8), "ZC"], [kD2], s2=ZC[:, 0:1], op1=ALU.min)
                        yield
                        act(D1_, D1_, AF.Exp, [kD1], [kD1], scale=-1.0)
                        act(D2_, D2_, AF.Exp, [kD2], [kD2])
                        yield
                        tt(D13_, D13_, MSK[:, dr, :].unsqueeze(1).to_broadcast([128, NBC, 128]), ALU.mult, [kD1, "MSK"], [kD1])
                        tt(D23_, D23_, MSK[:, 2 + dr, :].unsqueeze(1).to_broadcast([128, NBC, 128]), ALU.mult, [kD2, "MSK"], [kD2])
                        for ci, c in enumerate(cs):
                            stt(An_[:, ci, :], pq(0, ci), bcol[:, c:c + 1], D13_[:, ci, :], ALU.mult, ALU.mult, pk(b0) + ["BTOK", kD1], [kA])
                        tt(QKD[:, bq * NBC:(bq + 1) * NBC, :], pq3(1), D23_, ALU.mult, pk(b0 + 1) + [kD2], [("qkd", bq)])
                        yield
                        for ci in range(NBC):
                            mm(pq(2, ci), [(An_[:, ci, :], IDF[:])], [kA, "IDF"], pk(b0 + 2))
                        yield
                        act(At_, pq3(2), AF.Copy, pk(b0 + 2), [kAt])
                        tt(D13_, An_, lm3(0), ALU.mult, [kA, "LMK"], [kD1], eng="pool")
                        yield
                        tt(D23_, At_, lm3(0), ALU.mult, [kAt, "LMK"], [kD2], eng="pool")
                        tt(TtL, idb3, D13_, ALU.subtract, ["IDF", kD1], [kT_])
                        yield
                        tt(YyL, idb3, D23_, ALU.subtract, ["IDF", kD2], [kY])
                        for li in range(1, 7):
                            last = li == 6
                            dbuf, dkey = (D1L, kD1) if li % 2 else (D2L, kD2)
                            tt(dbuf, At_, lm3(li), ALU.mult, [kAt, "LMK"], [dkey], eng="pool")
                            yield
                            for ci in range(NBC):
                                mm(pq(0, ci), [(dbuf[:, ci, :], TtL[:, ci, :])], [dkey, kT_], pk(b0))
                            yield
                            act(M1L, pq3(0), AF.Copy, pk(b0), [kA])
                            yield
                            if not last:
                                for ci in range(NBC):
                                    mm(pq(1, ci), [(YyL[:, ci, :], M1L[:, ci, :])], [kY, kA], pk(b0 + 1))
                            for ci in range(NBC):
                                mm(pq(2, ci), [(M1L[:, ci, :], YyL[:, ci, :])], [kY, kA], pk(b0 + 2))
                            yield
                            if not last:
                                tt(TtL, TtL, pq3(1), ALU.subtract, [kT_] + pk(b0 + 1), [kT_])
                            tt(YyL, YyL, pq3(2), ALU.subtract, [kY] + pk(b0 + 2), [kY])
                        yield
                        for ci, c in enumerate(cs):
                            tsc(YS[:, c, :], YyL[:, ci, :], bcol[:, c:c + 1], ALU.mult, [kY, "BTOK"], [("ys", bq)])

                    order = list(range(16)) if dr == 0 else list(range(15, -1, -1))

                    def rec_gen(si0, chunks):
                        for k_, c in enumerate(chunks):
                            si = si0 + k_
                            par_ = si % 2
                            sl = slice(c * 128, (c + 1) * 128)
                            bq = c // NBC
                            o_ = par_ * 256
                            p1 = PS[:, 3 * 512 + o_:3 * 512 + o_ + 128]
                            p3 = PS[:, 3 * 512 + o_ + 128:3 * 512 + o_ + 256]
                            p2 = PS[:, 7 * 512 + o_:7 * 512 + o_ + 128]
                            p4 = PS[:, 7 * 512 + o_ + 128:7 * 512 + o_ + 256]
                            mm(p1, [(KT[:, sl], SB)], ["KT", "SB"], pk(3))
                            yield
                            stt(RH[par_], p1, NEGT[:, c:c + 1], VTOK[:, c, :], ALU.mult, ALU.add, pk(3) + [("NEGT", c // 8), "VTOK"], [("rh", par_)])
                            yield
                            mm(p2, [(YS[:, c, :], RH[par_])], [("ys", bq), ("rh", par_)], pk(7))
                            yield
                            act(VN[par_], p2, AF.Copy, pk(7), [("vn", par_)])
                            tsc(VN2[par_], p2, KDT[:, c:c + 1], ALU.mult, pk(7) + [("KDT", c // 8)], [("vn2", par_)])
                            yield
                            mm(p3, [(SB, QG[:, sl]), (VN[par_], QKD[:, c, :])], ["SB", "QG", ("vn", par_), ("qkd", bq)], pk(3))
                            mm(p4, [(KTOK[:, c, :], VN2[par_])], ["KTOK", ("vn2", par_)], pk(7))
                            yield
                            stt(SB, S32, EGL[:, c:c + 1], p4, ALU.mult, ALU.add, ["S32", ("EGLT", c // 8)] + pk(7), ["SB"])
                            stt(S32, S32, EGL[:, c:c + 1], p4, ALU.mult, ALU.add, ["S32", ("EGLT", c // 8)] + pk(7), ["S32"])
                            if dr == 0:
                                act(OT[:, sl], p3, AF.Copy, pk(3), [("ot", c)])
                            else:
                                tt(OT[:, sl], p3, OT[:, sl], ALU.add, pk(3) + [("ot", c)], [("ot", c)])
                            yield

                    def drive(gens):
                        while gens:
                            for g in list(gens):
                                try:
                                    next(g)
                                except StopIteration:
                                    gens.remove(g)

                    npair = 16 // NBC // 2
                    pairs = list(range(npair)) if dr == 0 else list(range(npair - 1, -1, -1))
                    nper = 16 // npair
                    if KSTOPC > 2:
                        drive([batch_gen(2 * pairs[0], 0), batch_gen(2 * pairs[0] + 1, 1)])
                    S.op("dve", lambda e: e.memset(S32, 0.0), W=["S32"])
                    S.op("dve", lambda e: e.memset(SB, 0.0), W=["SB"])
                    for pi in range(1, npair + 1):
                        gens = []
                        if pi < npair and KSTOPC > 2:
                            gens += [batch_gen(2 * pairs[pi], 0), batch_gen(2 * pairs[pi] + 1, 1)]
                        if KSTOPC > 5:
                            gens.append(rec_gen((pi - 1) * nper, order[(pi - 1) * nper:pi * nper]))
                        drive(gens)
                    S.op("dve", lambda e: e.memset(ZC[:, 1:2], 0.0), W=S1K + S2K + SET1K + ["ZCf"])
                ok_ = [("ot", c) for c in range(16)]
                SQ = YS.rearrange("p c n -> p (c n)")
                act(SQ, OT, AF.Square, ok_, [("ys", b_) for b_ in range(16 // NBC)])
                for t in range(4):
                    tsl = slice(t * 512, (t + 1) * 512)
                    mm(psb(t), [(ONB[:], SQ[:, tsl])], ["ONB"] + [("ys", b_) for b_ in range(16 // NBC)], pk(t))
                act(S3, psb(0, 4), AF.Sqrt, pk(0, 4), gk + [*S3K], scale=1.0 / 128, bias=1e-6)
                S.op("dve", lambda e: e.reciprocal(out=S3, in_=S3), gk + [*S3K], gk + [*S3K])
                for t in range(4):
                    tsl = slice(t * 512, (t + 1) * 512)
                    mm(psb(4 + t), [(win[:, k, 384:512], H[:, k, tsl]) for k in range(8)], [rk] + hkeys(t), pk(4 + t))
                WR.done()
                act(S2, psb(4, 4), AF.Silu, pk(4, 4), [*S2K])
                tt(S3, S3, OT, ALU.mult, gk + [*S3K] + ok_, gk + [*S3K])
                stt(QG, S3, pcol(("c_ng", l), 0), S2, ALU.mult, ALU.mult, gk + [*S3K, *S2K, "PAR"], ["QG"])
                slot, rk = WR.get(("c_out", l, hd))
                wo = slot[:, 0:1024]
                for t in range(4):
                    tsl = slice(t * 512, (t + 1) * 512)
                    for mo in range(8):
                        bank = mo
                        mm(psb(bank), [(wo[:, mo * 128:(mo + 1) * 128], QG[:, tsl])], [rk, "QG"], pk(bank))
                        tt(X[:, mo, tsl], psb(bank), X[:, mo, tsl], ALU.add, pk(bank) + [("x", mo, t)], [("x", mo, t)])
                WR.done()
            S.barrier()

        def program():
            WR.reset()
            setup()
            for s in range(nseq):
                for t in range(4):
                    tsl = slice(t * 512, (t + 1) * 512)
                    S.op("sp", lambda e, s=s, tsl=tsl: e.dma_start(
                        out=X[:, :, tsl], in_=xT[s].rearrange("(k p) n -> p k n", p=128)[:, :, tsl]),
                        W=[("x", k, t) for k in range(8)], dma=("xin", t))
                bcount = 0
                for l, (kind, j) in enumerate(cfg):
                    if kind != "N":
                        rmsnorm(("g_mix", l))
                    if kind == "N":
                        pass
                    elif kind == "A":
                        mixer_a(l)
                    elif kind == "B":
                        mixer_b(l, bcount)
                        bcount += 1
                    else:
                        mixer_c(l)
                    rmsnorm(("g_mlp", l))
                    mlp(l)
                rmsnorm(("g_fin",), out_x=True)
                for t in range(4):
                    tsl = slice(t * 512, (t + 1) * 512)
                    S.op("sp", lambda e, s=s, tsl=tsl: e.dma_start(
                        out=yT[s].rearrange("(k p) n -> p k n", p=128)[:, :, tsl], in_=X[:, :, tsl]),
                        R=[("x", k, t) for k in range(8)], W=[("y", s, t)], dma=("yout", t))
            S.final_wait("sp", [("y", s, t) for s in range(nseq) for t in range(4)])

        S.dry = True
        program()
        S.dry = False
        program()
        S.emit()
    return nc


DEFAULT_CFG = [("A", 0), ("B", 0), ("C", 0), ("A", 1)]


def run(inputs, cfg, ncore, nseq):
    wb = weight_blocks(inputs, cfg)
    woffs, wtot = layout(wb)
    wsarr = np.concatenate(list(wb.values()), axis=1).astype(np.float32)
    pb = param_blocks(inputs, cfg)
    poffs, ptot = layout(pb)
    pararr = np.concatenate(list(pb.values()), axis=1).astype(np.float32)
    parb = paramb_blocks(inputs, cfg)
    nc = build(nseq, cfg, woffs, wtot, poffs, ptot, parb.shape[1])
    x = np.asarray(inputs["x"], dtype=np.float32)
    in_maps = []
    for c in range(ncore):
        xs = np.ascontiguousarray(x[c * nseq:(c + 1) * nseq].transpose(0, 2, 1))
        in_maps.append({"xT": xs, "ws": wsarr, "par": pararr, "parb": parb})
    if os.environ.get("KTRACE") == "1":
        res = run_bass_kernel_spmd(nc, in_maps, core_ids=list(range(ncore)), trace=True)
        print("EXEC_NS", res.exec_time_ns)
    else:
        res = run_bass_kernel_spmd(nc, in_maps, core_ids=list(range(ncore)))
    outs = [np.asarray(r["yT"]).transpose(0, 2, 1) for r in res.results]
    return np.ascontiguousarray(np.concatenate(outs, axis=0)).astype(np.float32)


def kernel(**inputs):
    inputs = {k: np.asarray(v) for k, v in inputs.items()}
    return run(inputs, DEFAULT_CFG, NCORE, inputs["x"].shape[0] // NCORE)
```

```python
import contextlib
import numpy as np
import concourse.bass as bass
import concourse.mybir as mybir
from concourse.bass_utils import run_bass_kernel_spmd

F32 = mybir.dt.float32
BF16 = mybir.dt.bfloat16
AF = mybir.ActivationFunctionType
ALU = mybir.AluOpType

D = 1024
SEQ = 2048
NCORE = 8
SEM_LIMIT = 30000
INORDER_SAFE = ("pe", "sp")
NSLOT = 3
import os
KSTOP = int(os.environ.get('KSTOP', '99'))
KSTOPC = int(os.environ.get('KSTOPC', '99'))
KSUB = int(os.environ.get('KSUB', '99'))
TBF16 = os.environ.get('TBF16', '1') == '1'
ARENA_COLS = 20800


class Sched:
    ENGS = ("pe", "act", "dve", "pool", "sp")

    def __init__(self, nc, stack):
        self.nc = nc
        self.stack = stack
        self.dry = False
        self.E = {n: dict(ops=[], sem=None, cnt=0, seen={}, nsem=0, last=None) for n in self.ENGS}
        self.res = {}
        self.dsem = {}

    def _newsem(self, name):
        return self.stack.enter_context(self.nc.semaphore(name))

    def _tok(self, eng):
        E = self.E[eng]
        if E["sem"] is None or E["cnt"] >= SEM_LIMIT:
            E["sem"] = self._newsem("s_%s_%d" % (eng, E["nsem"]))
            E["nsem"] += 1
            E["cnt"] = 0
        E["cnt"] += 1
        t = (E["sem"], E["cnt"], eng, 1)
        E["last"] = t
        return t

    def _dtok(self, key):
        d = self.dsem.get(key)
        if d is None or d[1] >= SEM_LIMIT:
            n = 0 if d is None else d[2] + 1
            nm = "d_" + "".join(ch for ch in str(key) if ch.isalnum()) + "_%d" % n
            d = [self._newsem(nm), 0, n]
            self.dsem[key] = d
        d[1] += 16
        return (d[0], d[1], "dma", 16)

    def op(self, eng, fns, R=(), W=(), dma=None):
        if self.dry:
            return None
        if callable(fns):
            fns = [fns]
        psr = [r for r in R if isinstance(r, tuple) and r and r[0] == "ps"]
        if psr:
            R = [r for r in R if not (isinstance(r, tuple) and r and r[0] == "ps")]
            W = list(W) + psr
        E = self.E[eng]
        need = {}

        def add(tok):
            if tok is None:
                return
            sem, val, src, _ = tok
            if src == eng and eng in INORDER_SAFE:
                return
            k = id(sem)
            if k not in need or need[k][1] < val:
                need[k] = (sem, val)

        for r in R:
            st = self.res.get(r)
            if st:
                add(st[0])
        for w in W:
            st = self.res.get(w)
            if st:
                add(st[0])
                for t in st[1]:
                    add(t)
        waits = []
        for k, (sem, val) in need.items():
            if E["seen"].get(k, 0) >= val:
                continue
            E["seen"][k] = val
            waits.append((sem, val))
        tok = self._dtok(dma) if dma is not None else self._tok(eng)
        E["ops"].append((waits, fns, tok))
        for r in R:
            st = self.res.get(r)
            if st is None:
                self.res[r] = [None, [tok]]
            else:
                st[1].append(tok)
        for w in W:
            self.res[w] = [tok, []]
        return tok

    def barrier(self, engs=("pe", "act", "dve", "pool")):
        if self.dry:
            return
        lasts = [self.E[e]["last"] for e in engs if self.E[e]["last"] is not None]
        for e in engs:
            E = self.E[e]
            waits = []
            for (sem, val, src, _) in lasts:
                if src == e:
                    continue
                k = id(sem)
                if E["seen"].get(k, 0) >= val:
                    continue
                E["seen"][k] = val
                waits.append((sem, val))
            if waits:
                E["ops"].append((waits, [], None))
        E = self.E["sp"]
        waits = []
        for (sem, val, src, _) in lasts:
            k = id(sem)
            if E["seen"].get(k, 0) >= val:
                continue
            E["seen"][k] = val
            waits.append((sem, val))
        if waits:
            E["ops"].append((waits, [], None))

    def final_wait(self, eng, keys):
        if self.dry:
            return
        waits = []
        for k in keys:
            st = self.res.get(k)
            if not st:
                continue
            for t in [st[0]] + st[1]:
                if t is not None:
                    waits.append((t[0], t[1]))
        self.E[eng]["ops"].append((waits, [], None))

    def emit(self):
        nc = self.nc
        S = self

        def run(e, name):
            for waits, fns, tok in S.E[name]["ops"]:
                for sem, val in waits:
                    e.wait_ge(sem, val)
                inst = None
                for fn in fns:
                    inst = fn(e)
                if tok is not None and inst is not None:
                    inst.then_inc(tok[0], tok[3])

        with nc.Block() as block:
            @block.tensor
            def _(e):
                run(e, "pe")

            @block.scalar
            def _(e):
                run(e, "act")

            @block.vector
            def _(e):
                run(e, "dve")

            @block.gpsimd
            def _(e):
                run(e, "pool")

            @block.sync
            def _(e):
                run(e, "sp")


def _pk(w, k):
    n = w.shape[1]
    return np.ascontiguousarray(w.reshape(k, 128, n).transpose(1, 0, 2).reshape(128, k * n))


def weight_blocks(inp, cfg):
    B = {}
    for l, (kind, j) in enumerate(cfg):
        if kind == "A":
            w_in = inp["a_w_in"][j]
            gw = inp["a_gate_w"][j]
            for c in range(8):
                blk = np.concatenate([w_in[:, c * 128:(c + 1) * 128], w_in[:, 1024 + c * 128:1024 + (c + 1) * 128]], axis=1)
                B[("a_in", l, c)] = _pk(blk, 8)
                B[("a_gw", l, c)] = np.ascontiguousarray(gw[:, :, c].transpose(2, 0, 1, 3).reshape(128, 512))
            for half in range(2):
                B[("a_out", l, half)] = _pk(inp["a_w_out"][j][:, half * 512:(half + 1) * 512], 8)
        elif kind == "B":
            w_in = inp["b_w_in"][j]
            for blk in range(2):
                B[("b_in_u", l, blk)] = _pk(w_in[:, blk * 512:(blk + 1) * 512], 8)
            for blk in range(2):
                B[("b_in_v", l, blk)] = _pk(w_in[:, 1024 + blk * 512:1024 + (blk + 1) * 512], 8)
            B[("b_ws", l)] = np.ascontiguousarray(inp["b_w_s"][j].transpose(2, 0, 1).reshape(128, 1024))
            for half in range(2):
                B[("b_out", l, half)] = _pk(inp["b_w_out"][j][:, half * 512:(half + 1) * 512], 8)
        elif kind == "C":
            w_in = inp["c_w_in"][j]
            B[("c_ab", l)] = _pk(w_in[:, 4096:4128], 8)
            for hd in range(8):
                blk = np.concatenate([w_in[:, t * 1024 + hd * 128:t * 1024 + (hd + 1) * 128] for t in range(4)], axis=1)
                B[("c_in", l, hd)] = _pk(blk, 8)
                B[("c_out", l, hd)] = np.ascontiguousarray(inp["c_w_out"][j][hd * 128:(hd + 1) * 128, :])
        for fb in range(8):
            B[("mlp_up", l, fb)] = _pk(inp["mlp_w_up"][l][:, fb * 512:(fb + 1) * 512], 8)
            B[("mlp_dn", l, fb)] = _pk(inp["mlp_w_down"][l][fb * 512:(fb + 1) * 512, :], 4)
    return B


def _col(v):
    return np.ascontiguousarray(np.asarray(v).reshape(8, 128).T)


def param_blocks(inp, cfg):
    P = {}
    for l, (kind, j) in enumerate(cfg):
        P[("g_mix", l)] = _col(inp["norm_mix_g"][l])
        P[("g_mlp", l)] = _col(inp["norm_mlp_g"][l])
        if kind == "A":
            P[("a_cw", l)] = np.ascontiguousarray(inp["a_conv_w"][j].reshape(4, 8, 128).transpose(2, 1, 0).reshape(128, 32))
            P[("a_cb", l)] = _col(inp["a_conv_b"][j])
            P[("a_gb", l)] = np.ascontiguousarray(inp["a_gate_b"][j].transpose(3, 2, 0, 1).reshape(128, 32))
            P[("a_lam", l)] = np.ascontiguousarray(inp["a_lambda"][j].reshape(2, 8, 128).transpose(2, 1, 0).reshape(128, 16))
        elif kind == "C":
            P[("c_cw", l)] = np.ascontiguousarray(inp["c_conv_w"][j].reshape(4, 3, 8, 128).transpose(3, 2, 1, 0).reshape(128, 96))
            P[("c_alog", l)] = np.ascontiguousarray(np.broadcast_to(inp["c_a_log"][j].reshape(1, 16), (128, 16)))
            P[("c_dtb", l)] = np.ascontiguousarray(np.broadcast_to(inp["c_dt_bias"][j].reshape(1, 16), (128, 16)))
            P[("c_ng", l)] = np.ascontiguousarray(inp["c_norm_g"][j].reshape(128, 1))
    P[("g_fin",)] = _col(inp["norm_final_g"])
    return P


def paramb_blocks(inp, cfg):
    out = []
    for l, (kind, j) in enumerate(cfg):
        if kind == "B":
            row = np.concatenate([inp["b_ln_g"][j], inp["b_ln_b"][j], inp["b_b_s"][j].reshape(-1)])
            out.append(np.broadcast_to(row[None, :], (128, 3072)))
    if not out:
        out = [np.zeros((128, 3072), np.float32)]
    return np.ascontiguousarray(np.concatenate(out, axis=1)).astype(np.float32)


def layout(blocks):
    offs = {}
    o = 0
    for k, v in blocks.items():
        offs[k] = (o, v.shape[1])
        o += v.shape[1]
    return offs, o


def build(nseq, cfg, woffs, wtot, poffs, ptot, pbtot):
    nc = bass.Bass("TRN2", target_bir_lowering=False)
    xT = nc.dram_tensor("xT", [nseq, D, SEQ], F32, kind="ExternalInput").ap()
    ws = nc.dram_tensor("ws", [128, wtot], F32, kind="ExternalInput").ap()
    par = nc.dram_tensor("par", [128, ptot], F32, kind="ExternalInput").ap()
    parb = nc.dram_tensor("parb", [128, pbtot], F32, kind="ExternalInput").ap()
    yT = nc.dram_tensor("yT", [nseq, D, SEQ], F32, kind="ExternalOutput").ap()

    with contextlib.ExitStack() as st:
        S = Sched(nc, st)
        T = lambda name, shape, dt: st.enter_context(nc.sbuf_tensor(name, shape, dt))
        X = T("X", [128, 8, SEQ], F32)
        H = T("H", [128, 8, SEQ], BF16)
        RING = [T("ring%d" % i, [128, 4096], BF16) for i in range(NSLOT)]
        PAR = T("PAR", [128, ptot], F32)
        PC = T("PC", [128, 64], F32)
        ZC = T("ZC", [128, 2], F32)
        IDB = T("IDB", [128, 128], BF16)
        ONB = T("ONB", [128, 128], BF16)
        IDF = T("IDF", [128, 128], F32)
        MSK = T("MSK", [128, 4, 128], F32)
        LMK = T("LMK", [128, 7, 128], BF16)
        AR = T("AR", [128, ARENA_COLS], F32)
        PS = st.enter_context(nc.psum_tensor("PS", [128, 4096], F32))

        def psb(b, n=1):
            return PS[:, b * 512:(b + n) * 512]

        def pk(b, n=1):
            return [("ps", b + i) for i in range(n)]

        class Arena:
            def __init__(self):
                self.off = 0
                self.tag = 0

            def reset(self):
                self.off = 0
                self.tag += 1

            def f32(self, n):
                v = AR[:, self.off:self.off + n]
                self.off += n
                assert self.off <= ARENA_COLS, self.off
                return v

            def bf16(self, n):
                assert n % 2 == 0
                v = AR[:, self.off:self.off + n // 2].bitcast(BF16)
                self.off += n // 2
                assert self.off <= ARENA_COLS, self.off
                return v

        A = Arena()

        def act(out, in_, func, R, W, scale=1.0, bias=0.0):
            S.op("act", lambda e: e.activation(out=out, in_=in_, func=func, scale=scale, bias=bias), R, W)

        def tt(out, in0, in1, op, R, W, eng="dve"):
            S.op(eng, lambda e: e.tensor_tensor(out=out, in0=in0, in1=in1, op=op), R, W)

        def tsc(out, in0, s1, op0, R, W, s2=None, op1=None, eng="dve"):
            if op1 is None:
                S.op(eng, lambda e: e.tensor_scalar(out=out, in0=in0, scalar1=s1, scalar2=None, op0=op0), R, W)
            else:
                S.op(eng, lambda e: e.tensor_scalar(out=out, in0=in0, scalar1=s1, scalar2=s2, op0=op0, op1=op1), R, W)

        def stt(out, in0, scalar, in1, op0, op1, R, W):
            S.op("dve", lambda e: e.scalar_tensor_tensor(out=out, in0=in0, scalar=scalar, in1=in1, op0=op0, op1=op1), R, W)

        def mm(out, pairs, R, W):
            n = len(pairs)
            fns = []
            for i, (l, r) in enumerate(pairs):
                fns.append(lambda e, l=l, r=r, i=i: e.matmul(out, lhsT=l, rhs=r, start=(i == 0), stop=(i == n - 1)))
            S.op("pe", fns, R, W)

        def copy(out, in_, R, W, eng="dve"):
            S.op(eng, lambda e: e.tensor_copy(out=out, in_=in_), R, W)

        class WRing:
            def __init__(self):
                self.sched = []
                self.pos = 0
                self.issued = 0
                self.released = 0

            def reset(self):
                self.pos = 0
                self.issued = 0
                self.released = 0

            def pump(self):
                if S.dry:
                    return
                while self.issued < len(self.sched) and self.issued - NSLOT < self.released:
                    i = self.issued
                    off, n = woffs[self.sched[i]]
                    slot = i % NSLOT
                    dst = RING[slot][:, 0:n]
                    src = ws[:, off:off + n]
                    S.op("pool", lambda e, dst=dst, src=src: e.dma_start(out=dst, in_=src, max_dma_last_dim=8192),
                         W=[("ring", slot)], dma=("ring", slot))
                    self.issued += 1

            def get(self, name):
                if S.dry:
                    self.sched.append(name)
                    return RING[0], ("ring", 0)
                i = self.pos
                assert self.sched[i] == name, (self.sched[i], name)
                self.pos += 1
                self.pump()
                assert self.issued > i, "weight ring deadlock at %s" % (name,)
                return RING[i % NSLOT], ("ring", i % NSLOT)

            def done(self):
                if S.dry:
                    return
                self.released += 1
                self.pump()

        WR = WRing()

        def pcol(name, c, n=1):
            o, _ = poffs[name]
            return PAR[:, o + c:o + c + n]

        def setup():
            S.op("sp", lambda e: e.dma_start(out=PAR[:], in_=par[:, :]), W=["PAR"], dma="PAR")
            S.op("dve", lambda e: e.memset(ONB[:], 1.0), W=["ONB"])
            S.op("dve", lambda e: e.memset(ZC[:], 0.0), W=["ZC"])
            S.op("dve", lambda e: e.memset(IDF[:], 0.0), W=["IDF"])
            S.op("pool", lambda e: e.affine_select(out=IDF[:], in_=IDF[:], pattern=[[-1, 128]], base=0, channel_multiplier=1,
                                                   compare_op=ALU.not_equal, fill=1.0), R=["IDF"], W=["IDF"])
            copy(IDB[:], IDF[:], ["IDF"], ["IDB"])
            S.op("dve", lambda e: e.memset(MSK[:], 1.0), W=["MSK"])
            specs = [
                (0, 1, -1, 0, ALU.is_gt),
                (1, -1, 1, 0, ALU.is_gt),
                (2, -1, 1, 0, ALU.is_ge),
                (3, 1, -1, 0, ALU.is_ge),
            ]
            for (i, cm, stp, base, cmp_) in specs:
                S.op("pool", lambda e, i=i, cm=cm, stp=stp, base=base, cmp_=cmp_: e.affine_select(
                    out=MSK[:, i, :], in_=MSK[:, i, :], pattern=[[stp, 128]], base=base, channel_multiplier=cm,
                    compare_op=cmp_, fill=0.0), R=["MSK"], W=["MSK"])
            A.reset()
            Et = A.f32(128)
            BD = [A.f32(128), A.f32(128)]
            prev = IDF[:]
            prevk = "IDF"
            for li in range(7):
                s2 = 2 << li
                cur = BD[li % 2]
                curk = ("BD", li % 2)
                if s2 == 128:
                    S.op("dve", lambda e, cur=cur: e.memset(cur, 1.0), W=[curk])
                else:
                    nb = 128 // s2
                    S.op("dve", lambda e, nb=nb: e.memset(Et[0:nb, :], 1.0), W=["Et"])
                    S.op("pool", lambda e, nb=nb, s2=s2: e.affine_select(out=Et[0:nb, :], in_=Et[0:nb, :], pattern=[[1, 128]], base=0,
                                                                       channel_multiplier=-s2, compare_op=ALU.is_ge, fill=0.0),
                         R=["Et"], W=["Et"])
                    S.op("pool", lambda e, nb=nb, s2=s2: e.affine_select(out=Et[0:nb, :], in_=Et[0:nb, :], pattern=[[-1, 128]], base=s2 - 1,
                                                                       channel_multiplier=s2, compare_op=ALU.is_ge, fill=0.0),
                         R=["Et"], W=["Et"])
                    mm(PS[:, 0:128], [(Et[0:nb, :], Et[0:nb, :])], ["Et"], pk(0))
                    copy(cur, PS[:, 0:128], pk(0), [curk])
                tt(LMK[:, li, :], cur, prev, ALU.subtract, [curk, prevk], ["LMK"])
                prev, prevk = cur, curk
            S.barrier()

        def rmsnorm(gname, out_x=False):
            A.reset()
            SQ = [A.bf16(8 * 512).rearrange("p (k n) -> p k n", k=8) for _ in range(2)]
            RS = [A.f32(512) for _ in range(2)]
            for t in range(4):
                b = t % 2
                tsl = slice(t * 512, (t + 1) * 512)
                act(SQ[b], X[:, :, tsl], AF.Square, [("x", k, t) for k in range(8)], [("sq", b, k) for k in range(8)])
                bank = 2 * b
                mm(psb(bank), [(ONB[:], SQ[b][:, k, :]) for k in range(8)],
                   ["ONB"] + [("sq", b, k) for k in range(8)], pk(bank))
                act(RS[b], psb(bank), AF.Sqrt, pk(bank), [("rs", b)], scale=1.0 / D, bias=1e-6)
                S.op("dve", lambda e, b=b: e.reciprocal(out=RS[b], in_=RS[b]), [("rs", b)], [("rs", b)])
                for k in range(8):
                    if out_x:
                        stt(X[:, k, tsl], X[:, k, tsl], pcol(gname, k), RS[b], ALU.mult, ALU.mult,
                            [("x", k, t), ("rs", b), "PAR"], [("x", k, t)])
                    else:
                        stt(H[:, k, tsl], X[:, k, tsl], pcol(gname, k), RS[b], ALU.mult, ALU.mult,
                            [("x", k, t), ("rs", b), "PAR"], [("h", k, t)])
            S.barrier()

        def hkeys(t):
            return [("h", k, t) for k in range(8)]

        def outproj(name_fn, YT, ykeys):
            for half in range(2):
                slot, rk = WR.get(name_fn(half))
                wo = slot[:, 0:4096].rearrange("p (c n) -> p c n", c=8)
                for t in range(4):
                    tsl = slice(t * 512, (t + 1) * 512)
                    for mo in range(4):
                        bank = (t * 4 + mo) % 8
                        mm(psb(bank), [(wo[:, cc, mo * 128:(mo + 1) * 128], YT[:, cc, tsl]) for cc in range(8)],
                           [rk] + ykeys, pk(bank))
                        ko = half * 4 + mo
                        tt(X[:, ko, tsl], psb(bank), X[:, ko, tsl], ALU.add, pk(bank) + [("x", ko, t)], [("x", ko, t)])
                WR.done()

        def mlp(l):
            A.reset()
            H1 = [A.bf16(4 * 512).rearrange("p (c n) -> p c n", c=4) for _ in range(2)]
            SQ1 = [A.bf16(512) for _ in range(4)]
            steps = [(fb, t) for fb in range(8) for t in range(4)]
            held = {}

            def up(i):
                fb, t = steps[i]
                if t == 0:
                    held[("u", fb)] = WR.get(("mlp_up", l, fb))
                slot, rk = held[("u", fb)]
                wu = slot[:, 0:4096].rearrange("p (k n) -> p k n", k=8)
                tsl = slice(t * 512, (t + 1) * 512)
                hb = i % 2
                for mi in range(4):
                    bank = mi
                    mm(psb(bank), [(wu[:, k, mi * 128:(mi + 1) * 128], H[:, k, tsl]) for k in range(8)],
                       [rk] + hkeys(t), pk(bank))
                    act(SQ1[mi], psb(bank), AF.Square, pk(bank), [("sq1", mi)])
                    stt(H1[hb][:, mi, :], psb(bank), 0.0, SQ1[mi], ALU.is_gt, ALU.mult,
                        pk(bank) + [("sq1", mi)], [("h1", hb, mi)])
                if t == 3:
                    WR.done()

            def down(i):
                fb, t = steps[i]
                if t == 0:
                    held[("d", fb)] = WR.get(("mlp_dn", l, fb))
                slot, rk = held[("d", fb)]
                wd = slot[:, 0:4096].rearrange("p (c n) -> p c n", c=4)
                tsl = slice(t * 512, (t + 1) * 512)
                hb = i % 2
                for mo in range(8):
                    bank = 4 + (mo % 4)
                    mm(psb(bank), [(wd[:, c, mo * 128:(mo + 1) * 128], H1[hb][:, c, :]) for c in range(4)],
                       [rk] + [("h1", hb, c) for c in range(4)], pk(bank))
                    tt(X[:, mo, tsl], psb(bank), X[:, mo, tsl], ALU.add, pk(bank) + [("x", mo, t)], [("x", mo, t)])
                if t == 3:
                    WR.done()

            n = len(steps)
            for i in range(n + 1):
                if i < n:
                    up(i)
                if i >= 1:
                    down(i - 1)
            S.barrier()

        def mixer_a(l):
            A.reset()
            YT = A.bf16(8 * SEQ).rearrange("p (c n) -> p c n", c=8)
            XP = A.f32(SEQ + 4)
            XR = A.f32(SEQ)
            XB = A.bf16(SEQ)
            T1 = A.f32(SEQ)
            T2 = A.f32(SEQ)
            T3 = A.f32(SEQ)
            lo, _ = poffs[("a_lam", l)]
            act(PC[:, 32:48], PAR[:, lo:lo + 16], AF.Exp, ["PAR"], ["PC"], scale=-1.0)
            act(PC[:, 32:48], PC[:, 32:48], AF.Ln, ["PC"], ["PC"], bias=1.0)
            tsc(PC[:, 0:16], PC[:, 32:48], -8.0, ALU.mult, ["PC"], ["PC"])
            tsc(PC[:, 16:32], PC[:, 32:48], -16.0, ALU.mult, ["PC"], ["PC"])
            S.op("dve", lambda e: e.memset(XP[:, 0:2], 0.0), W=[("XP", 0)])
            S.op("dve", lambda e: e.memset(XP[:, SEQ + 2:SEQ + 4], 0.0), W=[("XP", 1)])
            HW_ = SEQ // 2

            def a_gen(c, hf, wv, rk, gw, rkg):
                hs = slice(hf * HW_, (hf + 1) * HW_)
                xs = slice(2 + hf * HW_, 2 + (hf + 1) * HW_)
                t0 = 2 * hf
                kXP, kXR, kXB, kT1, kT2, kT3 = [(nm, hf) for nm in ("XP", "XR", "XB", "T1", "T2", "T3")]
                XPK = [("XP", 0), ("XP", 1)]
                for t in (t0, t0 + 1):
                    tsl = slice(t * 512, (t + 1) * 512)
                    mm(psb(t), [(wv[:, k, 0:128], H[:, k, tsl]) for k in range(8)], [rk] + hkeys(t), pk(t))
                    mm(psb(4 + t), [(wv[:, k, 128:256], H[:, k, tsl]) for k in range(8)], [rk] + hkeys(t), pk(4 + t))
                yield
                act(YT[:, c, hs], psb(t0, 2), AF.Gelu_apprx_tanh, pk(t0, 2), [("yt", c)])
                act(XP[:, xs], psb(4 + t0, 2), AF.Copy, pk(4 + t0, 2), [kXP])
                yield
                tsc(XR[:, hs], XP[:, hf * HW_:hf * HW_ + HW_], pcol(("a_cw", l), c * 4 + 0), ALU.mult, XPK + ["PAR"], [kXR],
                    s2=pcol(("a_cb", l), c), op1=ALU.add)
                for tap in range(1, 4):
                    stt(XR[:, hs], XP[:, hf * HW_ + tap:hf * HW_ + tap + HW_], pcol(("a_cw", l), c * 4 + tap), XR[:, hs],
                        ALU.mult, ALU.add, XPK + [kXR, "PAR"], [kXR])
                yield
                act(XB[:, hs], XR[:, hs], AF.Copy, [kXR], [kXB])
                yield
                for dr in range(2):
                    for g in range(2):
                        for t in (t0, t0 + 1):
                            tsl = slice(t * 512, (t + 1) * 512)
                            mm(psb(g * 4 + t), [(gw[:, dr * 2 + g, :], XB[:, tsl])], [rkg, kXB], pk(g * 4 + t))
                    yield
                    gbo = c * 4 + dr * 2
                    act(T1[:, hs], psb(t0, 2), AF.Sigmoid, pk(t0, 2) + ["PAR"], [kT1], bias=pcol(("a_gb", l), gbo))
                    act(T2[:, hs], psb(4 + t0, 2), AF.Sigmoid, pk(4 + t0, 2) + ["PAR"], [kT2], bias=pcol(("a_gb", l), gbo + 1))
                    ci = c * 2 + dr
                    act(T3[:, hs], T1[:, hs], AF.Exp, [kT1, "PC"], [kT3], scale=PC[:, ci:ci + 1])
                    act(T1[:, hs], T1[:, hs], AF.Exp, [kT1, "PC"], [kT1], scale=PC[:, 16 + ci:16 + ci + 1])
                    act(T1[:, hs], T1[:, hs], AF.Sqrt, [kT1], [kT1], scale=-1.0, bias=1.0)
                    yield
                    tt(T2[:, hs], T2[:, hs], T1[:, hs], ALU.mult, [kT1, kT2], [kT2], eng="pool")
                    tt(T2[:, hs], T2[:, hs], XR[:, hs], ALU.mult, [kT2, kXR], [kT2], eng="pool")
                    yield
                    if dr == 0:
                        if hf == 0:
                            S.op("dve", lambda e: e.tensor_tensor_scan(out=XP[:, xs], data0=T3[:, hs], data1=T2[:, hs], initial=0.0,
                                                                       op0=ALU.mult, op1=ALU.add), [kT3, kT2], [kXP])
                        else:
                            S.op("dve", lambda e: e.tensor_tensor_scan(out=XP[:, xs], data0=T3[:, hs], data1=T2[:, hs],
                                                                       initial=XP[:, 2 + HW_ - 1:2 + HW_],
                                                                       op0=ALU.mult, op1=ALU.add), [kT3, kT2, ("XP", 0)], [kXP])
                    else:
                        if hf == 1:
                            S.op("dve", lambda e: e.tensor_tensor_scan(out=T1[:, hs][:, ::-1], data0=T3[:, hs][:, ::-1],
                                                                       data1=T2[:, hs][:, ::-1], initial=0.0,
                                                                       op0=ALU.mult, op1=ALU.add), [kT3, kT2], [kT1])
                        else:
                            yield
                            S.op("dve", lambda e: e.tensor_tensor_scan(out=T1[:, hs][:, ::-1], data0=T3[:, hs][:, ::-1],
                                                                       data1=T2[:, hs][:, ::-1], initial=T1[:, HW_:HW_ + 1],
                                                                       op0=ALU.mult, op1=ALU.add), [kT3, kT2, ("T1", 1)], [kT1])
                    yield
                tt(T1[:, hs], T1[:, hs], XP[:, xs], ALU.add, [kT1, kXP], [kT1])
                tt(YT[:, c, hs], T1[:, hs], YT[:, c, hs], ALU.mult, [kT1, ("yt", c)], [("yt", c)])

            for c in range(8):
                slot, rk = WR.get(("a_in", l, c))
                wv = slot[:, 0:2048].rearrange("p (k n) -> p k n", k=8)
                slotg, rkg = WR.get(("a_gw", l, c))
                gw = slotg[:, 0:512].rearrange("p (q o) -> p q o", q=4)
                gens = [a_gen(c, 0, wv, rk, gw, rkg), a_gen(c, 1, wv, rk, gw, rkg)]
                while gens:
                    for g_ in list(gens):
                        try:
                            next(g_)
                        except StopIteration:
                            gens.remove(g_)
                WR.done()
                WR.done()
            outproj(lambda half: ("a_out", l, half), YT, [("yt", c) for c in range(8)])
            S.barrier()

        def mixer_b(l, bidx):
            A.reset()
            UT = A.bf16(8 * SEQ).rearrange("p (c n) -> p c n", c=8)
            LGB = A.f32(3072)
            VT = [A.f32(1024) for _ in range(2)]
            VN = [A.bf16(1024) for _ in range(2)]
            TB = [A.f32(1024) for _ in range(2)]
            STT = [A.f32(12) for _ in range(2)]
            MV = [A.f32(4) for _ in range(2)]
            S.op("sp", lambda e: e.dma_start(out=LGB, in_=parb[:, bidx * 3072:(bidx + 1) * 3072]), W=["LGB"], dma="LGB")
            for blk in range(2):
                slot, rk = WR.get(("b_in_u", l, blk))
                wb = slot[:, 0:4096].rearrange("p (k n) -> p k n", k=8)
                for t in range(4):
                    tsl = slice(t * 512, (t + 1) * 512)
                    b0 = (t % 2) * 4
                    for mi in range(4):
                        mm(psb(b0 + mi), [(wb[:, k, mi * 128:(mi + 1) * 128], H[:, k, tsl]) for k in range(8)],
                           [rk] + hkeys(t), pk(b0 + mi))
                    act(UT[:, blk * 4:(blk + 1) * 4, tsl], PS[:, b0 * 512:(b0 + 4) * 512].rearrange("p (c n) -> p c n", c=4),
                        AF.Gelu_apprx_tanh, pk(b0, 4), [("ut", blk * 4 + mi, t) for mi in range(4)])
                WR.done()
            slot0, rk0 = WR.get(("b_in_v", l, 0))
            slot1, rk1 = WR.get(("b_in_v", l, 1))
            slot2, rk2 = WR.get(("b_ws", l))
            wvv = [slot0[:, 0:4096].rearrange("p (k n) -> p k n", k=8), slot1[:, 0:4096].rearrange("p (k n) -> p k n", k=8)]
            rkv = [rk0, rk1]
            wst = slot2[:, 0:1024].rearrange("p (g n) -> p g n", g=8)
            for tt_ in range(16):
                b = tt_ % 2
                tok = slice(tt_ * 128, (tt_ + 1) * 128)
                t = tt_ // 4
                pb = b * 4
                for blk in range(2):
                    mm(psb(pb + blk), [(H[:, k, tok], wvv[blk][:, k, :]) for k in range(8)], [rkv[blk]] + hkeys(t), pk(pb + blk))
                act(VT[b], psb(pb, 2), AF.Gelu_apprx_tanh, pk(pb, 2), [("vt", b)])
                S.op("dve", lambda e, b=b: e.bn_stats(out=STT[b][:, 0:6], in_=VT[b][:, 0:512]), [("vt", b)], [("st", b)])
                S.op("dve", lambda e, b=b: e.bn_stats(out=STT[b][:, 6:12], in_=VT[b][:, 512:1024]), [("vt", b)], [("st2", b)])
                S.op("dve", lambda e, b=b: e.bn_aggr(out=MV[b][:, 0:2], in_=STT[b][:, 0:12]), [("st", b), ("st2", b)], [("mv", b)])
                act(MV[b][:, 2:3], MV[b][:, 1:2], AF.Sqrt, [("mv", b)], [("mv2", b)], bias=1e-5)
                S.op("dve", lambda e, b=b: e.reciprocal(out=MV[b][:, 2:3], in_=MV[b][:, 2:3]), [("mv2", b)], [("mv2", b)])
                tsc(VT[b], VT[b], MV[b][:, 0:1], ALU.subtract, [("vt", b), ("mv", b), ("mv2", b)], [("vt", b)],
                    s2=MV[b][:, 2:3], op1=ALU.mult)
                tt(VT[b], VT[b], LGB[:, 0:1024], ALU.mult, [("vt", b), "LGB"], [("vt", b)], eng="pool")
                tt(VN[b], VT[b], LGB[:, 1024:2048], ALU.add, [("vt", b), "LGB"], [("vn", b)], eng="pool")
                for g in range(8):
                    bank = pb + 2 + g // 4
                    o = PS[:, bank * 512 + (g % 4) * 128: bank * 512 + (g % 4 + 1) * 128]
                    mm(o, [(VN[b][:, g * 128:(g + 1) * 128], wst[:, g, :])], [("vn", b), rk2], [("ps", bank)])
                tt(TB[b], psb(pb + 2, 2), LGB[:, 2048:3072], ALU.add, pk(pb + 2, 2) + ["LGB"], [("tb", b)])
                uk = [("ut", g, t) for g in range(8)]
                tt(UT[:, :, tok], TB[b][:].rearrange("p (g n) -> p g n", g=8), UT[:, :, tok], ALU.mult,
                   [("tb", b)] + uk, uk, eng="pool")
            WR.done()
            WR.done()
            WR.done()
            outproj(lambda half: ("b_out", l, half), UT, [("ut", g, t) for g in range(8) for t in range(4)])
            S.barrier()

        def mixer_c(l):
            A.reset()
            QSC = 128.0 ** -0.5
            S1K = [("S1", 0), ("S1", 1)]
            S2K = [("S2", 0), ("S2", 1)]
            S3K = [("gcb", c) for c in range(16)]
            BTOK = A.f32(512)
            BT3 = BTOK.rearrange("p (c n) -> p c n", c=16)
            WAB = A.bf16(256).rearrange("p (k n) -> p k n", k=8)
            SM = A.f32(128)
            GCT, EGT, NEGT, KDT, NBT, EGLT = [SM[:, i * 16:(i + 1) * 16] for i in range(6)]
            QT = A.bf16(SEQ)
            KT = A.bf16(SEQ)
            KTOK = A.bf16(SEQ).rearrange("p (c n) -> p c n", c=16)
            VTOK = A.bf16(SEQ).rearrange("p (c n) -> p c n", c=16)
            OT = A.f32(SEQ)
            S1 = A.f32(SEQ + 4)
            S2 = A.f32(SEQ)
            S3 = A.f32(SEQ)
            QG = A.bf16(SEQ)
            YS = A.bf16(SEQ).rearrange("p (c n) -> p c n", c=16)
            QKD = A.bf16(SEQ).rearrange("p (c n) -> p c n", c=16)
            NBC = 4
            An = A.f32(NBC * 128).rearrange("p (c n) -> p c n", c=NBC)
            At = A.f32(NBC * 128).rearrange("p (c n) -> p c n", c=NBC)
            Tt = A.f32(NBC * 128).rearrange("p (c n) -> p c n", c=NBC)
            Yy = A.f32(NBC * 128).rearrange("p (c n) -> p c n", c=NBC)
            D1 = A.f32(NBC * 128)
            D2 = A.f32(NBC * 128)
            M1s = An
            WREP = D1.bitcast(BF16).rearrange("p (k n) -> p k n", k=8)
            RH = [A.bf16(128) for _ in range(2)]
            VN = [A.bf16(128) for _ in range(2)]
            VN2 = [A.bf16(128) for _ in range(2)]
            S32 = A.f32(128)
            SB = A.bf16(128)
            RAW, ACC, SIL = S1, S2, S3
            D13 = D1.rearrange("p (c n) -> p c n", c=NBC)
            D23 = D2.rearrange("p (c n) -> p c n", c=NBC)

            def bank3(b):
                return PS[:, b * 512:b * 512 + NBC * 128].rearrange("p (c n) -> p c n", c=NBC)

            ao, _ = poffs[("c_alog", l)]
            act(PC[:, 48:64], PAR[:, ao:ao + 16], AF.Exp, ["PAR"], ["PCc"])
            tsc(PC[:, 48:64], PC[:, 48:64], -1.0, ALU.mult, ["PCc"], ["PCc"])
            S.op("dve", lambda e: e.memset(S1[:, 0:2], 0.0), W=[*S1K])
            S.op("dve", lambda e: e.memset(S1[:, SEQ + 2:SEQ + 4], 0.0), W=[*S1K])
            slot, rk = WR.get(("c_ab", l))
            wab = slot[:, 0:256].rearrange("p (k n) -> p k n", k=8)
            copy(WAB, wab, [rk], ["WAB"])
            for c in range(16):
                tok = slice(c * 128, (c + 1) * 128)
                mm(PS[:, c * 32:(c + 1) * 32], [(H[:, k, tok], wab[:, k, :]) for k in range(8)], [rk] + hkeys(c // 4), pk(0))
            WR.done()
            act(BTOK, psb(0), AF.Sigmoid, pk(0), ["BTOK"])

            for hd in range(8):
                slot, rk = WR.get(("c_in", l, hd))
                win = slot[:, 0:4096].rearrange("p (k n) -> p k n", k=8)
                cwo, _ = poffs[("c_cw", l)]
                SQ = YS.rearrange("p c n -> p (c n)")
                VTf = QG

                def proj_gen(typ, hf):
                    pb = (typ % 2) * 4
                    t0 = 2 * hf
                    HW_ = SEQ // 2
                    hs = slice(hf * HW_, (hf + 1) * HW_)
                    ys_k = [("ys", b_) for b_ in range(hf * (8 // NBC), (hf + 1) * (8 // NBC))]
                    ot_k = [("ot", c) for c in range(8 * hf, 8 * hf + 8)]
                    s3_k = [("gcb", c) for c in range(8 * hf, 8 * hf + 8)]
                    for t in (t0, t0 + 1):
                        tsl = slice(t * 512, (t + 1) * 512)
                        mm(psb(pb + t), [(win[:, k, typ * 128:(typ + 1) * 128], H[:, k, tsl]) for k in range(8)],
                           [rk] + hkeys(t), pk(pb + t))
                    yield
                    act(RAW[:, 2 + hf * HW_:2 + (hf + 1) * HW_], psb(pb + t0, 2), AF.Copy, pk(pb + t0, 2), [("S1", hf)])
                    yield
                    co = cwo + hd * 12 + typ * 4
                    tsc(ACC[:, hs], RAW[:, hf * HW_:hf * HW_ + HW_], PAR[:, co:co + 1], ALU.mult, S1K + ["PAR"], [("S2", hf)])
                    for tap in range(1, 4):
                        stt(ACC[:, hs], RAW[:, hf * HW_ + tap:hf * HW_ + tap + HW_], PAR[:, co + tap:co + tap + 1], ACC[:, hs],
                            ALU.mult, ALU.add, S1K + [("S2", hf), "PAR"], [("S2", hf)])
                    yield
                    p3d = PS[:, hf * HW_:(hf + 1) * HW_].rearrange("p (c n) -> p c n", c=8)
                    if typ == 2:
                        act(VTf[:, hs], ACC[:, hs], AF.Silu, [("S2", hf)], ["QG"])
                        yield
                        for c in range(8 * hf, 8 * hf + 8):
                            mm(PS[:, c * 128:(c + 1) * 128], [(VTf[:, c * 128:(c + 1) * 128], IDB[:])], ["QG", "IDB"], pk(c // 4))
                        yield
                        copy(VTOK[:, 8 * hf:8 * hf + 8, :], p3d, pk(2 * hf, 2), ["VTOK"])
                    else:
                        act(SIL[:, hs], ACC[:, hs], AF.Silu, [("S2", hf)], s3_k)
                        yield
                        act(SQ[:, hs], SIL[:, hs], AF.Square, s3_k, ys_k)
                        yield
                        pb2 = 4 - pb
                        for t in (t0, t0 + 1):
                            tsl = slice(t * 512, (t + 1) * 512)
                            mm(psb(pb2 + t), [(ONB[:], SQ[:, tsl])], ["ONB"] + ys_k, pk(pb2 + t))
                        yield
                        act(OT[:, hs], psb(pb2 + t0, 2), AF.Sqrt, pk(pb2 + t0, 2), ot_k, bias=1e-6)
                        yield
                        S.op("dve", lambda e: e.reciprocal(out=OT[:, hs], in_=OT[:, hs]), ot_k, ot_k)
                        yield
                        if typ == 0:
                            stt(QT[:, hs], SIL[:, hs], QSC, OT[:, hs], ALU.mult, ALU.mult, s3_k + ot_k, ["QT"])
                        else:
                            tt(KT[:, hs], SIL[:, hs], OT[:, hs], ALU.mult, s3_k + ot_k, ["KT"])
                            yield
                            for c in range(8 * hf, 8 * hf + 8):
                                mm(PS[:, c * 128:(c + 1) * 128], [(KT[:, c * 128:(c + 1) * 128], IDB[:])], ["KT", "IDB"], pk(c // 4))
                            yield
                            act(KTOK[:, 8 * hf:8 * hf + 8, :], p3d, AF.Copy, pk(2 * hf, 2), ["KTOK"])

                for typ in range(3):
                    gens = [proj_gen(typ, 0), proj_gen(typ, 1)]
                    while gens:
                        for g in list(gens):
                            try:
                                next(g)
                            except StopIteration:
                                gens.remove(g)

                gk = [("gcb", c) for c in range(16)]
                for dr in range(2):
                    if KSTOPC <= 1:
                        continue
                    n = dr * 8 + hd
                    bcol = BT3[:, :, 16 + n]
                    copy(WREP, WAB[:, :, n:n + 1].to_broadcast([128, 8, 128]), ["WAB"], ["D1"])
                    GCB, EG = S3, S2[:, 0:SEQ // 2].bitcast(BF16)
                    gk = [("gcb", c) for c in range(16)]
                    lastc = 127 if dr == 0 else 0
                    GL = GCB[:, lastc::128]
                    EGL = EGLT

                    def dir_gen(hf):
                        HW_ = SEQ // 2
                        hs = slice(hf * HW_, (hf + 1) * HW_)
                        cr = range(8 * hf, 8 * hf + 8)
                        cols = slice(8 * hf, 8 * hf + 8)
                        t0 = 2 * hf
                        g1 = S1[:, 2 + hf * HW_:2 + (hf + 1) * HW_]
                        kS1 = ("S1", hf)
                        gkh = [("gcb", c) for c in cr]
                        for t in (t0, t0 + 1):
                            tsl = slice(t * 512, (t + 1) * 512)
                            mm(psb(t), [(WREP[:, k, :], H[:, k, tsl]) for k in range(8)], ["D1"] + hkeys(t), pk(t))
                        yield
                        act(g1, psb(t0, 2), AF.Exp, pk(t0, 2) + ["PAR"], [kS1], bias=pcol(("c_dtb", l), n))
                        act(g1, g1, AF.Ln, [kS1], [kS1], bias=1.0)
                        yield
                        tsc(g1, g1, PC[:, 48 + n:48 + n + 1], ALU.mult, [kS1, "PCc"], [kS1])
                        for c in cr:
                            sl = slice(c * 128, (c + 1) * 128)
                            gl = S1[:, 2 + c * 128:2 + (c + 1) * 128]
                            if dr == 0:
                                S.op("dve", lambda e, sl=sl, gl=gl: e.tensor_tensor_scan(out=GCB[:, sl], data0=ONB[:], data1=gl, initial=0.0,
                                                                                        op0=ALU.mult, op1=ALU.add), [kS1, "ONB"], [("gcb", c)])
                            else:
                                S.op("dve", lambda e, sl=sl, gl=gl: e.tensor_tensor_scan(out=GCB[:, sl][:, ::-1], data0=ONB[:], data1=gl[:, ::-1],
                                                                                        initial=0.0, op0=ALU.mult, op1=ALU.add),
                                     [kS1, "ONB"], [("gcb", c)])
                        yield
                        act(EG[:, hs], GCB[:, hs], AF.Exp, gkh, [("S2", 0)])
                        TMPh = g1.rearrange("p (c n) -> p c n", c=8)
                        tt(TMPh, GCB[:, hs].rearrange("p (c n) -> p c n", c=8), IDF[:].unsqueeze(1).to_broadcast([128, 8, 128]), ALU.mult,
                           gkh + ["IDF"], [kS1])
                        S.op("dve", lambda e: e.tensor_reduce(out=GCT[:, cols], in_=TMPh, op=ALU.add, axis=mybir.AxisListType.X),
                             [kS1], [("GCT", hf)])
                        yield
                        act(EGT[:, cols], GCT[:, cols], AF.Exp, [("GCT", hf)], [("EGT", hf)])
                        act(EGLT[:, cols], GL[:, cols], AF.Exp, gkh, [("EGLT", hf)])
                        tt(QG[:, hs], QT[:, hs], EG[:, hs], ALU.mult, ["QT", ("S2", 0)], ["QG"])
                        yield
                        tsc(NEGT[:, cols], EGT[:, cols], -1.0, ALU.mult, [("EGT", hf)], [("NEGT", hf)])
                        tt(KDT[:, cols], GL[:, cols], GCT[:, cols], ALU.subtract, gkh + [("GCT", hf)], [("KDT", hf)])
                        yield
                        act(KDT[:, cols], KDT[:, cols], AF.Exp, [("KDT", hf)], [("KDT", hf)])

                    gens = [dir_gen(0), dir_gen(1)]
                    while gens:
                        for g in list(gens):
                            try:
                                next(g)
                            except StopIteration:
                                gens.remove(g)
                    SET1K = ["An1", "At1", "T1", "Y1", "D1x1", "D2x1"]
                    S.op("dve", lambda e: e.memset(ZC[:, 1:2], 0.0), W=S1K + S2K + SET1K + ["ZCf"])
                    idb3 = IDF[:].unsqueeze(1).to_broadcast([128, NBC, 128])

                    def lm3(li):
                        return LMK[:, li, :].unsqueeze(1).to_broadcast([128, NBC, 128])

                    def v3(ap):
                        return ap.rearrange("p (c n) -> p c n", c=NBC)

                    def batch_gen(bq, sid):
                        W_ = NBC * 128
                        if sid == 0:
                            base = [An.rearrange("p c n -> p (c n)"), At.rearrange("p c n -> p (c n)"), Tt.rearrange("p c n -> p (c n)"),
                                    Yy.rearrange("p c n -> p (c n)"), D1, D2]
                            b0 = 4
                        else:
                            base = [S1[:, 2 + i * W_:2 + (i + 1) * W_] for i in range(4)] + \
                                   [S2[:, SEQ // 2:SEQ // 2 + W_], S2[:, SEQ // 2 + W_:SEQ // 2 + 2 * W_]]
                            b0 = 0
                        An_, At_, Tt_, Yy_ = [v3(b) for b in base[0:4]]
                        D1_, D2_ = base[4], base[5]
                        if TBF16:
                            lowv = [v3(b.bitcast(BF16)[:, 0:W_]) for b in base]
                        else:
                            lowv = [v3(b) for b in base]
                        M1L, _, TtL, YyL, D1L, D2L = lowv
                        kA, kAt, kT_, kY, kD1, kD2 = ["%s%d" % (nm, sid) for nm in ("An", "At", "T", "Y", "D1x", "D2x")]
                        if sid == 0:
                            kD1, kD2 = "D1", "D2"
                        D13_, D23_ = v3(D1_), v3(D2_)
                        M1_ = An_

                        def pq(bi, ci):
                            return PS[:, (b0 + bi) * 512 + ci * 128:(b0 + bi) * 512 + (ci + 1) * 128]

                        def pq3(bi):
                            return v3(PS[:, (b0 + bi) * 512:(b0 + bi) * 512 + W_])
                        cs = [bq * NBC + ci for ci in range(NBC)]
                        for ci, c in enumerate(cs):
                            sl = slice(c * 128, (c + 1) * 128)
                            mm(pq(0, ci), [(KT[:, sl], KT[:, sl])], ["KT"], pk(b0))
                            mm(pq(1, ci), [(KT[:, sl], QT[:, sl])], ["KT", "QT"], pk(b0 + 1))
                        bsl = slice(bq * NBC * 128, (bq + 1) * NBC * 128)
                        gkb = [("gcb", c) for c in cs]
                        gct3 = GCT[:, bq * NBC:(bq + 1) * NBC].unsqueeze(2).to_broadcast([128, NBC, 128])
                        beta3 = bcol[:, bq * NBC:(bq + 1) * NBC].unsqueeze(2).to_broadcast([128, NBC, 128])
                        tt(D13_, v3(GCB[:, bsl]), gct3, ALU.subtract, gkb + [("GCT", cs[0] // 8)], [kD1])
                        tsc(D23_, D13_, 0.0, ALU.min, [kD1], [kD2])
                        tsc(D13_, D13_, 0.0, ALU.max, [kD1], [kD1])
                        yield
                        act(D1_, D1_, AF.Exp, [kD1], [kD1], scale=-1.0)
                        act(D2_, D2_, AF.Exp, [kD2], [kD2])
                        yield
                        tt(D13_, D13_, MSK[:, dr, :].unsqueeze(1).to_broadcast([128, NBC, 128]), ALU.mult, [kD1, "MSK"], [kD1])
                        tt(D23_, D23_, MSK[:, 2 + dr, :].unsqueeze(1).to_broadcast([128, NBC, 128]), ALU.mult, [kD2, "MSK"], [kD2])
                        tt(D13_, D13_, beta3, ALU.mult, [kD1, "BTOK"], [kD1])
                        tt(An_, pq3(0), D13_, ALU.mult, pk(b0) + [kD1], [kA])
                        tt(QKD[:, bq * NBC:(bq + 1) * NBC, :], pq3(1), D23_, ALU.mult, pk(b0 + 1) + [kD2], [("qkd", bq)])
                        yield
                        for ci in range(NBC):
                            mm(pq(2, ci), [(An_[:, ci, :], IDF[:])], [kA, "IDF"], pk(b0 + 2))
                        yield
                        act(At_, pq3(2), AF.Copy, pk(b0 + 2), [kAt])
                        tt(D13_, An_, lm3(0), ALU.mult, [kA, "LMK"], [kD1], eng="pool")
                        yield
                        tt(D23_, At_, lm3(0), ALU.mult, [kAt, "LMK"], [kD2], eng="pool")
                        tt(TtL, idb3, D13_, ALU.subtract, ["IDF", kD1], [kT_])
                        yield
                        tt(YyL, idb3, D23_, ALU.subtract, ["IDF", kD2], [kY])
                        for li in range(1, 7):
                            last = li == 6
                            dbuf, dkey = (D1L, kD1) if li % 2 else (D2L, kD2)
                            tt(dbuf, At_, lm3(li), ALU.mult, [kAt, "LMK"], [dkey], eng="pool")
                            yield
                            for ci in range(NBC):
                                mm(pq(0, ci), [(dbuf[:, ci, :], TtL[:, ci, :])], [dkey, kT_], pk(b0))
                            yield
                            act(M1L, pq3(0), AF.Copy, pk(b0), [kA])
                            yield
                            if not last:
                                for ci in range(NBC):
                                    mm(pq(1, ci), [(YyL[:, ci, :], M1L[:, ci, :])], [kY, kA], pk(b0 + 1))
                            for ci in range(NBC):
                                mm(pq(2, ci), [(M1L[:, ci, :], YyL[:, ci, :])], [kY, kA], pk(b0 + 2))
                            yield
                            if not last:
                                tt(TtL, TtL, pq3(1), ALU.subtract, [kT_] + pk(b0 + 1), [kT_])
                            tt(YyL, YyL, pq3(2), ALU.subtract, [kY] + pk(b0 + 2), [kY])
                        yield
                        tt(YS[:, bq * NBC:(bq + 1) * NBC, :], YyL, beta3, ALU.mult, [kY, "BTOK"], [("ys", bq)])

                    order = list(range(16)) if dr == 0 else list(range(15, -1, -1))

                    def rec_gen(si0, chunks):
                        for k_, c in enumerate(chunks):
                            si = si0 + k_
                            par_ = si % 2
                            sl = slice(c * 128, (c + 1) * 128)
                            bq = c // NBC
                            o_ = par_ * 256
                            p1 = PS[:, 3 * 512 + o_:3 * 512 + o_ + 128]
                            p3 = PS[:, 3 * 512 + o_ + 128:3 * 512 + o_ + 256]
                            p2 = PS[:, 7 * 512 + o_:7 * 512 + o_ + 128]
                            p4 = PS[:, 7 * 512 + o_ + 128:7 * 512 + o_ + 256]
                            mm(p1, [(KT[:, sl], SB)], ["KT", "SB"], pk(3))
                            yield
                            stt(RH[par_], p1, NEGT[:, c:c + 1], VTOK[:, c, :], ALU.mult, ALU.add, pk(3) + [("NEGT", c // 8), "VTOK"], [("rh", par_)])
                            yield
                            mm(p2, [(YS[:, c, :], RH[par_])], [("ys", bq), ("rh", par_)], pk(7))
                            yield
                            act(VN[par_], p2, AF.Copy, pk(7), [("vn", par_)])
                            tsc(VN2[par_], p2, KDT[:, c:c + 1], ALU.mult, pk(7) + [("KDT", c // 8)], [("vn2", par_)])
                            yield
                            mm(p3, [(SB, QG[:, sl]), (VN[par_], QKD[:, c, :])], ["SB", "QG", ("vn", par_), ("qkd", bq)], pk(3))
                            mm(p4, [(KTOK[:, c, :], VN2[par_])], ["KTOK", ("vn2", par_)], pk(7))
                            yield
                            stt(SB, S32, EGL[:, c:c + 1], p4, ALU.mult, ALU.add, ["S32", ("EGLT", c // 8)] + pk(7), ["SB"])
                            stt(S32, S32, EGL[:, c:c + 1], p4, ALU.mult, ALU.add, ["S32", ("EGLT", c // 8)] + pk(7), ["S32"])
                            if dr == 0:
                                act(OT[:, sl], p3, AF.Copy, pk(3), [("ot", c)])
                            else:
                                tt(OT[:, sl], p3, OT[:, sl], ALU.add, pk(3) + [("ot", c)], [("ot", c)])
                            yield

                    def drive(gens):
                        while gens:
                            for g in list(gens):
                                try:
                                    next(g)
                                except StopIteration:
                                    gens.remove(g)

                    npair = 16 // NBC // 2
                    pairs = list(range(npair)) if dr == 0 else list(range(npair - 1, -1, -1))
                    nper = 16 // npair
                    if KSTOPC > 2:
                        drive([batch_gen(2 * pairs[0], 0), batch_gen(2 * pairs[0] + 1, 1)])
                    S.op("dve", lambda e: e.memset(S32, 0.0), W=["S32"])
                    S.op("dve", lambda e: e.memset(SB, 0.0), W=["SB"])
                    for pi in range(1, npair + 1):
                        gens = []
                        if pi < npair and KSTOPC > 2:
                            gens += [batch_gen(2 * pairs[pi], 0), batch_gen(2 * pairs[pi] + 1, 1)]
                        if KSTOPC > 5:
                            gens.append(rec_gen((pi - 1) * nper, order[(pi - 1) * nper:pi * nper]))
                        drive(gens)
                    S.op("dve", lambda e: e.memset(ZC[:, 1:2], 0.0), W=S1K + S2K + SET1K + ["ZCf"])
                ok_ = [("ot", c) for c in range(16)]
                SQ = YS.rearrange("p c n -> p (c n)")
                act(SQ, OT, AF.Square, ok_, [("ys", b_) for b_ in range(16 // NBC)])
                for t in range(4):
                    tsl = slice(t * 512, (t + 1) * 512)
                    mm(psb(t), [(ONB[:], SQ[:, tsl])], ["ONB"] + [("ys", b_) for b_ in range(16 // NBC)], pk(t))
                act(S3, psb(0, 4), AF.Sqrt, pk(0, 4), gk + [*S3K], scale=1.0 / 128, bias=1e-6)
                S.op("dve", lambda e: e.reciprocal(out=S3, in_=S3), gk + [*S3K], gk + [*S3K])
                for t in range(4):
                    tsl = slice(t * 512, (t + 1) * 512)
                    mm(psb(4 + t), [(win[:, k, 384:512], H[:, k, tsl]) for k in range(8)], [rk] + hkeys(t), pk(4 + t))
                WR.done()
                act(S2, psb(4, 4), AF.Silu, pk(4, 4), [*S2K])
                tt(S3, S3, OT, ALU.mult, gk + [*S3K] + ok_, gk + [*S3K])
                stt(QG, S3, pcol(("c_ng", l), 0), S2, ALU.mult, ALU.mult, gk + [*S3K, *S2K, "PAR"], ["QG"])
                slot, rk = WR.get(("c_out", l, hd))
                wo = slot[:, 0:1024]
                for t in range(4):
                    tsl = slice(t * 512, (t + 1) * 512)
                    for mo in range(8):
                        bank = mo
                        mm(psb(bank), [(wo[:, mo * 128:(mo + 1) * 128], QG[:, tsl])], [rk, "QG"], pk(bank))
                        tt(X[:, mo, tsl], psb(bank), X[:, mo, tsl], ALU.add, pk(bank) + [("x", mo, t)], [("x", mo, t)])
                WR.done()
            S.barrier()

        def program():
            WR.reset()
            setup()
            for s in range(nseq):
                for t in range(4):
                    tsl = slice(t * 512, (t + 1) * 512)
                    S.op("sp", lambda e, s=s, tsl=tsl: e.dma_start(
                        out=X[:, :, tsl], in_=xT[s].rearrange("(k p) n -> p k n", p=128)[:, :, tsl]),
                        W=[("x", k, t) for k in range(8)], dma=("xin", t))
                bcount = 0
                for l, (kind, j) in enumerate(cfg):
                    if kind != "N":
                        rmsnorm(("g_mix", l))
                    if kind == "N":
                        pass
                    elif kind == "A":
                        mixer_a(l)
                    elif kind == "B":
                        mixer_b(l, bcount)
                        bcount += 1
                    else:
                        mixer_c(l)
                    rmsnorm(("g_mlp", l))
                    mlp(l)
                rmsnorm(("g_fin",), out_x=True)
                for t in range(4):
                    tsl = slice(t * 512, (t + 1) * 512)
                    S.op("sp", lambda e, s=s, tsl=tsl: e.dma_start(
                        out=yT[s].rearrange("(k p) n -> p k n", p=128)[:, :, tsl], in_=X[:, :, tsl]),
                        R=[("x", k, t) for k in range(8)], W=[("y", s, t)], dma=("yout", t))
            S.final_wait("sp", [("y", s, t) for s in range(nseq) for t in range(4)])

        S.dry = True
        program()
        S.dry = False
        program()
        S.emit()
    return nc


DEFAULT_CFG = [("A", 0), ("B", 0), ("C", 0), ("A", 1)]


def run(inputs, cfg, ncore, nseq):
    wb = weight_blocks(inputs, cfg)
    woffs, wtot = layout(wb)
    wsarr = np.concatenate(list(wb.values()), axis=1).astype(np.float32)
    pb = param_blocks(inputs, cfg)
    poffs, ptot = layout(pb)
    pararr = np.concatenate(list(pb.values()), axis=1).astype(np.float32)
    parb = paramb_blocks(inputs, cfg)
    nc = build(nseq, cfg, woffs, wtot, poffs, ptot, parb.shape[1])
    x = np.asarray(inputs["x"], dtype=np.float32)
    in_maps = []
    for c in range(ncore):
        xs = np.ascontiguousarray(x[c * nseq:(c + 1) * nseq].transpose(0, 2, 1))
        in_maps.append({"xT": xs, "ws": wsarr, "par": pararr, "parb": parb})
    if os.environ.get("KTRACE") == "1":
        res = run_bass_kernel_spmd(nc, in_maps, core_ids=list(range(ncore)), trace=True)
        print("EXEC_NS", res.exec_time_ns)
    else:
        res = run_bass_kernel_spmd(nc, in_maps, core_ids=list(range(ncore)))
    outs = [np.asarray(r["yT"]).transpose(0, 2, 1) for r in res.results]
    return np.ascontiguousarray(np.concatenate(outs, axis=0)).astype(np.float32)


def kernel(**inputs):
    inputs = {k: np.asarray(v) for k, v in inputs.items()}
    return run(inputs, DEFAULT_CFG, NCORE, inputs["x"].shape[0] // NCORE)
```

```python
import contextlib
import numpy as np
import concourse.bass as bass
import concourse.mybir as mybir
from concourse.bass_utils import run_bass_kernel_spmd

F32 = mybir.dt.float32
BF16 = mybir.dt.bfloat16
AF = mybir.ActivationFunctionType
ALU = mybir.AluOpType

D = 1024
SEQ = 2048
NCORE = 8
SEM_LIMIT = 30000
INORDER_SAFE = ("pe", "sp")
NSLOT = 3
import os
KSTOP = int(os.environ.get('KSTOP', '99'))
KSTOPC = int(os.environ.get('KSTOPC', '99'))
KSUB = int(os.environ.get('KSUB', '99'))
TBF16 = os.environ.get('TBF16', '1') == '1'
ARENA_COLS = 20800


class Sched:
    ENGS = ("pe", "act", "dve", "pool", "sp")

    def __init__(self, nc, stack):
        self.nc = nc
        self.stack = stack
        self.dry = False
        self.E = {n: dict(ops=[], sem=None, cnt=0, seen={}, nsem=0, last=None) for n in self.ENGS}
        self.res = {}
        self.dsem = {}

    def _newsem(self, name):
        return self.stack.enter_context(self.nc.semaphore(name))

    def _tok(self, eng):
        E = self.E[eng]
        if E["sem"] is None or E["cnt"] >= SEM_LIMIT:
            E["sem"] = self._newsem("s_%s_%d" % (eng, E["nsem"]))
            E["nsem"] += 1
            E["cnt"] = 0
        E["cnt"] += 1
        t = (E["sem"], E["cnt"], eng, 1)
        E["last"] = t
        return t

    def _dtok(self, key):
        d = self.dsem.get(key)
        if d is None or d[1] >= SEM_LIMIT:
            n = 0 if d is None else d[2] + 1
            nm = "d_" + "".join(ch for ch in str(key) if ch.isalnum()) + "_%d" % n
            d = [self._newsem(nm), 0, n]
            self.dsem[key] = d
        d[1] += 16
        return (d[0], d[1], "dma", 16)

    def op(self, eng, fns, R=(), W=(), dma=None):
        if self.dry:
            return None
        if callable(fns):
            fns = [fns]
        psr = [r for r in R if isinstance(r, tuple) and r and r[0] == "ps"]
        if psr:
            R = [r for r in R if not (isinstance(r, tuple) and r and r[0] == "ps")]
            W = list(W) + psr
        E = self.E[eng]
        need = {}

        def add(tok):
            if tok is None:
                return
            sem, val, src, _ = tok
            if src == eng and eng in INORDER_SAFE:
                return
            k = id(sem)
            if k not in need or need[k][1] < val:
                need[k] = (sem, val)

        for r in R:
            st = self.res.get(r)
            if st:
                add(st[0])
        for w in W:
            st = self.res.get(w)
            if st:
                add(st[0])
                for t in st[1]:
                    add(t)
        waits = []
        for k, (sem, val) in need.items():
            if E["seen"].get(k, 0) >= val:
                continue
            E["seen"][k] = val
            waits.append((sem, val))
        tok = self._dtok(dma) if dma is not None else self._tok(eng)
        E["ops"].append((waits, fns, tok))
        for r in R:
            st = self.res.get(r)
            if st is None:
                self.res[r] = [None, [tok]]
            else:
                st[1].append(tok)
        for w in W:
            self.res[w] = [tok, []]
        return tok

    def barrier(self, engs=("pe", "act", "dve", "pool")):
        if self.dry:
            return
        lasts = [self.E[e]["last"] for e in engs if self.E[e]["last"] is not None]
        for e in engs:
            E = self.E[e]
            waits = []
            for (sem, val, src, _) in lasts:
                if src == e:
                    continue
                k = id(sem)
                if E["seen"].get(k, 0) >= val:
                    continue
                E["seen"][k] = val
                waits.append((sem, val))
            if waits:
                E["ops"].append((waits, [], None))
        E = self.E["sp"]
        waits = []
        for (sem, val, src, _) in lasts:
            k = id(sem)
            if E["seen"].get(k, 0) >= val:
                continue
            E["seen"][k] = val
            waits.append((sem, val))
        if waits:
            E["ops"].append((waits, [], None))

    def final_wait(self, eng, keys):
        if self.dry:
            return
        waits = []
        for k in keys:
            st = self.res.get(k)
            if not st:
                continue
            for t in [st[0]] + st[1]:
                if t is not None:
                    waits.append((t[0], t[1]))
        self.E[eng]["ops"].append((waits, [], None))

    def emit(self):
        nc = self.nc
        S = self

        def run(e, name):
            for waits, fns, tok in S.E[name]["ops"]:
                for sem, val in waits:
                    e.wait_ge(sem, val)
                inst = None
                for fn in fns:
                    inst = fn(e)
                if tok is not None and inst is not None:
                    inst.then_inc(tok[0], tok[3])

        with nc.Block() as block:
            @block.tensor
            def _(e):
                run(e, "pe")

            @block.scalar
            def _(e):
                run(e, "act")

            @block.vector
            def _(e):
                run(e, "dve")

            @block.gpsimd
            def _(e):
                run(e, "pool")

            @block.sync
            def _(e):
                run(e, "sp")


def _pk(w, k):
    n = w.shape[1]
    return np.ascontiguousarray(w.reshape(k, 128, n).transpose(1, 0, 2).reshape(128, k * n))


def weight_blocks(inp, cfg):
    B = {}
    for l, (kind, j) in enumerate(cfg):
        if kind == "A":
            w_in = inp["a_w_in"][j]
            gw = inp["a_gate_w"][j]
            for c in range(8):
                blk = np.concatenate([w_in[:, c * 128:(c + 1) * 128], w_in[:, 1024 + c * 128:1024 + (c + 1) * 128]], axis=1)
                B[("a_in", l, c)] = _pk(blk, 8)
                B[("a_gw", l, c)] = np.ascontiguousarray(gw[:, :, c].transpose(2, 0, 1, 3).reshape(128, 512))
            for half in range(2):
                B[("a_out", l, half)] = _pk(inp["a_w_out"][j][:, half * 512:(half + 1) * 512], 8)
        elif kind == "B":
            w_in = inp["b_w_in"][j]
            for blk in range(2):
                B[("b_in_u", l, blk)] = _pk(w_in[:, blk * 512:(blk + 1) * 512], 8)
            for blk in range(2):
                B[("b_in_v", l, blk)] = _pk(w_in[:, 1024 + blk * 512:1024 + (blk + 1) * 512], 8)
            B[("b_ws", l)] = np.ascontiguousarray(inp["b_w_s"][j].transpose(2, 0, 1).reshape(128, 1024))
            for half in range(2):
                B[("b_out", l, half)] = _pk(inp["b_w_out"][j][:, half * 512:(half + 1) * 512], 8)
        elif kind == "C":
            w_in = inp["c_w_in"][j]
            B[("c_ab", l)] = _pk(w_in[:, 4096:4128], 8)
            for hd in range(8):
                blk = np.concatenate([w_in[:, t * 1024 + hd * 128:t * 1024 + (hd + 1) * 128] for t in range(4)], axis=1)
                B[("c_in", l, hd)] = _pk(blk, 8)
                B[("c_out", l, hd)] = np.ascontiguousarray(inp["c_w_out"][j][hd * 128:(hd + 1) * 128, :])
        for fb in range(8):
            B[("mlp_up", l, fb)] = _pk(inp["mlp_w_up"][l][:, fb * 512:(fb + 1) * 512], 8)
            B[("mlp_dn", l, fb)] = _pk(inp["mlp_w_down"][l][fb * 512:(fb + 1) * 512, :], 4)
    return B


def _col(v):
    return np.ascontiguousarray(np.asarray(v).reshape(8, 128).T)


def param_blocks(inp, cfg):
    P = {}
    for l, (kind, j) in enumerate(cfg):
        P[("g_mix", l)] = _col(inp["norm_mix_g"][l])
        P[("g_mlp", l)] = _col(inp["norm_mlp_g"][l])
        if kind == "A":
            P[("a_cw", l)] = np.ascontiguousarray(inp["a_conv_w"][j].reshape(4, 8, 128).transpose(2, 1, 0).reshape(128, 32))
            P[("a_cb", l)] = _col(inp["a_conv_b"][j])
            P[("a_gb", l)] = np.ascontiguousarray(inp["a_gate_b"][j].transpose(3, 2, 0, 1).reshape(128, 32))
            P[("a_lam", l)] = np.ascontiguousarray(inp["a_lambda"][j].reshape(2, 8, 128).transpose(2, 1, 0).reshape(128, 16))
        elif kind == "C":
            P[("c_cw", l)] = np.ascontiguousarray(inp["c_conv_w"][j].reshape(4, 3, 8, 128).transpose(3, 2, 1, 0).reshape(128, 96))
            P[("c_alog", l)] = np.ascontiguousarray(np.broadcast_to(inp["c_a_log"][j].reshape(1, 16), (128, 16)))
            P[("c_dtb", l)] = np.ascontiguousarray(np.broadcast_to(inp["c_dt_bias"][j].reshape(1, 16), (128, 16)))
            P[("c_ng", l)] = np.ascontiguousarray(inp["c_norm_g"][j].reshape(128, 1))
    P[("g_fin",)] = _col(inp["norm_final_g"])
    return P


def paramb_blocks(inp, cfg):
    out = []
    for l, (kind, j) in enumerate(cfg):
        if kind == "B":
            row = np.concatenate([inp["b_ln_g"][j], inp["b_ln_b"][j], inp["b_b_s"][j].reshape(-1)])
            out.append(np.broadcast_to(row[None, :], (128, 3072)))
    if not out:
        out = [np.zeros((128, 3072), np.float32)]
    return np.ascontiguousarray(np.concatenate(out, axis=1)).astype(np.float32)


def layout(blocks):
    offs = {}
    o = 0
    for k, v in blocks.items():
        offs[k] = (o, v.shape[1])
        o += v.shape[1]
    return offs, o


def build(nseq, cfg, woffs, wtot, poffs, ptot, pbtot):
    nc = bass.Bass("TRN2", target_bir_lowering=False)
    xT = nc.dram_tensor("xT", [nseq, D, SEQ], F32, kind="ExternalInput").ap()
    ws = nc.dram_tensor("ws", [128, wtot], F32, kind="ExternalInput").ap()
    par = nc.dram_tensor("par", [128, ptot], F32, kind="ExternalInput").ap()
    parb = nc.dram_tensor("parb", [128, pbtot], F32, kind="ExternalInput").ap()
    yT = nc.dram_tensor("yT", [nseq, D, SEQ], F32, kind="ExternalOutput").ap()

    with contextlib.ExitStack() as st:
        S = Sched(nc, st)
        T = lambda name, shape, dt: st.enter_context(nc.sbuf_tensor(name, shape, dt))
        X = T("X", [128, 8, SEQ], F32)
        H = T("H", [128, 8, SEQ], BF16)
        RING = [T("ring%d" % i, [128, 4096], BF16) for i in range(NSLOT)]
        PAR = T("PAR", [128, ptot], F32)
        PC = T("PC", [128, 64], F32)
        ZC = T("ZC", [128, 2], F32)
        IDB = T("IDB", [128, 128], BF16)
        ONB = T("ONB", [128, 128], BF16)
        IDF = T("IDF", [128, 128], F32)
        MSK = T("MSK", [128, 4, 128], F32)
        LMK = T("LMK", [128, 7, 128], BF16)
        AR = T("AR", [128, ARENA_COLS], F32)
        PS = st.enter_context(nc.psum_tensor("PS", [128, 4096], F32))

        def psb(b, n=1):
            return PS[:, b * 512:(b + n) * 512]

        def pk(b, n=1):
            return [("ps", b + i) for i in range(n)]

        class Arena:
            def __init__(self):
                self.off = 0
                self.tag = 0

            def reset(self):
                self.off = 0
                self.tag += 1

            def f32(self, n):
                v = AR[:, self.off:self.off + n]
                self.off += n
                assert self.off <= ARENA_COLS, self.off
                return v

            def bf16(self, n):
                assert n % 2 == 0
                v = AR[:, self.off:self.off + n // 2].bitcast(BF16)
                self.off += n // 2
                assert self.off <= ARENA_COLS, self.off
                return v

        A = Arena()

        def act(out, in_, func, R, W, scale=1.0, bias=0.0):
            S.op("act", lambda e: e.activation(out=out, in_=in_, func=func, scale=scale, bias=bias), R, W)

        def tt(out, in0, in1, op, R, W, eng="dve"):
            S.op(eng, lambda e: e.tensor_tensor(out=out, in0=in0, in1=in1, op=op), R, W)

        def tsc(out, in0, s1, op0, R, W, s2=None, op1=None, eng="dve"):
            if op1 is None:
                S.op(eng, lambda e: e.tensor_scalar(out=out, in0=in0, scalar1=s1, scalar2=None, op0=op0), R, W)
            else:
                S.op(eng, lambda e: e.tensor_scalar(out=out, in0=in0, scalar1=s1, scalar2=s2, op0=op0, op1=op1), R, W)

        def stt(out, in0, scalar, in1, op0, op1, R, W):
            S.op("dve", lambda e: e.scalar_tensor_tensor(out=out, in0=in0, scalar=scalar, in1=in1, op0=op0, op1=op1), R, W)

        def mm(out, pairs, R, W):
            n = len(pairs)
            fns = []
            for i, (l, r) in enumerate(pairs):
                fns.append(lambda e, l=l, r=r, i=i: e.matmul(out, lhsT=l, rhs=r, start=(i == 0), stop=(i == n - 1)))
            S.op("pe", fns, R, W)

        def copy(out, in_, R, W, eng="dve"):
            S.op(eng, lambda e: e.tensor_copy(out=out, in_=in_), R, W)

        class WRing:
            def __init__(self):
                self.sched = []
                self.pos = 0
                self.issued = 0
                self.released = 0

            def reset(self):
                self.pos = 0
                self.issued = 0
                self.released = 0

            def pump(self):
                if S.dry:
                    return
                while self.issued < len(self.sched) and self.issued - NSLOT < self.released:
                    i = self.issued
                    off, n = woffs[self.sched[i]]
                    slot = i % NSLOT
                    dst = RING[slot][:, 0:n]
                    src = ws[:, off:off + n]
                    S.op("pool", lambda e, dst=dst, src=src: e.dma_start(out=dst, in_=src, max_dma_last_dim=8192),
                         W=[("ring", slot)], dma=("ring", slot))
                    self.issued += 1

            def get(self, name):
                if S.dry:
                    self.sched.append(name)
                    return RING[0], ("ring", 0)
                i = self.pos
                assert self.sched[i] == name, (self.sched[i], name)
                self.pos += 1
                self.pump()
                assert self.issued > i, "weight ring deadlock at %s" % (name,)
                return RING[i % NSLOT], ("ring", i % NSLOT)

            def done(self):
                if S.dry:
                    return
                self.released += 1
                self.pump()

        WR = WRing()

        def pcol(name, c, n=1):
            o, _ = poffs[name]
            return PAR[:, o + c:o + c + n]

        def setup():
            S.op("sp", lambda e: e.dma_start(out=PAR[:], in_=par[:, :]), W=["PAR"], dma="PAR")
            S.op("dve", lambda e: e.memset(ONB[:], 1.0), W=["ONB"])
            S.op("dve", lambda e: e.memset(ZC[:], 0.0), W=["ZC"])
            S.op("dve", lambda e: e.memset(IDF[:], 0.0), W=["IDF"])
            S.op("pool", lambda e: e.affine_select(out=IDF[:], in_=IDF[:], pattern=[[-1, 128]], base=0, channel_multiplier=1,
                                                   compare_op=ALU.not_equal, fill=1.0), R=["IDF"], W=["IDF"])
            copy(IDB[:], IDF[:], ["IDF"], ["IDB"])
            S.op("dve", lambda e: e.memset(MSK[:], 1.0), W=["MSK"])
            specs = [
                (0, 1, -1, 0, ALU.is_gt),
                (1, -1, 1, 0, ALU.is_gt),
                (2, -1, 1, 0, ALU.is_ge),
                (3, 1, -1, 0, ALU.is_ge),
            ]
            for (i, cm, stp, base, cmp_) in specs:
                S.op("pool", lambda e, i=i, cm=cm, stp=stp, base=base, cmp_=cmp_: e.affine_select(
                    out=MSK[:, i, :], in_=MSK[:, i, :], pattern=[[stp, 128]], base=base, channel_multiplier=cm,
                    compare_op=cmp_, fill=0.0), R=["MSK"], W=["MSK"])
            A.reset()
            Et = A.f32(128)
            BD = [A.f32(128), A.f32(128)]
            prev = IDF[:]
            prevk = "IDF"
            for li in range(7):
                s2 = 2 << li
                cur = BD[li % 2]
                curk = ("BD", li % 2)
                if s2 == 128:
                    S.op("dve", lambda e, cur=cur: e.memset(cur, 1.0), W=[curk])
                else:
                    nb = 128 // s2
                    S.op("dve", lambda e, nb=nb: e.memset(Et[0:nb, :], 1.0), W=["Et"])
                    S.op("pool", lambda e, nb=nb, s2=s2: e.affine_select(out=Et[0:nb, :], in_=Et[0:nb, :], pattern=[[1, 128]], base=0,
                                                                       channel_multiplier=-s2, compare_op=ALU.is_ge, fill=0.0),
                         R=["Et"], W=["Et"])
                    S.op("pool", lambda e, nb=nb, s2=s2: e.affine_select(out=Et[0:nb, :], in_=Et[0:nb, :], pattern=[[-1, 128]], base=s2 - 1,
                                                                       channel_multiplier=s2, compare_op=ALU.is_ge, fill=0.0),
                         R=["Et"], W=["Et"])
                    mm(PS[:, 0:128], [(Et[0:nb, :], Et[0:nb, :])], ["Et"], pk(0))
                    copy(cur, PS[:, 0:128], pk(0), [curk])
                tt(LMK[:, li, :], cur, prev, ALU.subtract, [curk, prevk], ["LMK"])
                prev, prevk = cur, curk
            S.barrier()

        def rmsnorm(gname, out_x=False):
            A.reset()
            SQ = [A.bf16(8 * 512).rearrange("p (k n) -> p k n", k=8) for _ in range(2)]
            RS = [A.f32(512) for _ in range(2)]
            for t in range(4):
                b = t % 2
                tsl = slice(t * 512, (t + 1) * 512)
                act(SQ[b], X[:, :, tsl], AF.Square, [("x", k, t) for k in range(8)], [("sq", b, k) for k in range(8)])
                bank = 2 * b
                mm(psb(bank), [(ONB[:], SQ[b][:, k, :]) for k in range(8)],
                   ["ONB"] + [("sq", b, k) for k in range(8)], pk(bank))
                act(RS[b], psb(bank), AF.Sqrt, pk(bank), [("rs", b)], scale=1.0 / D, bias=1e-6)
                S.op("dve", lambda e, b=b: e.reciprocal(out=RS[b], in_=RS[b]), [("rs", b)], [("rs", b)])
                for k in range(8):
                    if out_x:
                        stt(X[:, k, tsl], X[:, k, tsl], pcol(gname, k), RS[b], ALU.mult, ALU.mult,
                            [("x", k, t), ("rs", b), "PAR"], [("x", k, t)])
                    else:
                        stt(H[:, k, tsl], X[:, k, tsl], pcol(gname, k), RS[b], ALU.mult, ALU.mult,
                            [("x", k, t), ("rs", b), "PAR"], [("h", k, t)])
            S.barrier()

        def hkeys(t):
            return [("h", k, t) for k in range(8)]

        def outproj(name_fn, YT, ykeys):
            for half in range(2):
                slot, rk = WR.get(name_fn(half))
                wo = slot[:, 0:4096].rearrange("p (c n) -> p c n", c=8)
                for t in range(4):
                    tsl = slice(t * 512, (t + 1) * 512)
                    for mo in range(4):
                        bank = (t * 4 + mo) % 8
                        mm(psb(bank), [(wo[:, cc, mo * 128:(mo + 1) * 128], YT[:, cc, tsl]) for cc in range(8)],
                           [rk] + ykeys, pk(bank))
                        ko = half * 4 + mo
                        tt(X[:, ko, tsl], psb(bank), X[:, ko, tsl], ALU.add, pk(bank) + [("x", ko, t)], [("x", ko, t)])
                WR.done()

        def mlp(l):
            A.reset()
            H1 = [A.bf16(4 * 512).rearrange("p (c n) -> p c n", c=4) for _ in range(2)]
            SQ1 = [A.bf16(512) for _ in range(4)]
            steps = [(fb, t) for fb in range(8) for t in range(4)]
            held = {}

            def up(i):
                fb, t = steps[i]
                if t == 0:
                    held[("u", fb)] = WR.get(("mlp_up", l, fb))
                slot, rk = held[("u", fb)]
                wu = slot[:, 0:4096].rearrange("p (k n) -> p k n", k=8)
                tsl = slice(t * 512, (t + 1) * 512)
                hb = i % 2
                for mi in range(4):
                    bank = mi
                    mm(psb(bank), [(wu[:, k, mi * 128:(mi + 1) * 128], H[:, k, tsl]) for k in range(8)],
                       [rk] + hkeys(t), pk(bank))
                    act(SQ1[mi], psb(bank), AF.Square, pk(bank), [("sq1", mi)])
                    stt(H1[hb][:, mi, :], psb(bank), 0.0, SQ1[mi], ALU.is_gt, ALU.mult,
                        pk(bank) + [("sq1", mi)], [("h1", hb, mi)])
                if t == 3:
                    WR.done()

            def down(i):
                fb, t = steps[i]
                if t == 0:
                    held[("d", fb)] = WR.get(("mlp_dn", l, fb))
                slot, rk = held[("d", fb)]
                wd = slot[:, 0:4096].rearrange("p (c n) -> p c n", c=4)
                tsl = slice(t * 512, (t + 1) * 512)
                hb = i % 2
                for mo in range(8):
                    bank = 4 + (mo % 4)
                    mm(psb(bank), [(wd[:, c, mo * 128:(mo + 1) * 128], H1[hb][:, c, :]) for c in range(4)],
                       [rk] + [("h1", hb, c) for c in range(4)], pk(bank))
                    tt(X[:, mo, tsl], psb(bank), X[:, mo, tsl], ALU.add, pk(bank) + [("x", mo, t)], [("x", mo, t)])
                if t == 3:
                    WR.done()

            n = len(steps)
            for i in range(n + 1):
                if i < n:
                    up(i)
                if i >= 1:
                    down(i - 1)
            S.barrier()

        def mixer_a(l):
            A.reset()
            YT = A.bf16(8 * SEQ).rearrange("p (c n) -> p c n", c=8)
            XP = A.f32(SEQ + 4)
            XR = A.f32(SEQ)
            XB = A.bf16(SEQ)
            T1 = A.f32(SEQ)
            T2 = A.f32(SEQ)
            T3 = A.f32(SEQ)
            lo, _ = poffs[("a_lam", l)]
            act(PC[:, 32:48], PAR[:, lo:lo + 16], AF.Exp, ["PAR"], ["PC"], scale=-1.0)
            act(PC[:, 32:48], PC[:, 32:48], AF.Ln, ["PC"], ["PC"], bias=1.0)
            tsc(PC[:, 0:16], PC[:, 32:48], -8.0, ALU.mult, ["PC"], ["PC"])
            tsc(PC[:, 16:32], PC[:, 32:48], -16.0, ALU.mult, ["PC"], ["PC"])
            S.op("dve", lambda e: e.memset(XP[:, 0:2], 0.0), W=[("XP", 0)])
            S.op("dve", lambda e: e.memset(XP[:, SEQ + 2:SEQ + 4], 0.0), W=[("XP", 1)])
            HW_ = SEQ // 2

            def a_gen(c, hf, wv, rk, gw, rkg):
                hs = slice(hf * HW_, (hf + 1) * HW_)
                xs = slice(2 + hf * HW_, 2 + (hf + 1) * HW_)
                t0 = 2 * hf
                kXP, kXR, kXB, kT1, kT2, kT3 = [(nm, hf) for nm in ("XP", "XR", "XB", "T1", "T2", "T3")]
                XPK = [("XP", 0), ("XP", 1)]
                for t in (t0, t0 + 1):
                    tsl = slice(t * 512, (t + 1) * 512)
                    mm(psb(t), [(wv[:, k, 0:128], H[:, k, tsl]) for k in range(8)], [rk] + hkeys(t), pk(t))
                    mm(psb(4 + t), [(wv[:, k, 128:256], H[:, k, tsl]) for k in range(8)], [rk] + hkeys(t), pk(4 + t))
                yield
                act(YT[:, c, hs], psb(t0, 2), AF.Gelu_apprx_tanh, pk(t0, 2), [("yt", c)])
                act(XP[:, xs], psb(4 + t0, 2), AF.Copy, pk(4 + t0, 2), [kXP])
                yield
                tsc(XR[:, hs], XP[:, hf * HW_:hf * HW_ + HW_], pcol(("a_cw", l), c * 4 + 0), ALU.mult, XPK + ["PAR"], [kXR],
                    s2=pcol(("a_cb", l), c), op1=ALU.add)
                for tap in range(1, 4):
                    stt(XR[:, hs], XP[:, hf * HW_ + tap:hf * HW_ + tap + HW_], pcol(("a_cw", l), c * 4 + tap), XR[:, hs],
                        ALU.mult, ALU.add, XPK + [kXR, "PAR"], [kXR])
                yield
                act(XB[:, hs], XR[:, hs], AF.Copy, [kXR], [kXB])
                yield
                for dr in range(2):
                    for g in range(2):
                        for t in (t0, t0 + 1):
                            tsl = slice(t * 512, (t + 1) * 512)
                            mm(psb(g * 4 + t), [(gw[:, dr * 2 + g, :], XB[:, tsl])], [rkg, kXB], pk(g * 4 + t))
                    yield
                    gbo = c * 4 + dr * 2
                    act(T1[:, hs], psb(t0, 2), AF.Sigmoid, pk(t0, 2) + ["PAR"], [kT1], bias=pcol(("a_gb", l), gbo))
                    act(T2[:, hs], psb(4 + t0, 2), AF.Sigmoid, pk(4 + t0, 2) + ["PAR"], [kT2], bias=pcol(("a_gb", l), gbo + 1))
                    ci = c * 2 + dr
                    act(T3[:, hs], T1[:, hs], AF.Exp, [kT1, "PC"], [kT3], scale=PC[:, ci:ci + 1])
                    act(T1[:, hs], T1[:, hs], AF.Exp, [kT1, "PC"], [kT1], scale=PC[:, 16 + ci:16 + ci + 1])
                    act(T1[:, hs], T1[:, hs], AF.Sqrt, [kT1], [kT1], scale=-1.0, bias=1.0)
                    yield
                    tt(T2[:, hs], T2[:, hs], T1[:, hs], ALU.mult, [kT1, kT2], [kT2], eng="pool")
                    tt(T2[:, hs], T2[:, hs], XR[:, hs], ALU.mult, [kT2, kXR], [kT2], eng="pool")
                    yield
                    if dr == 0:
                        if hf == 0:
                            S.op("dve", lambda e: e.tensor_tensor_scan(out=XP[:, xs], data0=T3[:, hs], data1=T2[:, hs], initial=0.0,
                                                                       op0=ALU.mult, op1=ALU.add), [kT3, kT2], [kXP])
                        else:
                            S.op("dve", lambda e: e.tensor_tensor_scan(out=XP[:, xs], data0=T3[:, hs], data1=T2[:, hs],
                                                                       initial=XP[:, 2 + HW_ - 1:2 + HW_],
                                                                       op0=ALU.mult, op1=ALU.add), [kT3, kT2, ("XP", 0)], [kXP])
                    else:
                        if hf == 1:
                            S.op("dve", lambda e: e.tensor_tensor_scan(out=T1[:, hs][:, ::-1], data0=T3[:, hs][:, ::-1],
                                                                       data1=T2[:, hs][:, ::-1], initial=0.0,
                                                                       op0=ALU.mult, op1=ALU.add), [kT3, kT2], [kT1])
                        else:
                            yield
                            S.op("dve", lambda e: e.tensor_tensor_scan(out=T1[:, hs][:, ::-1], data0=T3[:, hs][:, ::-1],
                                                                       data1=T2[:, hs][:, ::-1], initial=T1[:, HW_:HW_ + 1],
                                                                       op0=ALU.mult, op1=ALU.add), [kT3, kT2, ("T1", 1)], [kT1])
                    yield
                tt(T1[:, hs], T1[:, hs], XP[:, xs], ALU.add, [kT1, kXP], [kT1])
                tt(YT[:, c, hs], T1[:, hs], YT[:, c, hs], ALU.mult, [kT1, ("yt", c)], [("yt", c)])

            for c in range(8):
                slot, rk = WR.get(("a_in", l, c))
                wv = slot[:, 0:2048].rearrange("p (k n) -> p k n", k=8)
                slotg, rkg = WR.get(("a_gw", l, c))
                gw = slotg[:, 0:512].rearrange("p (q o) -> p q o", q=4)
                gens = [a_gen(c, 0, wv, rk, gw, rkg), a_gen(c, 1, wv, rk, gw, rkg)]
                while gens:
                    for g_ in list(gens):
                        try:
                            next(g_)
                        except StopIteration:
                            gens.remove(g_)
                WR.done()
                WR.done()
            outproj(lambda half: ("a_out", l, half), YT, [("yt", c) for c in range(8)])
            S.barrier()

        def mixer_b(l, bidx):
            A.reset()
            UT = A.bf16(8 * SEQ).rearrange("p (c n) -> p c n", c=8)
            LGB = A.f32(3072)
            VT = [A.f32(1024) for _ in range(2)]
            VN = [A.bf16(1024) for _ in range(2)]
            TB = [A.f32(1024) for _ in range(2)]
            STT = [A.f32(12) for _ in range(2)]
            MV = [A.f32(4) for _ in range(2)]
            S.op("sp", lambda e: e.dma_start(out=LGB, in_=parb[:, bidx * 3072:(bidx + 1) * 3072]), W=["LGB"], dma="LGB")
            for blk in range(2):
                slot, rk = WR.get(("b_in_u", l, blk))
                wb = slot[:, 0:4096].rearrange("p (k n) -> p k n", k=8)
                for t in range(4):
                    tsl = slice(t * 512, (t + 1) * 512)
                    b0 = (t % 2) * 4
                    for mi in range(4):
                        mm(psb(b0 + mi), [(wb[:, k, mi * 128:(mi + 1) * 128], H[:, k, tsl]) for k in range(8)],
                           [rk] + hkeys(t), pk(b0 + mi))
                    act(UT[:, blk * 4:(blk + 1) * 4, tsl], PS[:, b0 * 512:(b0 + 4) * 512].rearrange("p (c n) -> p c n", c=4),
                        AF.Gelu_apprx_tanh, pk(b0, 4), [("ut", blk * 4 + mi, t) for mi in range(4)])
                WR.done()
            slot0, rk0 = WR.get(("b_in_v", l, 0))
            slot1, rk1 = WR.get(("b_in_v", l, 1))
            slot2, rk2 = WR.get(("b_ws", l))
            wvv = [slot0[:, 0:4096].rearrange("p (k n) -> p k n", k=8), slot1[:, 0:4096].rearrange("p (k n) -> p k n", k=8)]
            rkv = [rk0, rk1]
            wst = slot2[:, 0:1024].rearrange("p (g n) -> p g n", g=8)
            for tt_ in range(16):
                b = tt_ % 2
                tok = slice(tt_ * 128, (tt_ + 1) * 128)
                t = tt_ // 4
                pb = b * 4
                for blk in range(2):
                    mm(psb(pb + blk), [(H[:, k, tok], wvv[blk][:, k, :]) for k in range(8)], [rkv[blk]] + hkeys(t), pk(pb + blk))
                act(VT[b], psb(pb, 2), AF.Gelu_apprx_tanh, pk(pb, 2), [("vt", b)])
                S.op("dve", lambda e, b=b: e.bn_stats(out=STT[b][:, 0:6], in_=VT[b][:, 0:512]), [("vt", b)], [("st", b)])
                S.op("dve", lambda e, b=b: e.bn_stats(out=STT[b][:, 6:12], in_=VT[b][:, 512:1024]), [("vt", b)], [("st2", b)])
                S.op("dve", lambda e, b=b: e.bn_aggr(out=MV[b][:, 0:2], in_=STT[b][:, 0:12]), [("st", b), ("st2", b)], [("mv", b)])
                act(MV[b][:, 2:3], MV[b][:, 1:2], AF.Sqrt, [("mv", b)], [("mv2", b)], bias=1e-5)
                S.op("dve", lambda e, b=b: e.reciprocal(out=MV[b][:, 2:3], in_=MV[b][:, 2:3]), [("mv2", b)], [("mv2", b)])
                stt(VT[b], VT[b], MV[b][:, 0:1], LGB[:, 0:1024], ALU.subtract, ALU.mult, [("vt", b), ("mv", b), "LGB"], [("vt", b)])
                stt(VN[b], VT[b], MV[b][:, 2:3], LGB[:, 1024:2048], ALU.mult, ALU.add, [("vt", b), ("mv2", b), "LGB"], [("vn", b)])
                for g in range(8):
                    bank = pb + 2 + g // 4
                    o = PS[:, bank * 512 + (g % 4) * 128: bank * 512 + (g % 4 + 1) * 128]
                    mm(o, [(VN[b][:, g * 128:(g + 1) * 128], wst[:, g, :])], [("vn", b), rk2], [("ps", bank)])
                tt(TB[b], psb(pb + 2, 2), LGB[:, 2048:3072], ALU.add, pk(pb + 2, 2) + ["LGB"], [("tb", b)])
                uk = [("ut", g, t) for g in range(8)]
                tt(UT[:, :, tok], TB[b][:].rearrange("p (g n) -> p g n", g=8), UT[:, :, tok], ALU.mult,
                   [("tb", b)] + uk, uk, eng="pool")
            WR.done()
            WR.done()
            WR.done()
            outproj(lambda half: ("b_out", l, half), UT, [("ut", g, t) for g in range(8) for t in range(4)])
            S.barrier()

        def mixer_c(l):
            A.reset()
            QSC = 128.0 ** -0.5
            S1K = [("S1", 0), ("S1", 1)]
            S2K = [("S2", 0), ("S2", 1)]
            S3K = [("gcb", c) for c in range(16)]
            BTOK = A.f32(512)
            BT3 = BTOK.rearrange("p (c n) -> p c n", c=16)
            WAB = A.bf16(256).rearrange("p (k n) -> p k n", k=8)
            SM = A.f32(128)
            GCT, EGT, NEGT, KDT, NBT, EGLT = [SM[:, i * 16:(i + 1) * 16] for i in range(6)]
            QT = A.bf16(SEQ)
            KT = A.bf16(SEQ)
            KTOK = A.bf16(SEQ).rearrange("p (c n) -> p c n", c=16)
            VTOK = A.bf16(SEQ).rearrange("p (c n) -> p c n", c=16)
            OT = A.f32(SEQ)
            S1 = A.f32(SEQ + 4)
            S2 = A.f32(SEQ)
            S3 = A.f32(SEQ)
            QG = A.bf16(SEQ)
            YS = A.bf16(SEQ).rearrange("p (c n) -> p c n", c=16)
            QKD = A.bf16(SEQ).rearrange("p (c n) -> p c n", c=16)
            NBC = 4
            An = A.f32(NBC * 128).rearrange("p (c n) -> p c n", c=NBC)
            At = A.f32(NBC * 128).rearrange("p (c n) -> p c n", c=NBC)
            Tt = A.f32(NBC * 128).rearrange("p (c n) -> p c n", c=NBC)
            Yy = A.f32(NBC * 128).rearrange("p (c n) -> p c n", c=NBC)
            D1 = A.f32(NBC * 128)
            D2 = A.f32(NBC * 128)
            M1s = An
            WREP = D1.bitcast(BF16).rearrange("p (k n) -> p k n", k=8)
            RH = [A.bf16(128) for _ in range(2)]
            VN = [A.bf16(128) for _ in range(2)]
            VN2 = [A.bf16(128) for _ in range(2)]
            S32 = A.f32(128)
            SB = A.bf16(128)
            RAW, ACC, SIL = S1, S2, S3
            D13 = D1.rearrange("p (c n) -> p c n", c=NBC)
            D23 = D2.rearrange("p (c n) -> p c n", c=NBC)

            def bank3(b):
                return PS[:, b * 512:b * 512 + NBC * 128].rearrange("p (c n) -> p c n", c=NBC)

            ao, _ = poffs[("c_alog", l)]
            act(PC[:, 48:64], PAR[:, ao:ao + 16], AF.Exp, ["PAR"], ["PCc"])
            tsc(PC[:, 48:64], PC[:, 48:64], -1.0, ALU.mult, ["PCc"], ["PCc"])
            S.op("dve", lambda e: e.memset(S1[:, 0:2], 0.0), W=[*S1K])
            S.op("dve", lambda e: e.memset(S1[:, SEQ + 2:SEQ + 4], 0.0), W=[*S1K])
            slot, rk = WR.get(("c_ab", l))
            wab = slot[:, 0:256].rearrange("p (k n) -> p k n", k=8)
            copy(WAB, wab, [rk], ["WAB"])
            for c in range(16):
                tok = slice(c * 128, (c + 1) * 128)
                mm(PS[:, c * 32:(c + 1) * 32], [(H[:, k, tok], wab[:, k, :]) for k in range(8)], [rk] + hkeys(c // 4), pk(0))
            WR.done()
            act(BTOK, psb(0), AF.Sigmoid, pk(0), ["BTOK"])

            for hd in range(8):
                slot, rk = WR.get(("c_in", l, hd))
                win = slot[:, 0:4096].rearrange("p (k n) -> p k n", k=8)
                cwo, _ = poffs[("c_cw", l)]
                SQ = YS.rearrange("p c n -> p (c n)")
                VTf = QG

                def proj_gen(typ, hf):
                    pb = (typ % 2) * 4
                    t0 = 2 * hf
                    HW_ = SEQ // 2
                    hs = slice(hf * HW_, (hf + 1) * HW_)
                    ys_k = [("ys", b_) for b_ in range(hf * (8 // NBC), (hf + 1) * (8 // NBC))]
                    ot_k = [("ot", c) for c in range(8 * hf, 8 * hf + 8)]
                    s3_k = [("gcb", c) for c in range(8 * hf, 8 * hf + 8)]
                    for t in (t0, t0 + 1):
                        tsl = slice(t * 512, (t + 1) * 512)
                        mm(psb(pb + t), [(win[:, k, typ * 128:(typ + 1) * 128], H[:, k, tsl]) for k in range(8)],
                           [rk] + hkeys(t), pk(pb + t))
                    yield
                    act(RAW[:, 2 + hf * HW_:2 + (hf + 1) * HW_], psb(pb + t0, 2), AF.Copy, pk(pb + t0, 2), [("S1", hf)])
                    yield
                    co = cwo + hd * 12 + typ * 4
                    tsc(ACC[:, hs], RAW[:, hf * HW_:hf * HW_ + HW_], PAR[:, co:co + 1], ALU.mult, S1K + ["PAR"], [("S2", hf)])
                    for tap in range(1, 4):
                        stt(ACC[:, hs], RAW[:, hf * HW_ + tap:hf * HW_ + tap + HW_], PAR[:, co + tap:co + tap + 1], ACC[:, hs],
                            ALU.mult, ALU.add, S1K + [("S2", hf), "PAR"], [("S2", hf)])
                    yield
                    p3d = PS[:, hf * HW_:(hf + 1) * HW_].rearrange("p (c n) -> p c n", c=8)
                    if typ == 2:
                        act(VTf[:, hs], ACC[:, hs], AF.Silu, [("S2", hf)], ["QG"])
                        yield
                        for c in range(8 * hf, 8 * hf + 8):
                            mm(PS[:, c * 128:(c + 1) * 128], [(VTf[:, c * 128:(c + 1) * 128], IDB[:])], ["QG", "IDB"], pk(c // 4))
                        yield
                        copy(VTOK[:, 8 * hf:8 * hf + 8, :], p3d, pk(2 * hf, 2), ["VTOK"])
                    else:
                        act(SIL[:, hs], ACC[:, hs], AF.Silu, [("S2", hf)], s3_k)
                        yield
                        act(SQ[:, hs], SIL[:, hs], AF.Square, s3_k, ys_k)
                        yield
                        pb2 = 4 - pb
                        for t in (t0, t0 + 1):
                            tsl = slice(t * 512, (t + 1) * 512)
                            mm(psb(pb2 + t), [(ONB[:], SQ[:, tsl])], ["ONB"] + ys_k, pk(pb2 + t))
                        yield
                        act(OT[:, hs], psb(pb2 + t0, 2), AF.Sqrt, pk(pb2 + t0, 2), ot_k, bias=1e-6)
                        yield
                        S.op("dve", lambda e: e.reciprocal(out=OT[:, hs], in_=OT[:, hs]), ot_k, ot_k)
                        yield
                        if typ == 0:
                            stt(QT[:, hs], SIL[:, hs], QSC, OT[:, hs], ALU.mult, ALU.mult, s3_k + ot_k, ["QT"])
                        else:
                            tt(KT[:, hs], SIL[:, hs], OT[:, hs], ALU.mult, s3_k + ot_k, ["KT"])
                            yield
                            for c in range(8 * hf, 8 * hf + 8):
                                mm(PS[:, c * 128:(c + 1) * 128], [(KT[:, c * 128:(c + 1) * 128], IDB[:])], ["KT", "IDB"], pk(c // 4))
                            yield
                            act(KTOK[:, 8 * hf:8 * hf + 8, :], p3d, AF.Copy, pk(2 * hf, 2), ["KTOK"])

                for typ in range(3):
                    gens = [proj_gen(typ, 0), proj_gen(typ, 1)]
                    while gens:
                        for g in list(gens):
                            try:
                                next(g)
                            except StopIteration:
                                gens.remove(g)

                gk = [("gcb", c) for c in range(16)]
                for dr in range(2):
                    if KSTOPC <= 1:
                        continue
                    n = dr * 8 + hd
                    bcol = BT3[:, :, 16 + n]
                    copy(WREP, WAB[:, :, n:n + 1].to_broadcast([128, 8, 128]), ["WAB"], ["D1"])
                    GCB, EG = S3, S2[:, 0:SEQ // 2].bitcast(BF16)
                    gk = [("gcb", c) for c in range(16)]
                    lastc = 127 if dr == 0 else 0
                    GL = GCB[:, lastc::128]
                    EGL = EGLT

                    def dir_gen(hf):
                        HW_ = SEQ // 2
                        hs = slice(hf * HW_, (hf + 1) * HW_)
                        cr = range(8 * hf, 8 * hf + 8)
                        cols = slice(8 * hf, 8 * hf + 8)
                        t0 = 2 * hf
                        g1 = S1[:, 2 + hf * HW_:2 + (hf + 1) * HW_]
                        kS1 = ("S1", hf)
                        gkh = [("gcb", c) for c in cr]
                        for t in (t0, t0 + 1):
                            tsl = slice(t * 512, (t + 1) * 512)
                            mm(psb(t), [(WREP[:, k, :], H[:, k, tsl]) for k in range(8)], ["D1"] + hkeys(t), pk(t))
                        yield
                        act(g1, psb(t0, 2), AF.Exp, pk(t0, 2) + ["PAR"], [kS1], bias=pcol(("c_dtb", l), n))
                        act(g1, g1, AF.Ln, [kS1], [kS1], bias=1.0)
                        yield
                        tsc(g1, g1, PC[:, 48 + n:48 + n + 1], ALU.mult, [kS1, "PCc"], [kS1])
                        for c in cr:
                            sl = slice(c * 128, (c + 1) * 128)
                            gl = S1[:, 2 + c * 128:2 + (c + 1) * 128]
                            if dr == 0:
                                S.op("dve", lambda e, sl=sl, gl=gl: e.tensor_tensor_scan(out=GCB[:, sl], data0=ONB[:], data1=gl, initial=0.0,
                                                                                        op0=ALU.mult, op1=ALU.add), [kS1, "ONB"], [("gcb", c)])
                            else:
                                S.op("dve", lambda e, sl=sl, gl=gl: e.tensor_tensor_scan(out=GCB[:, sl][:, ::-1], data0=ONB[:], data1=gl[:, ::-1],
                                                                                        initial=0.0, op0=ALU.mult, op1=ALU.add),
                                     [kS1, "ONB"], [("gcb", c)])
                        yield
                        act(EG[:, hs], GCB[:, hs], AF.Exp, gkh, [("S2", 0)])
                        TMPh = g1.rearrange("p (c n) -> p c n", c=8)
                        tt(TMPh, GCB[:, hs].rearrange("p (c n) -> p c n", c=8), IDF[:].unsqueeze(1).to_broadcast([128, 8, 128]), ALU.mult,
                           gkh + ["IDF"], [kS1])
                        S.op("dve", lambda e: e.tensor_reduce(out=GCT[:, cols], in_=TMPh, op=ALU.add, axis=mybir.AxisListType.X),
                             [kS1], [("GCT", hf)])
                        yield
                        act(EGT[:, cols], GCT[:, cols], AF.Exp, [("GCT", hf)], [("EGT", hf)])
                        act(EGLT[:, cols], GL[:, cols], AF.Exp, gkh, [("EGLT", hf)])
                        tt(QG[:, hs], QT[:, hs], EG[:, hs], ALU.mult, ["QT", ("S2", 0)], ["QG"])
                        yield
                        tsc(NEGT[:, cols], EGT[:, cols], -1.0, ALU.mult, [("EGT", hf)], [("NEGT", hf)])
                        tt(KDT[:, cols], GL[:, cols], GCT[:, cols], ALU.subtract, gkh + [("GCT", hf)], [("KDT", hf)])
                        yield
                        act(KDT[:, cols], KDT[:, cols], AF.Exp, [("KDT", hf)], [("KDT", hf)])

                    gens = [dir_gen(0), dir_gen(1)]
                    while gens:
                        for g in list(gens):
                            try:
                                next(g)
                            except StopIteration:
                                gens.remove(g)
                    SET1K = ["An1", "At1", "T1", "Y1", "D1x1", "D2x1"]
                    S.op("dve", lambda e: e.memset(ZC[:, 1:2], 0.0), W=S1K + S2K + SET1K + ["ZCf"])
                    idb3 = IDF[:].unsqueeze(1).to_broadcast([128, NBC, 128])

                    def lm3(li):
                        return LMK[:, li, :].unsqueeze(1).to_broadcast([128, NBC, 128])

                    def v3(ap):
                        return ap.rearrange("p (c n) -> p c n", c=NBC)

                    def batch_gen(bq, sid):
                        W_ = NBC * 128
                        if sid == 0:
                            base = [An.rearrange("p c n -> p (c n)"), At.rearrange("p c n -> p (c n)"), Tt.rearrange("p c n -> p (c n)"),
                                    Yy.rearrange("p c n -> p (c n)"), D1, D2]
                            b0 = 4
                        else:
                            base = [S1[:, 2 + i * W_:2 + (i + 1) * W_] for i in range(4)] + \
                                   [S2[:, SEQ // 2:SEQ // 2 + W_], S2[:, SEQ // 2 + W_:SEQ // 2 + 2 * W_]]
                            b0 = 0
                        An_, At_, Tt_, Yy_ = [v3(b) for b in base[0:4]]
                        D1_, D2_ = base[4], base[5]
                        if TBF16:
                            lowv = [v3(b.bitcast(BF16)[:, 0:W_]) for b in base]
                        else:
                            lowv = [v3(b) for b in base]
                        M1L, _, TtL, YyL, D1L, D2L = lowv
                        kA, kAt, kT_, kY, kD1, kD2 = ["%s%d" % (nm, sid) for nm in ("An", "At", "T", "Y", "D1x", "D2x")]
                        if sid == 0:
                            kD1, kD2 = "D1", "D2"
                        D13_, D23_ = v3(D1_), v3(D2_)
                        M1_ = An_

                        def pq(bi, ci):
                            return PS[:, (b0 + bi) * 512 + ci * 128:(b0 + bi) * 512 + (ci + 1) * 128]

                        def pq3(bi):
                            return v3(PS[:, (b0 + bi) * 512:(b0 + bi) * 512 + W_])
                        cs = [bq * NBC + ci for ci in range(NBC)]
                        for ci, c in enumerate(cs):
                            sl = slice(c * 128, (c + 1) * 128)
                            mm(pq(0, ci), [(KT[:, sl], KT[:, sl])], ["KT"], pk(b0))
                            mm(pq(1, ci), [(KT[:, sl], QT[:, sl])], ["KT", "QT"], pk(b0 + 1))
                        bsl = slice(bq * NBC * 128, (bq + 1) * NBC * 128)
                        gkb = [("gcb", c) for c in cs]
                        gct3 = GCT[:, bq * NBC:(bq + 1) * NBC].unsqueeze(2).to_broadcast([128, NBC, 128])
                        beta3 = bcol[:, bq * NBC:(bq + 1) * NBC].unsqueeze(2).to_broadcast([128, NBC, 128])
                        tt(D13_, v3(GCB[:, bsl]), gct3, ALU.subtract, gkb + [("GCT", cs[0] // 8)], [kD1])
                        tsc(D23_, D13_, 0.0, ALU.min, [kD1], [kD2])
                        tsc(D13_, D13_, 0.0, ALU.max, [kD1], [kD1])
                        yield
                        act(D1_, D1_, AF.Exp, [kD1], [kD1], scale=-1.0)
                        act(D2_, D2_, AF.Exp, [kD2], [kD2])
                        yield
                        tt(D13_, D13_, MSK[:, dr, :].unsqueeze(1).to_broadcast([128, NBC, 128]), ALU.mult, [kD1, "MSK"], [kD1])
                        tt(D23_, D23_, MSK[:, 2 + dr, :].unsqueeze(1).to_broadcast([128, NBC, 128]), ALU.mult, [kD2, "MSK"], [kD2])
                        tt(D13_, D13_, beta3, ALU.mult, [kD1, "BTOK"], [kD1])
                        tt(An_, pq3(0), D13_, ALU.mult, pk(b0) + [kD1], [kA])
                        tt(QKD[:, bq * NBC:(bq + 1) * NBC, :], pq3(1), D23_, ALU.mult, pk(b0 + 1) + [kD2], [("qkd", bq)])
                        yield
                        for ci in range(NBC):
                            mm(pq(2, ci), [(An_[:, ci, :], IDF[:])], [kA, "IDF"], pk(b0 + 2))
                        yield
                        act(At_, pq3(2), AF.Copy, pk(b0 + 2), [kAt])
                        tt(D13_, An_, lm3(0), ALU.mult, [kA, "LMK"], [kD1], eng="pool")
                        yield
                        tt(D23_, At_, lm3(0), ALU.mult, [kAt, "LMK"], [kD2], eng="pool")
                        tt(TtL, idb3, D13_, ALU.subtract, ["IDF", kD1], [kT_])
                        yield
                        tt(YyL, idb3, D23_, ALU.subtract, ["IDF", kD2], [kY])
                        for li in range(1, 7):
                            last = li == 6
                            dbuf, dkey = (D1L, kD1) if li % 2 else (D2L, kD2)
                            tt(dbuf, At_, lm3(li), ALU.mult, [kAt, "LMK"], [dkey], eng="pool")
                            yield
                            for ci in range(NBC):
                                mm(pq(0, ci), [(dbuf[:, ci, :], TtL[:, ci, :])], [dkey, kT_], pk(b0))
                            yield
                            act(M1L, pq3(0), AF.Copy, pk(b0), [kA])
                            yield
                            if not last:
                                for ci in range(NBC):
                                    mm(pq(1, ci), [(YyL[:, ci, :], M1L[:, ci, :])], [kY, kA], pk(b0 + 1))
                            for ci in range(NBC):
                                mm(pq(2, ci), [(M1L[:, ci, :], YyL[:, ci, :])], [kY, kA], pk(b0 + 2))
                            yield
                            if not last:
                                tt(TtL, TtL, pq3(1), ALU.subtract, [kT_] + pk(b0 + 1), [kT_])
                            tt(YyL, YyL, pq3(2), ALU.subtract, [kY] + pk(b0 + 2), [kY])
                        yield
                        tt(YS[:, bq * NBC:(bq + 1) * NBC, :], YyL, beta3, ALU.mult, [kY, "BTOK"], [("ys", bq)])

                    order = list(range(16)) if dr == 0 else list(range(15, -1, -1))

                    def rec_gen(si0, chunks):
                        for k_, c in enumerate(chunks):
                            si = si0 + k_
                            par_ = si % 2
                            sl = slice(c * 128, (c + 1) * 128)
                            bq = c // NBC
                            o_ = par_ * 256
                            p1 = PS[:, 3 * 512 + o_:3 * 512 + o_ + 128]
                            p3 = PS[:, 3 * 512 + o_ + 128:3 * 512 + o_ + 256]
                            p2 = PS[:, 7 * 512 + o_:7 * 512 + o_ + 128]
                            p4 = PS[:, 7 * 512 + o_ + 128:7 * 512 + o_ + 256]
                            mm(p1, [(KT[:, sl], SB)], ["KT", "SB"], pk(3))
                            yield
                            stt(RH[par_], p1, NEGT[:, c:c + 1], VTOK[:, c, :], ALU.mult, ALU.add, pk(3) + [("NEGT", c // 8), "VTOK"], [("rh", par_)])
                            yield
                            mm(p2, [(YS[:, c, :], RH[par_])], [("ys", bq), ("rh", par_)], pk(7))
                            yield
                            act(VN[par_], p2, AF.Copy, pk(7), [("vn", par_)])
                            tsc(VN2[par_], p2, KDT[:, c:c + 1], ALU.mult, pk(7) + [("KDT", c // 8)], [("vn2", par_)])
                            yield
                            mm(p3, [(SB, QG[:, sl]), (VN[par_], QKD[:, c, :])], ["SB", "QG", ("vn", par_), ("qkd", bq)], pk(3))
                            mm(p4, [(KTOK[:, c, :], VN2[par_])], ["KTOK", ("vn2", par_)], pk(7))
                            yield
                            stt(SB, S32, EGL[:, c:c + 1], p4, ALU.mult, ALU.add, ["S32", ("EGLT", c // 8)] + pk(7), ["SB"])
                            stt(S32, S32, EGL[:, c:c + 1], p4, ALU.mult, ALU.add, ["S32", ("EGLT", c // 8)] + pk(7), ["S32"])
                            if dr == 0:
                                act(OT[:, sl], p3, AF.Copy, pk(3), [("ot", c)])
                            else:
                                tt(OT[:, sl], p3, OT[:, sl], ALU.add, pk(3) + [("ot", c)], [("ot", c)])
                            yield

                    def drive(gens):
                        while gens:
                            for g in list(gens):
                                try:
                                    next(g)
                                except StopIteration:
                                    gens.remove(g)

                    npair = 16 // NBC // 2
                    pairs = list(range(npair)) if dr == 0 else list(range(npair - 1, -1, -1))
                    nper = 16 // npair
                    if KSTOPC > 2:
                        drive([batch_gen(2 * pairs[0], 0), batch_gen(2 * pairs[0] + 1, 1)])
                    S.op("dve", lambda e: e.memset(S32, 0.0), W=["S32"])
                    S.op("dve", lambda e: e.memset(SB, 0.0), W=["SB"])
                    for pi in range(1, npair + 1):
                        gens = []
                        if pi < npair and KSTOPC > 2:
                            gens += [batch_gen(2 * pairs[pi], 0), batch_gen(2 * pairs[pi] + 1, 1)]
                        if KSTOPC > 5:
                            gens.append(rec_gen((pi - 1) * nper, order[(pi - 1) * nper:pi * nper]))
                        drive(gens)
                    S.op("dve", lambda e: e.memset(ZC[:, 1:2], 0.0), W=S1K + S2K + SET1K + ["ZCf"])
                ok_ = [("ot", c) for c in range(16)]
                SQ = YS.rearrange("p c n -> p (c n)")
                act(SQ, OT, AF.Square, ok_, [("ys", b_) for b_ in range(16 // NBC)])
                for t in range(4):
                    tsl = slice(t * 512, (t + 1) * 512)
                    mm(psb(t), [(ONB[:], SQ[:, tsl])], ["ONB"] + [("ys", b_) for b_ in range(16 // NBC)], pk(t))
                act(S3, psb(0, 4), AF.Sqrt, pk(0, 4), gk + [*S3K], scale=1.0 / 128, bias=1e-6)
                S.op("dve", lambda e: e.reciprocal(out=S3, in_=S3), gk + [*S3K], gk + [*S3K])
                for t in range(4):
                    tsl = slice(t * 512, (t + 1) * 512)
                    mm(psb(4 + t), [(win[:, k, 384:512], H[:, k, tsl]) for k in range(8)], [rk] + hkeys(t), pk(4 + t))
                WR.done()
                act(S2, psb(4, 4), AF.Silu, pk(4, 4), [*S2K])
                tt(S3, S3, OT, ALU.mult, gk + [*S3K] + ok_, gk + [*S3K])
                stt(QG, S3, pcol(("c_ng", l), 0), S2, ALU.mult, ALU.mult, gk + [*S3K, *S2K, "PAR"], ["QG"])
                slot, rk = WR.get(("c_out", l, hd))
                wo = slot[:, 0:1024]
                for t in range(4):
                    tsl = slice(t * 512, (t + 1) * 512)
                    for mo in range(8):
                        bank = mo
                        mm(psb(bank), [(wo[:, mo * 128:(mo + 1) * 128], QG[:, tsl])], [rk, "QG"], pk(bank))
                        tt(X[:, mo, tsl], psb(bank), X[:, mo, tsl], ALU.add, pk(bank) + [("x", mo, t)], [("x", mo, t)])
                WR.done()
            S.barrier()

        def program():
            WR.reset()
            setup()
            for s in range(nseq):
                for t in range(4):
                    tsl = slice(t * 512, (t + 1) * 512)
                    S.op("sp", lambda e, s=s, tsl=tsl: e.dma_start(
                        out=X[:, :, tsl], in_=xT[s].rearrange("(k p) n -> p k n", p=128)[:, :, tsl]),
                        W=[("x", k, t) for k in range(8)], dma=("xin", t))
                bcount = 0
                for l, (kind, j) in enumerate(cfg):
                    if kind != "N":
                        rmsnorm(("g_mix", l))
                    if kind == "N":
                        pass
                    elif kind == "A":
                        mixer_a(l)
                    elif kind == "B":
                        mixer_b(l, bcount)
                        bcount += 1
                    else:
                        mixer_c(l)
                    rmsnorm(("g_mlp", l))
                    mlp(l)
                rmsnorm(("g_fin",), out_x=True)
                for t in range(4):
                    tsl = slice(t * 512, (t + 1) * 512)
                    S.op("sp", lambda e, s=s, tsl=tsl: e.dma_start(
                        out=yT[s].rearrange("(k p) n -> p k n", p=128)[:, :, tsl], in_=X[:, :, tsl]),
                        R=[("x", k, t) for k in range(8)], W=[("y", s, t)], dma=("yout", t))
            S.final_wait("sp", [("y", s, t) for s in range(nseq) for t in range(4)])

        S.dry = True
        program()
        S.dry = False
        program()
        S.emit()
    return nc


DEFAULT_CFG = [("A", 0), ("B", 0), ("C", 0), ("A", 1)]


def run(inputs, cfg, ncore, nseq):
    wb = weight_blocks(inputs, cfg)
    woffs, wtot = layout(wb)
    wsarr = np.concatenate(list(wb.values()), axis=1).astype(np.float32)
    pb = param_blocks(inputs, cfg)
    poffs, ptot = layout(pb)
    pararr = np.concatenate(list(pb.values()), axis=1).astype(np.float32)
    parb = paramb_blocks(inputs, cfg)
    nc = build(nseq, cfg, woffs, wtot, poffs, ptot, parb.shape[1])
    x = np.asarray(inputs["x"], dtype=np.float32)
    in_maps = []
    for c in range(ncore):
        xs = np.ascontiguousarray(x[c * nseq:(c + 1) * nseq].transpose(0, 2, 1))
        in_maps.append({"xT": xs, "ws": wsarr, "par": pararr, "parb": parb})
    if os.environ.get("KTRACE") == "1":
        res = run_bass_kernel_spmd(nc, in_maps, core_ids=list(range(ncore)), trace=True)
        print("EXEC_NS", res.exec_time_ns)
    else:
        res = run_bass_kernel_spmd(nc, in_maps, core_ids=list(range(ncore)))
    outs = [np.asarray(r["yT"]).transpose(0, 2, 1) for r in res.results]
    return np.ascontiguousarray(np.concatenate(outs, axis=0)).astype(np.float32)


def kernel(**inputs):
    inputs = {k: np.asarray(v) for k, v in inputs.items()}
    return run(inputs, DEFAULT_CFG, NCORE, inputs["x"].shape[0] // NCORE)
```
